# Optimizing a Trainium2 kernel written in Bass

```python
import jax, jax.numpy as jnp
from jax import lax
import numpy as np

D_MODEL = 2048
BATCH = 4
SEQ = 2048
DEPTH = 4
DEC_BATCH = 128
DEC_SEQ = 4
PAST_LEN = 16384
PAGE_SIZE = 128

SGU_WIDTH = D_MODEL
SGU_CHUNK = 128
SGU_GROUP_DIM = 128
SGU_GROUPS = SGU_WIDTH // SGU_GROUP_DIM
SSM_EXPAND = 2
D_INNER = SSM_EXPAND * D_MODEL
SSM_HEAD_DIM = 64
SSM_HEADS = D_INNER // SSM_HEAD_DIM
SSM_STATE = 128
SSM_GROUPS = 8
CONV_WIDTH = 4
CONV_DIM = D_INNER + 2 * SSM_GROUPS * SSM_STATE
SSD_CHUNK = 128
FFN_HIDDEN = -(-8 * D_MODEL // (3 * 256)) * 256
EPS = 1e-6
IN_SIZES = (2 * SGU_WIDTH, D_INNER, CONV_DIM, SSM_HEADS, D_MODEL, D_MODEL)
IN_COLS = sum(IN_SIZES)

kernel_name = "gated_gmlp_ssd_hybrid_step"


def rmsnorm(x, g):
    xf = x.astype(jnp.float32)
    y = xf * lax.rsqrt(jnp.mean(xf * xf, axis=-1, keepdims=True) + EPS)
    return (y * g.astype(jnp.float32)).astype(x.dtype)


def layernorm(x, g, b):
    xf = x.astype(jnp.float32)
    mu = jnp.mean(xf, axis=-1, keepdims=True)
    var = jnp.mean(jnp.square(xf - mu), axis=-1, keepdims=True)
    y = (xf - mu) * lax.rsqrt(var + EPS)
    return (y * g.astype(jnp.float32) + b.astype(jnp.float32)).astype(x.dtype)


def sgu_branch(s_in, ln_g, ln_b, w_s, b_s):
    u, v = jnp.split(jax.nn.gelu(s_in), 2, axis=-1)
    v = layernorm(v, ln_g, ln_b)
    bsz, L, _ = v.shape
    t = min(SGU_CHUNK, L)
    assert L % t == 0
    vc = v.reshape(bsz, L // t, t, SGU_GROUPS, SGU_GROUP_DIM)
    mask = jnp.tril(jnp.ones((t, t), dtype=bool))
    w = jnp.where(mask, w_s[:, :t, :t], 0).astype(v.dtype)
    bias = jnp.transpose(b_s[:, :t])[None, None, :, :, None].astype(v.dtype)
    mixed = jnp.einsum("gts,bcsgd->bctgd", w, vc) + bias
    return u * mixed.reshape(bsz, L, SGU_WIDTH), v


def ssd(x, dt, a, b_in, c_in, h0):
    bsz, L, H, P = x.shape
    G, N = b_in.shape[2], b_in.shape[3]
    R = H // G
    q = min(SSD_CHUNK, L)
    assert L % q == 0
    nc = L // q
    f32 = jnp.float32
    xdt = (x.astype(f32) * dt[..., None]).reshape(bsz, nc, q, G, R, P)
    da = (dt * a).reshape(bsz, nc, q, G, R).transpose(0, 3, 4, 1, 2)
    cs = jnp.cumsum(da, axis=-1)
    bc = b_in.astype(f32).reshape(bsz, nc, q, G, N)
    cc = c_in.astype(f32).reshape(bsz, nc, q, G, N)
    causal = jnp.tril(jnp.ones((q, q), dtype=bool))
    seg = cs[..., :, None] - cs[..., None, :]
    lmat = jnp.where(causal, jnp.exp(jnp.where(causal, seg, 0.0)), 0.0)
    cb = jnp.einsum("bclgn,bcsgn->bcgls", cc, bc)
    y_diag = jnp.einsum("bcgls,bgrcls,bcsgrp->bclgrp", cb, lmat, xdt)
    decay_s = jnp.exp(cs[..., -1:] - cs)
    states = jnp.einsum("bclgn,bgrcl,bclgrp->bcgrpn", bc, decay_s, xdt)
    chunk_decay = jnp.exp(cs[..., -1])

    def step(h, inp):
        s, dec = inp
        return dec[..., None, None] * h + s, h

    h_init = h0.astype(f32).reshape(bsz, G, R, P, N)
    h_fin, prev = lax.scan(step, h_init, (jnp.moveaxis(states, 1, 0), jnp.moveaxis(chunk_decay, -1, 0)))
    prev = jnp.moveaxis(prev, 0, 1)
    y_off = jnp.einsum("bclgn,bcgrpn,bgrcl->bclgrp", cc, prev, jnp.exp(cs))
    y = (y_diag + y_off).reshape(bsz, L, H, P)
    return y, h_fin.reshape(bsz, H, P, N).astype(h0.dtype)


def mamba_branch(z, xbc, dt_raw, conv_buf, h0, conv_w, conv_b, dt_bias, a_log, d_skip, norm_g):
    bsz, L, _ = xbc.shape
    xpad = jnp.concatenate([conv_buf.astype(xbc.dtype), xbc], axis=1)
    conv = conv_b.astype(xbc.dtype) + sum(conv_w[k].astype(xbc.dtype) * xpad[:, k:k + L] for k in range(CONV_WIDTH))
    conv = jax.nn.silu(conv)
    new_buf = xpad[:, -(CONV_WIDTH - 1):]
    xs, bm, cm = jnp.split(conv, [D_INNER, D_INNER + SSM_GROUPS * SSM_STATE], axis=-1)
    xs = xs.reshape(bsz, L, SSM_HEADS, SSM_HEAD_DIM)
    bm = bm.reshape(bsz, L, SSM_GROUPS, SSM_STATE)
    cm = cm.reshape(bsz, L, SSM_GROUPS, SSM_STATE)
    dt = jax.nn.softplus(dt_raw.astype(jnp.float32) + dt_bias.astype(jnp.float32))
    a = -jnp.exp(a_log.astype(jnp.float32))
    y, h_new = ssd(xs, dt, a, bm, cm, h0)
    y = y + d_skip.astype(jnp.float32)[None, None, :, None] * xs.astype(jnp.float32)
    y = y.reshape(bsz, L, D_INNER).astype(z.dtype) * jax.nn.silu(z)
    gsz = D_INNER // SSM_GROUPS
    y = rmsnorm(y.reshape(bsz, L, SSM_GROUPS, gsz), norm_g.reshape(SSM_GROUPS, gsz)).reshape(bsz, L, D_INNER)
    return y, h_new, new_buf


def layer(x, h0, conv_buf, norm1_g, w_in, conv_w, conv_b, dt_bias, a_log, d_skip, ssm_norm_g,
          sgu_ln_g, sgu_ln_b, sgu_w, sgu_b, w_out_a, w_out_b, w_o, norm2_g, w_ffn_gate, w_ffn_up, w_ffn_down):
    h = rmsnorm(x, norm1_g)
    proj = h @ w_in
    s_in, z, xbc, dt_raw, ga, gb = jnp.split(proj, list(np.cumsum(IN_SIZES)[:-1]), axis=-1)
    ya, v_rows = sgu_branch(s_in, sgu_ln_g, sgu_ln_b, sgu_w, sgu_b)
    yb, h_new, new_buf = mamba_branch(z, xbc, dt_raw, conv_buf, h0, conv_w, conv_b, dt_bias, a_log, d_skip, ssm_norm_g)
    merged = jax.nn.sigmoid(ga) * (ya @ w_out_a) + jax.nn.sigmoid(gb) * (yb @ w_out_b)
    x = x + merged @ w_o
    h2 = rmsnorm(x, norm2_g)
    x = x + (jax.nn.silu(h2 @ w_ffn_gate) * (h2 @ w_ffn_up)) @ w_ffn_down
    return x, h_new, new_buf, v_rows


def setup_inputs(seed: int = 0) -> dict:
    key = jax.random.key(seed)
    ks = jax.random.split(key, 24)
    nrm = jax.random.normal
    f32 = jnp.float32
    dt0 = jnp.exp(jax.random.uniform(ks[9], (DEPTH, SSM_HEADS), f32, np.log(1e-3), np.log(1e-1)))
    return {
        "x_prompt": nrm(ks[0], (BATCH, SEQ, D_MODEL), f32),
        "x_sample": nrm(ks[1], (DEC_BATCH, DEC_SEQ, D_MODEL), f32),
        "state_ssm": 0.1 * nrm(ks[2], (DEPTH, DEC_BATCH, SSM_HEADS, SSM_HEAD_DIM, SSM_STATE), f32),
        "state_conv": nrm(ks[3], (DEPTH, DEC_BATCH, CONV_WIDTH - 1, CONV_DIM), f32),
        "norm1_g": 1.0 + 0.02 * nrm(ks[4], (DEPTH, D_MODEL), f32),
        "w_in": nrm(ks[5], (DEPTH, D_MODEL, IN_COLS), f32) * D_MODEL ** -0.5,
        "conv_w": nrm(ks[6], (DEPTH, CONV_WIDTH, CONV_DIM), f32) * CONV_WIDTH ** -0.5,
        "conv_b": 0.02 * nrm(ks[7], (DEPTH, CONV_DIM), f32),
        "dt_bias": dt0 + jnp.log(-jnp.expm1(-dt0)),
        "a_log": jnp.log(jax.random.uniform(ks[10], (DEPTH, SSM_HEADS), f32, 1.0, 16.0)),
        "d_skip": 1.0 + 0.02 * nrm(ks[11], (DEPTH, SSM_HEADS), f32),
        "ssm_norm_g": 1.0 + 0.02 * nrm(ks[12], (DEPTH, D_INNER), f32),
        "sgu_ln_g": 1.0 + 0.02 * nrm(ks[13], (DEPTH, SGU_WIDTH), f32),
        "sgu_ln_b": 0.02 * nrm(ks[14], (DEPTH, SGU_WIDTH), f32),
        "sgu_w": nrm(ks[15], (DEPTH, SGU_GROUPS, SGU_CHUNK, SGU_CHUNK), f32) * SGU_CHUNK ** -0.5,
        "sgu_b": 1.0 + 0.02 * nrm(ks[16], (DEPTH, SGU_GROUPS, SGU_CHUNK), f32),
        "w_out_a": nrm(ks[17], (DEPTH, SGU_WIDTH, D_MODEL), f32) * SGU_WIDTH ** -0.5,
        "w_out_b": nrm(ks[18], (DEPTH, D_INNER, D_MODEL), f32) * D_INNER ** -0.5,
        "w_o": nrm(ks[19], (DEPTH, D_MODEL, D_MODEL), f32) * D_MODEL ** -0.5,
        "norm2_g": 1.0 + 0.02 * nrm(ks[20], (DEPTH, D_MODEL), f32),
        "w_ffn_gate": nrm(ks[21], (DEPTH, D_MODEL, FFN_HIDDEN), f32) * D_MODEL ** -0.5,
        "w_ffn_up": nrm(ks[22], (DEPTH, D_MODEL, FFN_HIDDEN), f32) * D_MODEL ** -0.5,
        "w_ffn_down": nrm(ks[23], (DEPTH, FFN_HIDDEN, D_MODEL), f32) * FFN_HIDDEN ** -0.5,
        "final_norm_g": 1.0 + 0.02 * nrm(ks[8], (D_MODEL,), f32),
    }


def reference(x_prompt, x_sample, state_ssm, state_conv, norm1_g, w_in, conv_w, conv_b, dt_bias, a_log,
              d_skip, ssm_norm_g, sgu_ln_g, sgu_ln_b, sgu_w, sgu_b, w_out_a, w_out_b, w_o, norm2_g,
              w_ffn_gate, w_ffn_up, w_ffn_down, final_norm_g):
    xp, xs = x_prompt, x_sample
    bp = x_prompt.shape[0]
    h0_prompt = jnp.zeros((bp, SSM_HEADS, SSM_HEAD_DIM, SSM_STATE), state_ssm.dtype)
    buf_prompt = jnp.zeros((bp, CONV_WIDTH - 1, CONV_DIM), state_conv.dtype)
    ssm_p, conv_p, ssm_s, conv_s, v_s = [], [], [], [], []
    for l in range(DEPTH):
        w = (norm1_g[l], w_in[l], conv_w[l], conv_b[l], dt_bias[l], a_log[l], d_skip[l], ssm_norm_g[l],
             sgu_ln_g[l], sgu_ln_b[l], sgu_w[l], sgu_b[l], w_out_a[l], w_out_b[l], w_o[l], norm2_g[l],
             w_ffn_gate[l], w_ffn_up[l], w_ffn_down[l])
        xp, hp, bufp, _ = layer(xp, h0_prompt, buf_prompt, *w)
        xs, hs, bufs, vs = layer(xs, state_ssm[l], state_conv[l], *w)
        ssm_p.append(hp)
        conv_p.append(bufp)
        ssm_s.append(hs)
        conv_s.append(bufs)
        v_s.append(vs)
    y_prompt = rmsnorm(xp, final_norm_g)
    y_sample = rmsnorm(xs, final_norm_g)
    return (y_prompt, y_sample, jnp.stack(ssm_p), jnp.stack(conv_p), jnp.stack(ssm_s), jnp.stack(conv_s), jnp.stack(v_s))
```

```python
import numpy as np
from contextlib import ExitStack
import concourse.bass as bass
import concourse.mybir as mybir
from concourse.bass_utils import run_bass_kernel_spmd

F32 = mybir.dt.float32
BF16 = mybir.dt.bfloat16
AF = mybir.ActivationFunctionType
ALU = mybir.AluOpType

D = 2048
DEPTH = 4
DI = 4096
NH = 64
CONVD = 6144
FFN = 5632
INC = 18496
C_U, C_V, C_Z, C_X, C_DT, C_GA, C_GB = 0, 2048, 4096, 8192, 14336, 14400, 16448
EPS = 1e-6
SEM_EPOCH = 30000
NW = 8


class Buf:
    __slots__ = ("name", "w", "r", "alias")

    def __init__(self, name):
        self.name = name
        self.w = None
        self.r = []
        self.alias = []


def alias(*bufs):
    for a in bufs:
        for b in bufs:
            if a is not b and b not in a.alias:
                a.alias.append(b)


class Prog:
    ENGS = ("pe", "act", "dve", "pool", "sp")

    def __init__(self, nc, stack):
        self.nc = nc
        self.stack = stack
        self.nsem = 0
        self.dma_total = {}
        self.q = {e: [] for e in self.ENGS}
        self.cnt = {e: 0 for e in self.ENGS}
        self.sem = {e: self.new_sem("c_" + e) for e in self.ENGS}
        self.waited = {e: {} for e in self.ENGS}

    def new_sem(self, name):
        self.nsem += 1
        return self.stack.enter_context(self.nc.semaphore("%s_%d" % (name, self.nsem)))

    def new_dma_sem(self, name):
        s = self.new_sem("d_" + name)
        self.dma_total[id(s)] = 0
        return s

    def op(self, e, fn, reads=(), writes=(), dma_sem=None):
        need = {}

        def add(dep):
            if dep is None:
                return
            sem, val, src = dep
            sid = id(sem)
            if sid in self.dma_total:
                val = self.dma_total[sid]
            elif src == e and e == "pe":
                return
            if need.get(sid, (None, 0))[1] < val:
                need[sid] = (sem, val)

        for b in reads:
            add(b.w)
        for b in writes:
            for bb in [b] + b.alias:
                add(bb.w)
                for r in bb.r:
                    add(r)
        waits = []
        wd = self.waited[e]
        for sid, (sem, val) in need.items():
            if wd.get(sid, 0) >= val:
                continue
            wd[sid] = val
            waits.append((sem, val))
        if fn is None:
            self.q[e].append((waits, None, None))
            return None
        if dma_sem is not None:
            sid = id(dma_sem)
            self.dma_total[sid] += 16
            comp = (dma_sem, self.dma_total[sid], "dma")
            inc = (dma_sem, 16)
        else:
            if self.cnt[e] >= SEM_EPOCH:
                self.sem[e] = self.new_sem("c_" + e)
                self.cnt[e] = 0
            self.cnt[e] += 1
            comp = (self.sem[e], self.cnt[e], e)
            inc = (self.sem[e], 1)
        for b in reads:
            b.r.append(comp)
            if len(b.r) > 64:
                b.r = b.r[-48:]
        for b in writes:
            b.w = comp
            b.r = []
        self.q[e].append((waits, fn, inc))
        return comp

    def emit(self):
        nc = self.nc
        with nc.Block() as block:
            def run(eng, lst):
                for waits, fn, inc in lst:
                    for sem, val in waits:
                        eng.wait_ge(sem, val)
                    if fn is None:
                        continue
                    ins = fn(eng)
                    ins.then_inc(inc[0], inc[1])

            @block.tensor
            def _(eng):
                run(eng, self.q["pe"])

            @block.scalar
            def _(eng):
                run(eng, self.q["act"])

            @block.vector
            def _(eng):
                run(eng, self.q["dve"])

            @block.gpsimd
            def _(eng):
                run(eng, self.q["pool"])

            @block.sync
            def _(eng):
                run(eng, self.q["sp"])


def build_program(TP=2048, NL=DEPTH, NSB=16):
    nc = bass.Bass("TRN2", target_bir_lowering=False)
    QS = NSB * 4
    NPC = TP // 128

    def din(name, shape, dt=F32):
        return nc.dram_tensor(name, list(shape), dt, kind="ExternalInput").ap()

    def dout(name, shape):
        return nc.dram_tensor(name, list(shape), F32, kind="ExternalOutput").ap()

    xp = din("xp", [TP, D]); xs = din("xs", [QS, D])
    sssm = din("sssm", [NL, NSB, 32, 128, 128]); sconv = din("sconv", [NL, NSB * 3, CONVD])
    w_in = din("w_in", [NL, D, INC]); w_oa = din("w_oa", [NL, D, D]); w_ob = din("w_ob", [NL, DI, D])
    w_o = din("w_o", [NL, D, D]); w_g = din("w_g", [NL, D, FFN]); w_u = din("w_u", [NL, D, FFN]); w_d = din("w_d", [NL, FFN, D])
    n1g = din("n1g", [NL, 128, 16]); n2g = din("n2g", [NL, 128, 16]); ssmg = din("ssmg", [NL, 128, 32])
    cwc = din("cwc", [NL, 128, 48 * 4]); cbc = din("cbc", [NL, 128, 48])
    lng_d = din("lng", [NL, 128, D]); lnb_d = din("lnb", [NL, 128, D])
    dtb_d = din("dtb", [NL, 128, NH]); alog_d = din("alog", [NL, 128, NH]); dsk_d = din("dsk", [NL, 128, NH])
    fng_d = din("fng", [128, D])
    wsp_d = din("wsp", [NL, 128, 16 * 128]); wss_d = din("wss", [NL, QS, 16 * QS])
    sbp_d = din("sbp", [NL, 1, 16 * 128]); sbs_d = din("sbs", [NL, 1, 16 * QS])
    cst = {}
    for nm, shp in (("ident", [128, 128]), ("trip", [128, 128]), ("m1p", [128, 128]), ("tris", [QS, QS]), ("m1s", [QS, QS]),
                    ("lh0", [128, 128]), ("lh1", [128, 128]), ("lastp", [128, 1]), ("lasts", [QS, NSB]), ("blks", [QS, NSB])):
        cst[nm] = din("c_" + nm, shp)
    yp = dout("yp", [TP, D]); ys = dout("ys", [QS, D])
    ossm_p = dout("ossm_p", [NL, 32, 128, 128]); oconv_p = dout("oconv_p", [NL, 3, CONVD])
    ossm_s = dout("ossm_s", [NL, NSB, 32, 128, 128]); oconv_s = dout("oconv_s", [NL, NSB * 3, CONVD])
    ov_s = dout("ov_s", [NL, QS, D])

    with ExitStack() as st:
        P = Prog(nc, st)

        def sb(name, shape, dt):
            return st.enter_context(nc.sbuf_tensor(name, list(shape), dt))

        xt = sb("xt", [128, D], F32); b_xt = Buf("xt")
        hn = sb("hn", [128, D], BF16); b_hn = Buf("hn")
        hT = sb("hT", [128, 16, 128], BF16); b_hT = Buf("hT")
        wring = [sb("wr%d" % i, [128, 1024], BF16) for i in range(NW)]
        b_wr = [Buf("wr%d" % i) for i in range(NW)]
        s_wr = [P.new_dma_sem("wr%d" % i) for i in range(NW)]
        uT = sb("uT", [128, 16, 128], BF16); b_uT = Buf("uT")
        A1 = sb("A1", [128, 4096], F32)
        b_v = Buf("v"); b_vn = Buf("vn"); b_y = Buf("y"); b_sg = Buf("sgate")
        alias(b_v, b_vn, b_y, b_sg)
        v_ap = A1[:, 0:2048]; vn_ap = A1[:, 2048:4096]; y_ap = A1
        sg_ap = A1[:, 0:FFN // 2].bitcast(BF16)
        A2 = sb("A2", [128, 4096], F32)
        b_xdt = Buf("xdt"); b_xdtd = Buf("xdtd"); b_hid = Buf("hid"); b_ctok = Buf("ctok")
        alias(b_xdt, b_xdtd, b_hid, b_ctok)
        xdt_ap = A2[:, 0:2048].bitcast(BF16); xdtd_ap = A2[:, 2048:4096].bitcast(BF16)
        hid_ap = A2[:, 0:FFN // 2].bitcast(BF16)
        ctok_ap = A2[:, 0:1024]
        A3 = sb("A3", [128, 48 * 128], BF16)
        b_cx = Buf("cxT"); b_hidT = Buf("hidT"); alias(b_cx, b_hidT)
        cxT = A3[:].rearrange("p (j t) -> p j t", t=128)
        hidT = A3[:, 0:44 * 128].rearrange("p (j t) -> p j t", t=128)
        A4 = sb("A4", [128, 48 * 131], F32)
        b_xbc = Buf("xbcT"); b_mg = Buf("merged"); b_ybb = Buf("ybb"); b_ybT = Buf("ybT")
        alias(b_xbc, b_mg, b_ybb, b_ybT)
        mg_ap = A4[:, 0:2048]
        ybb_ap = A4[:, 2048:4096].bitcast(BF16)
        ybT = A4[:, 4096:6144].bitcast(BF16).rearrange("p (j t) -> p j t", t=128)
        A5 = sb("A5", [128, 4096], F32)
        b_lng = Buf("lng"); b_lnb = Buf("lnb"); b_xst = Buf("xs_tok"); b_stT = Buf("stT"); b_fng = Buf("fng")
        alias(b_lng, b_lnb, b_xst, b_stT, b_fng)
        lng_ap = A5[:, 0:2048]; lnb_ap = A5[:, 2048:4096]; fng_ap = A5[:, 0:2048]
        xst_ap = A5[:, 0:2048].bitcast(BF16); stT_ap = A5[:, 2048:4096].bitcast(BF16)
        s_ln = P.new_dma_sem("ln")
        sz = sb("sz", [128, DI], BF16); b_sz = Buf("sz")
        Btok = sb("Btok", [128, 1024], BF16); b_Btok = Buf("Btok")
        sga = sb("sga", [128, D], BF16); b_sga = Buf("sga")
        sgb = sb("sgb", [128, D], BF16); b_sgb = Buf("sgb")
        St = sb("St", [128, 32, 128], F32); b_St = Buf("St"); s_St = P.new_dma_sem("St")
        Rf = sb("Rf", [128, 1024], F32); b_R = Buf("R")
        Ef = sb("Ef", [128, 1024], BF16); b_E = Buf("E")
        cbm = sb("cbm", [128, 128], BF16); b_cbm = Buf("cbm")
        cbLf = sb("cbLf", [128, 1024], BF16); b_cbL = Buf("cbL")
        tmpg = sb("tmpg", [128, 512], F32); b_tmpg = Buf("tmpg")
        tmp2 = sb("tmp2", [128, 512], F32); b_tmp2 = Buf("tmp2")
        gel = [sb("gel%d" % i, [128, 512], F32) for i in range(2)]; b_gel = [Buf("gel0"), Buf("gel1")]
        Bm = sb("Bm", [128, 1024], BF16); b_Bm = Buf("Bm")
        dcol = sb("dcol", [128, 32 * 16], F32); b_dcol = Buf("dcol")
        rh0 = sb("rh0", [128, 32 * 16], F32); b_rh0 = Buf("rh0")
        rh1 = sb("rh1", [128, 32 * 16], F32); b_rh1 = Buf("rh1")
        small = sb("small", [128, 8 * 64], F32); b_small = Buf("small")
        dtt = small[:, 0:64]; dat = small[:, 64:128]; cst_ = small[:, 128:192]; dte = small[:, 192:256]; ecs = small[:, 256:320]
        stat = sb("stat", [128, 32], F32); b_stat = Buf("stat")
        wsl = sb("wsl", [128, 16 * 128], BF16); b_wsl = Buf("wsl"); s_wsl = P.new_dma_sem("wsl")
        wsm = sb("wsm", [128, 16, 128], BF16); b_wsm = Buf("wsm")
        sbrow = sb("sbrow", [1, 16 * 128], BF16); b_sbrow = Buf("sbrow"); s_sbrow = P.new_dma_sem("sbrow")
        cw_t = sb("cw_t", [128, 48, 4], F32); cb_t = sb("cb_t", [128, 48], F32)
        n1g_t = sb("n1g_t", [128, 16], F32); n2g_t = sb("n2g_t", [128, 16], F32); ssmg_t = sb("ssmg_t", [128, 32], F32)
        dtb_t = sb("dtb_t", [128, NH], F32); a_t = sb("a_t", [128, NH], F32); dsk_t = sb("dsk_t", [128, NH], F32)
        b_par = Buf("par"); s_par = P.new_dma_sem("par")
        halo = sb("halo", [128, 48, 3], F32); b_halo = Buf("halo")
        acc = [sb("acc%d" % i, [128, 128], F32) for i in range(4)]; b_acc = [Buf("acc%d" % i) for i in range(4)]
        hs_t = sb("hs_t", [NSB * 3 if NSB * 3 <= 128 else 128, CONVD // 4], F32)
        b_hs = Buf("hs"); s_hs = P.new_dma_sem("hs")
        cT = {}
        for nm in cst:
            shp = list(cst[nm].shape)
            cT[nm] = sb("k_" + nm, shp, F32)
        identb = sb("identb", [128, 128], BF16); trib_p = sb("trib_p", [128, 128], BF16); trib_s = sb("trib_s", [QS, QS], BF16)
        onesb = sb("onesb", [1, 128], BF16)
        b_cst = Buf("cst"); s_cst = P.new_dma_sem("cst")
        s_x = P.new_dma_sem("x"); s_out = P.new_dma_sem("out")

        pbank = [st.enter_context(nc.psum_tensor("pb%d" % i, [128, 512], F32)) for i in range(8)]
        b_pb = [Buf("pb%d" % i) for i in range(8)]

        def dma(eng, out, in_, sem, reads=(), writes=()):
            P.op(eng, lambda e: e.dma_start(out=out, in_=in_), reads=reads, writes=writes, dma_sem=sem)

        def mm(out, lhsT, rhs, start, stop, reads, writes):
            P.op("pe", lambda e: e.matmul(out, lhsT=lhsT, rhs=rhs, start=start, stop=stop), reads=reads, writes=writes)

        def tr(out, in_, ident, reads, writes):
            P.op("pe", lambda e: e.transpose(out=out, in_=in_, identity=ident), reads=list(reads) + [b_cst], writes=writes)

        def act(out, in_, func, reads, writes, bias=None, scale=None, accum=None):
            kw = {}
            if bias is not None:
                kw["bias"] = bias
            if scale is not None:
                kw["scale"] = scale
            if accum is not None:
                kw["accum_out"] = accum
            P.op("act", lambda e: e.activation(out=out, in_=in_, func=func, **kw), reads=reads, writes=writes)

        def tt(out, in0, in1, op, reads, writes, eng="dve"):
            P.op(eng, lambda e: e.tensor_tensor(out=out, in0=in0, in1=in1, op=op), reads=reads, writes=writes)

        def ts(out, in0, s1, s2, op0, op1, reads, writes, eng="dve"):
            if op1 is None:
                P.op(eng, lambda e: e.tensor_scalar(out=out, in0=in0, scalar1=s1, scalar2=None, op0=op0), reads=reads, writes=writes)
            else:
                P.op(eng, lambda e: e.tensor_scalar(out=out, in0=in0, scalar1=s1, scalar2=s2, op0=op0, op1=op1), reads=reads, writes=writes)

        def stt(out, in0, scalar, in1, op0, op1, reads, writes, eng="dve"):
            P.op(eng, lambda e: e.scalar_tensor_tensor(out=out, in0=in0, scalar=scalar, in1=in1, op0=op0, op1=op1), reads=reads, writes=writes)

        def cp(out, in_, reads, writes, eng="dve"):
            P.op(eng, lambda e: e.tensor_copy(out=out, in_=in_), reads=reads, writes=writes)

        ring = {"slot": 0, "grp": 0, "trb": 0}

        def linear(lhs_fn, nk, W, col0, ncols, mode, consumer, q, act_reads):
            for g0 in range(col0, col0 + ncols, 1024):
                gw = min(1024, col0 + ncols - g0)
                gi = ring["grp"]; ring["grp"] ^= 1
                banks = (2 * gi, 2 * gi + 1)
                for k in range(nk):
                    s = ring["slot"]; ring["slot"] = (s + 1) % NW
                    dma("pool", wring[s][:, 0:gw], W[k * 128:(k + 1) * 128, g0:g0 + gw], s_wr[s], writes=[b_wr[s]])
                    if mode == "tok":
                        for n in range((gw + 511) // 512):
                            w = min(512, gw - n * 512)
                            mm(pbank[banks[n]][0:q, 0:w], lhs_fn(k), wring[s][:, n * 512:n * 512 + w], k == 0, k == nk - 1,
                               reads=[b_wr[s]] + act_reads, writes=[b_pb[banks[n]]])
                    else:
                        for jj in range(gw // 128):
                            n, r = divmod(jj, 4)
                            mm(pbank[banks[n]][:, r * 128:r * 128 + q], wring[s][:, jj * 128:(jj + 1) * 128], lhs_fn(k), (k == 0 and r == 0), k == nk - 1,
                               reads=[b_wr[s]] + act_reads, writes=[b_pb[banks[n]]])
                for n in range((gw + 511) // 512):
                    w = min(512, gw - n * 512)
                    consumer(banks[n], g0 + n * 512, w)

        def transpose_to(dst3, dst_buf, src, src_buf, ntile, q, scale_col=None, dt=BF16):
            for t0 in range(0, ntile, 4):
                nt = min(4, ntile - t0)
                bi = 4 + ring["trb"]; ring["trb"] ^= 1
                pv = pbank[bi][:].bitcast(BF16) if dt == BF16 else pbank[bi][:]
                for i in range(nt):
                    tr(pv[:, i * 128:i * 128 + q], src[0:q, (t0 + i) * 128:(t0 + i + 1) * 128],
                       (identb if dt == BF16 else cT["ident"])[0:q, 0:q], reads=[src_buf], writes=[b_pb[bi]])
                if scale_col is None:
                    if q == 128:
                        cp(dst3[:, t0:t0 + nt, :], pv[:, 0:nt * 128].rearrange("p (a b) -> p a b", b=128), reads=[b_pb[bi]], writes=[dst_buf])
                    else:
                        cp(dst3[:, t0:t0 + nt, 0:q], pv[:, 0:nt * 128].rearrange("p (a b) -> p a b", b=128)[:, :, 0:q], reads=[b_pb[bi]], writes=[dst_buf])
                else:
                    for i in range(nt):
                        act(dst3[:, t0 + i, 0:q], pv[:, i * 128:i * 128 + q], AF.Identity, reads=[b_pb[bi], b_par], writes=[dst_buf],
                            scale=scale_col[:, t0 + i:t0 + i + 1])

        def rmsnorm_to_T(q, gcol):
            act(hn[0:q, :], xt[0:q, :], AF.Square, reads=[b_xt], writes=[b_hn, b_stat], accum=stat[0:q, 0:1])
            P.op("dve", None, reads=[b_hn])
            ts(stat[0:q, 1:2], stat[0:q, 0:1], 1.0 / D, EPS, ALU.mult, ALU.add, reads=[b_hn, b_stat], writes=[b_stat])
            act(stat[0:q, 2:3], stat[0:q, 1:2], AF.Sqrt, reads=[b_stat], writes=[b_stat])
            P.op("dve", lambda e: e.reciprocal(out=stat[0:q, 3:4], in_=stat[0:q, 2:3]), reads=[b_stat], writes=[b_stat])
            act(hn[0:q, :], xt[0:q, :], AF.Identity, reads=[b_xt, b_stat], writes=[b_hn], scale=stat[0:q, 3:4])
            transpose_to(hT, b_hT, hn, b_hn, 16, q, scale_col=gcol)

        def gelu_bank(bi, out_ap, out_buf, shape_fn, extra_writes=()):
            gi = bi % 2
            src = shape_fn(pbank[bi])
            t = shape_fn(gel[gi])
            act(t, src, AF.Square, reads=[b_pb[bi]], writes=[b_gel[gi]])
            ts(t, t, 0.044715, 1.0, ALU.mult, ALU.add, reads=[b_gel[gi]], writes=[b_gel[gi]])
            tt(t, t, src, ALU.mult, reads=[b_gel[gi], b_pb[bi]], writes=[b_gel[gi]])
            act(t, t, AF.Sigmoid, reads=[b_gel[gi]], writes=[b_gel[gi]], scale=1.5957691216057308)
            tt(out_ap, t, src, ALU.mult, reads=[b_gel[gi], b_pb[bi]], writes=[out_buf] + list(extra_writes))

        for nm in cst:
            dma("sp", cT[nm][:], cst[nm], s_cst, writes=[b_cst])
        cp(identb[:], cT["ident"][:], reads=[b_cst], writes=[b_cst])
        cp(trib_p[:], cT["trip"][:], reads=[b_cst], writes=[b_cst])
        cp(trib_s[:], cT["tris"][:], reads=[b_cst], writes=[b_cst])
        P.op("dve", lambda e: e.memset(onesb[:], 1.0), writes=[b_cst])

        def chunk(l, kind, ci):
            prompt = kind == "p"
            q = 128 if prompt else QS
            nb = 1 if prompt else NSB
            qb = q // nb
            last_layer = l == NL - 1
            TRI = cT["trip"] if prompt else cT["tris"]
            M1 = cT["m1p"] if prompt else cT["m1s"]
            TRIB = trib_p if prompt else trib_s
            if l == 0:
                src = xp[ci * 128:(ci + 1) * 128, :] if prompt else xs[:, :]
            else:
                src = yp[ci * 128:(ci + 1) * 128, :] if prompt else ys[:, :]
            xdst = yp[ci * 128:(ci + 1) * 128, :] if prompt else ys[:, :]
            dma("sp", xt[0:q, :], src, s_x, writes=[b_xt])
            dma("sp", lng_ap, lng_d[l], s_ln, writes=[b_lng])
            dma("sp", lnb_ap, lnb_d[l], s_ln, writes=[b_lnb])
            rmsnorm_to_T(q, n1g_t)
            hfn = lambda k: hT[:, k, 0:q]
            W = w_in[l]

            def cons_u(bi, c0, w):
                j0 = (c0 - C_U) // 128
                gelu_bank(bi, uT[:, j0:j0 + 4, 0:q], b_uT,
                          lambda t: t[:, :].rearrange("p (a b) -> p a b", b=128)[:, :, 0:q])
            linear(hfn, 16, W, C_U, 2048, "feat", cons_u, q, [b_hT])

            def cons_v(bi, c0, w):
                gelu_bank(bi, v_ap[0:q, c0 - C_V:c0 - C_V + w], b_v, lambda t: t[0:q, 0:w])
            linear(hfn, 16, W, C_V, 2048, "tok", cons_v, q, [b_hT])
            act(vn_ap[0:q, :], v_ap[0:q, :], AF.Copy, reads=[b_v], writes=[b_vn, b_stat], accum=stat[0:q, 4:5])
            P.op("dve", None, reads=[b_vn])
            ts(stat[0:q, 5:6], stat[0:q, 4:5], -1.0 / D, None, ALU.mult, None, reads=[b_vn, b_stat], writes=[b_stat])
            act(vn_ap[0:q, :], v_ap[0:q, :], AF.Square, reads=[b_v, b_stat], writes=[b_vn, b_stat], bias=stat[0:q, 5:6], accum=stat[0:q, 6:7])
            P.op("dve", None, reads=[b_vn])
            ts(stat[0:q, 7:8], stat[0:q, 6:7], 1.0 / D, EPS, ALU.mult, ALU.add, reads=[b_vn, b_stat], writes=[b_stat])
            act(stat[0:q, 8:9], stat[0:q, 7:8], AF.Sqrt, reads=[b_stat], writes=[b_stat])
            P.op("dve", lambda e: e.reciprocal(out=stat[0:q, 9:10], in_=stat[0:q, 8:9]), reads=[b_stat], writes=[b_stat])
            tt(stat[0:q, 10:11], stat[0:q, 5:6], stat[0:q, 9:10], ALU.mult, reads=[b_stat], writes=[b_stat])
            act(vn_ap[0:q, :], v_ap[0:q, :], AF.Identity, reads=[b_v, b_stat], writes=[b_vn], bias=stat[0:q, 10:11], scale=stat[0:q, 9:10])
            tt(vn_ap[0:q, :], vn_ap[0:q, :], lng_ap[0:q, :], ALU.mult, reads=[b_vn, b_lng], writes=[b_vn])
            tt(vn_ap[0:q, :], vn_ap[0:q, :], lnb_ap[0:q, :], ALU.add, reads=[b_vn, b_lnb], writes=[b_vn])
            cp(hn[0:q, :], vn_ap[0:q, :], reads=[b_vn], writes=[b_hn])
            if not prompt:
                dma("sp", ov_s[l], vn_ap[0:q, :], s_out, reads=[b_vn])
            for g in range(16):
                bi = 4 + (g % 2)
                mm(pbank[bi][:, 0:q], hn[0:q, g * 128:(g + 1) * 128], wsm[0:q, g, 0:q], True, False, reads=[b_hn, b_wsm], writes=[b_pb[bi]])
                mm(pbank[bi][:, 0:q], onesb[0:1, :], sbrow[0:1, g * q:(g + 1) * q], False, True, reads=[b_sbrow, b_cst], writes=[b_pb[bi]])
                tt(uT[:, g, 0:q], uT[:, g, 0:q], pbank[bi][:, 0:q], ALU.mult, reads=[b_uT, b_pb[bi]], writes=[b_uT])

            def cons_z(bi, c0, w):
                act(sz[0:q, c0 - C_Z:c0 - C_Z + w], pbank[bi][0:q, 0:w], AF.Silu, reads=[b_pb[bi]], writes=[b_sz])
            linear(hfn, 16, W, C_Z, DI, "tok", cons_z, q, [b_hT])

            if prompt:
                xb3 = A4[:, :].rearrange("p (j t) -> p j t", t=131)
                newv = lambda j0, nj: xb3[:, j0:j0 + nj, 3:131]
                if ci == 0:
                    P.op("dve", lambda e: e.memset(halo[:], 0.0), writes=[b_halo])
                cp(xb3[:, :, 0:3], halo[:], reads=[b_halo], writes=[b_xbc])
            else:
                xb4 = A4[:, 0:48 * NSB * 7].rearrange("p (j b t) -> p j b t", b=NSB, t=7)
                newv = lambda j0, nj: xb4[:, j0:j0 + nj, :, 3:7]
                for qq in range(4):
                    dma("sp", hs_t[0:NSB * 3, :], sconv[l][:, qq * 1536:(qq + 1) * 1536], s_hs, writes=[b_hs])
                    for jj in range(12):
                        j = qq * 12 + jj
                        bi = 4 + (j % 2)
                        tr(pbank[bi][:, 0:NSB * 3], hs_t[0:NSB * 3, jj * 128:(jj + 1) * 128], cT["ident"][0:NSB * 3, 0:NSB * 3], reads=[b_hs], writes=[b_pb[bi]])
                        cp(xb4[:, j, :, 0:3], pbank[bi][:, 0:NSB * 3].rearrange("p (b r) -> p b r", r=3), reads=[b_pb[bi]], writes=[b_xbc])

            def cons_x(bi, c0, w):
                j0 = (c0 - C_X) // 128
                if prompt:
                    cp(newv(j0, 4), pbank[bi][:, :].rearrange("p (a b) -> p a b", b=128), reads=[b_pb[bi]], writes=[b_xbc])
                else:
                    cp(newv(j0, 4), pbank[bi][:, :].rearrange("p (a b) -> p a b", b=128)[:, :, 0:q].rearrange("p a (b t) -> p a b t", t=4),
                       reads=[b_pb[bi]], writes=[b_xbc])
            linear(hfn, 16, W, C_X, CONVD, "feat", cons_x, q, [b_hT])

            def cons_dt(bi, c0, w):
                tt(dtt[0:q, :], pbank[bi][0:q, 0:64], dtb_t[0:q, :], ALU.add, reads=[b_pb[bi], b_par], writes=[b_small])
                act(dtt[0:q, :], dtt[0:q, :], AF.Exp, reads=[b_small], writes=[b_small])
                ts(dtt[0:q, :], dtt[0:q, :], 1.0, None, ALU.add, None, reads=[b_small], writes=[b_small])
                act(dtt[0:q, :], dtt[0:q, :], AF.Ln, reads=[b_small], writes=[b_small])
                tt(dat[0:q, :], dtt[0:q, :], a_t[0:q, :], ALU.mult, reads=[b_small, b_par], writes=[b_small])
            linear(hfn, 16, W, C_DT, 64, "tok", cons_dt, q, [b_hT])

            def cons_ga(bi, c0, w):
                act(sga[0:q, c0 - C_GA:c0 - C_GA + w], pbank[bi][0:q, 0:w], AF.Sigmoid, reads=[b_pb[bi]], writes=[b_sga])

            def cons_gb(bi, c0, w):
                act(sgb[0:q, c0 - C_GB:c0 - C_GB + w], pbank[bi][0:q, 0:w], AF.Sigmoid, reads=[b_pb[bi]], writes=[b_sgb])
            linear(hfn, 16, W, C_GA, 2048, "tok", cons_ga, q, [b_hT])
            linear(hfn, 16, W, C_GB, 2048, "tok", cons_gb, q, [b_hT])

            if (prompt and ci == NPC - 1) or not prompt:
                dstv = None if prompt else oconv_s[l].rearrange("(b r) c -> r b c", r=3)
                for bt in range(6):
                    for r in range(1 if prompt else 3):
                        for half in range(2):
                            bi = 4 + half
                            for i in range(4):
                                j = bt * 8 + half * 4 + i
                                if prompt:
                                    tr(pbank[bi][0:q, i * 128:(i + 1) * 128], xb3[:, j, 3:131], cT["ident"][:, :], reads=[b_xbc], writes=[b_pb[bi]])
                                else:
                                    tr(pbank[bi][0:NSB, i * 128:(i + 1) * 128], xb4[:, j, :, 4 + r], cT["ident"][:, :], reads=[b_xbc], writes=[b_pb[bi]])
                            nr = q if prompt else NSB
                            cp(ctok_ap[0:nr, half * 512:(half + 1) * 512], pbank[bi][0:nr, :], reads=[b_pb[bi]], writes=[b_ctok])
                        if prompt:
                            dma("sp", oconv_p[l][:, bt * 1024:(bt + 1) * 1024], ctok_ap[125:128, :], s_out, reads=[b_ctok])
                        else:
                            dma("sp", dstv[r][:, bt * 1024:(bt + 1) * 1024], ctok_ap[0:NSB, :], s_out, reads=[b_ctok])
            if prompt:
                cp(halo[:], xb3[:, :, 128:131], reads=[b_xbc], writes=[b_halo])

            for j in range(48):
                a = acc[j % 4]; ba = b_acc[j % 4]
                if prompt:
                    av = a[:, 0:q]
                    xk = lambda k: xb3[:, j, k:k + q]
                else:
                    av = a[:, 0:q].rearrange("p (b t) -> p b t", t=4)
                    xk = lambda k: xb4[:, j, :, k:k + 4]
                ts(av, xk(0), cw_t[:, j, 0:1], cb_t[:, j:j + 1], ALU.mult, ALU.add, reads=[b_xbc, b_par], writes=[ba])
                for k in range(1, 4):
                    stt(av, xk(k), cw_t[:, j, k:k + 1], av, ALU.mult, ALU.add, reads=[b_xbc, b_par, ba], writes=[ba])
                act(cxT[:, j, 0:q], a[:, 0:q], AF.Silu, reads=[ba], writes=[b_cx])
            transpose_to_tok(q)

            ssd(l, kind, ci, q, nb, qb, TRI, M1, TRIB)

            tt(y_ap[0:q, :], y_ap[0:q, :], sz[0:q, :], ALU.mult, reads=[b_y, b_sz], writes=[b_y])
            for g in range(8):
                act(tmpg[0:q, :], y_ap[0:q, g * 512:(g + 1) * 512], AF.Square, reads=[b_y], writes=[b_tmpg, b_stat], accum=stat[0:q, 12 + g:13 + g])
            P.op("dve", None, reads=[b_tmpg])
            ts(stat[0:q, 20:28], stat[0:q, 12:20], 1.0 / 512, EPS, ALU.mult, ALU.add, reads=[b_tmpg, b_stat], writes=[b_stat])
            act(stat[0:q, 20:28], stat[0:q, 20:28], AF.Sqrt, reads=[b_stat], writes=[b_stat])
            P.op("dve", lambda e: e.reciprocal(out=stat[0:q, 20:28], in_=stat[0:q, 20:28]), reads=[b_stat], writes=[b_stat])
            tt(ybb_ap[0:q, :].rearrange("p (g c) -> p g c", c=512), y_ap[0:q, :].rearrange("p (g c) -> p g c", c=512),
               stat[0:q, 20:28].unsqueeze(2).to_broadcast([q, 8, 512]), ALU.mult, reads=[b_y, b_stat], writes=[b_ybb])
            transpose_to(ybT, b_ybT, ybb_ap, b_ybb, 32, q, scale_col=ssmg_t)

            def cons_pa(bi, c0, w):
                tt(mg_ap[0:q, c0:c0 + w], pbank[bi][0:q, 0:w], sga[0:q, c0:c0 + w], ALU.mult, reads=[b_pb[bi], b_sga], writes=[b_mg])
            linear(lambda k: uT[:, k, 0:q], 16, w_oa[l], 0, D, "tok", cons_pa, q, [b_uT])

            def cons_pb(bi, c0, w):
                gi = bi % 2
                tt(gel[gi][0:q, 0:w], pbank[bi][0:q, 0:w], sgb[0:q, c0:c0 + w], ALU.mult, reads=[b_pb[bi], b_sgb], writes=[b_gel[gi]])
                tt(mg_ap[0:q, c0:c0 + w], mg_ap[0:q, c0:c0 + w], gel[gi][0:q, 0:w], ALU.add, reads=[b_mg, b_gel[gi]], writes=[b_mg])
            linear(lambda k: ybT[:, k, 0:q], 32, w_ob[l], 0, D, "tok", cons_pb, q, [b_ybT])
            cp(hn[0:q, :], mg_ap[0:q, :], reads=[b_mg], writes=[b_hn])
            transpose_to(hT, b_hT, hn, b_hn, 16, q)

            def cons_res(bi, c0, w):
                tt(xt[0:q, c0:c0 + w], xt[0:q, c0:c0 + w], pbank[bi][0:q, 0:w], ALU.add, reads=[b_pb[bi], b_xt], writes=[b_xt])
            linear(hfn, 16, w_o[l], 0, D, "tok", cons_res, q, [b_hT])

            rmsnorm_to_T(q, n2g_t)

            def cons_g(bi, c0, w):
                act(sg_ap[0:q, c0:c0 + w], pbank[bi][0:q, 0:w], AF.Silu, reads=[b_pb[bi]], writes=[b_sg])

            def cons_up(bi, c0, w):
                tt(hid_ap[0:q, c0:c0 + w], pbank[bi][0:q, 0:w], sg_ap[0:q, c0:c0 + w], ALU.mult, reads=[b_pb[bi], b_sg], writes=[b_hid])
            linear(hfn, 16, w_g[l], 0, FFN, "tok", cons_g, q, [b_hT])
            linear(hfn, 16, w_u[l], 0, FFN, "tok", cons_up, q, [b_hT])
            transpose_to(hidT, b_hidT, hid_ap, b_hid, 44, q)
            linear(lambda k: hidT[:, k, 0:q], 44, w_d[l], 0, D, "tok", cons_res, q, [b_hidT])

            if last_layer:
                dma("sp", fng_ap, fng_d, s_ln, writes=[b_fng])
                act(hn[0:q, :], xt[0:q, :], AF.Square, reads=[b_xt], writes=[b_hn, b_stat], accum=stat[0:q, 0:1])
                P.op("dve", None, reads=[b_hn])
                ts(stat[0:q, 1:2], stat[0:q, 0:1], 1.0 / D, EPS, ALU.mult, ALU.add, reads=[b_hn, b_stat], writes=[b_stat])
                act(stat[0:q, 2:3], stat[0:q, 1:2], AF.Sqrt, reads=[b_stat], writes=[b_stat])
                P.op("dve", lambda e: e.reciprocal(out=stat[0:q, 3:4], in_=stat[0:q, 2:3]), reads=[b_stat], writes=[b_stat])
                stt(xt[0:q, :], xt[0:q, :], stat[0:q, 3:4], fng_ap[0:q, :], ALU.mult, ALU.mult, reads=[b_xt, b_stat, b_fng], writes=[b_xt])
            dma("sp", xdst, xt[0:q, :], s_x, reads=[b_xt])

        def transpose_to_tok(q):
            for t0 in range(0, 40, 4):
                bi = 4 + ring["trb"]; ring["trb"] ^= 1
                pv = pbank[bi][:].bitcast(BF16)
                for i in range(4):
                    tr(pv[0:q, i * 128:(i + 1) * 128], cxT[:, t0 + i, 0:q], identb[:, :], reads=[b_cx], writes=[b_pb[bi]])
                if t0 < 32:
                    cp(xst_ap[0:q, t0 * 128:(t0 + 4) * 128], pv[0:q, 0:512], reads=[b_pb[bi]], writes=[b_xst])
                else:
                    cp(Btok[0:q, (t0 - 32) * 128:(t0 - 28) * 128], pv[0:q, 0:512], reads=[b_pb[bi]], writes=[b_Btok])

        def ssd(l, kind, ci, q, nb, qb, TRI, M1, TRIB):
            prompt = kind == "p"
            BT = lambda g: cxT[:, 32 + g, 0:q]
            CT = lambda g: cxT[:, 40 + g, 0:q]
            mm(pbank[5][0:q, 0:64], TRI[0:q, 0:q], dat[0:q, :], True, True, reads=[b_small, b_cst], writes=[b_pb[5]])
            mm(pbank[5][0:q, 64:128], M1[0:q, 0:q], dat[0:q, :], True, True, reads=[b_small, b_cst], writes=[b_pb[5]])
            cp(cst_[0:q, :], pbank[5][0:q, 0:64], reads=[b_pb[5]], writes=[b_small])
            act(ecs[0:q, :], pbank[5][0:q, 0:64], AF.Exp, reads=[b_pb[5]], writes=[b_small])
            act(dte[0:q, :], pbank[5][0:q, 64:128], AF.Exp, reads=[b_pb[5]], writes=[b_small])
            xs3 = xst_ap[0:q, :].rearrange("p (h c) -> p h c", c=64)
            tt(xdt_ap[0:q, :].rearrange("p (h c) -> p h c", c=64), xs3, dtt[0:q, :].unsqueeze(2).to_broadcast([q, 64, 64]), ALU.mult,
               reads=[b_xst, b_small], writes=[b_xdt])
            tt(xdtd_ap[0:q, :].rearrange("p (h c) -> p h c", c=64), xdt_ap[0:q, :].rearrange("p (h c) -> p h c", c=64),
               dte[0:q, :].unsqueeze(2).to_broadcast([q, 64, 64]), ALU.mult, reads=[b_xdt, b_small], writes=[b_xdtd])
            LAST = cT["lastp"] if prompt else cT["lasts"]
            cs3 = cst_[0:q, :].rearrange("p (j two) -> p j two", two=2)
            for h2, rh, brh in ((0, rh0, b_rh0), (1, rh1, b_rh1)):
                tt(rh[0:q, 0:32 * nb].rearrange("p (j b) -> p j b", b=nb), cs3[:, :, h2:h2 + 1].to_broadcast([q, 32, nb]),
                   LAST[0:q, 0:nb].unsqueeze(1).to_broadcast([q, 32, nb]), ALU.mult, reads=[b_small, b_cst], writes=[brh])
            mm(pbank[5][:, 128:128 + 32 * nb] if nb == 1 else pbank[6][:, 0:32 * nb], cT["lh0"][0:q, :], rh0[0:q, 0:32 * nb], True, False,
               reads=[b_rh0, b_cst], writes=[b_pb[5] if nb == 1 else b_pb[6]])
            mm(pbank[5][:, 128:128 + 32 * nb] if nb == 1 else pbank[6][:, 0:32 * nb], cT["lh1"][0:q, :], rh1[0:q, 0:32 * nb], False, True,
               reads=[b_rh1, b_cst], writes=[b_pb[5] if nb == 1 else b_pb[6]])
            act(dcol[:, 0:32 * nb], pbank[5][:, 128:128 + 32 * nb] if nb == 1 else pbank[6][:, 0:32 * nb], AF.Exp,
                reads=[b_pb[5] if nb == 1 else b_pb[6]], writes=[b_dcol])
            dc3 = dcol[:, 0:32 * nb].rearrange("p (j b) -> p j b", b=nb)

            if prompt and ci == 0:
                P.op("dve", lambda e: e.memset(St[:], 0.0), writes=[b_St])
            yoT_banks = (0, 1, 2, 3)
            for b in range(nb):
                if not prompt:
                    dma("sp", St[:], sssm[l, b].rearrange("j m n -> m j n"), s_St, writes=[b_St])
                for t0 in range(0, 32, 4):
                    bi = 4 + ring["trb"]; ring["trb"] ^= 1
                    for i in range(4):
                        tr(pbank[bi][:, i * 128:(i + 1) * 128], St[:, t0 + i, :], cT["ident"][:, :], reads=[b_St], writes=[b_pb[bi]])
                    if (t0 // 4) % 2 == 0:
                        cp(stT_ap[:, t0 * 128:(t0 + 4) * 128], pbank[bi][:, :], reads=[b_pb[bi]], writes=[b_stT])
                    else:
                        act(stT_ap[:, t0 * 128:(t0 + 4) * 128], pbank[bi][:, :], AF.Copy, reads=[b_pb[bi]], writes=[b_stT])
                if not prompt:
                    for j in range(32):
                        bk = yoT_banks[j // 8]
                        mm(pbank[bk][:, (j % 8) * 64 + b * 4:(j % 8) * 64 + b * 4 + 4], stT_ap[:, j * 128:(j + 1) * 128], CT(j // 4)[:, b * 4:b * 4 + 4],
                           True, True, reads=[b_stT, b_cx], writes=[b_pb[bk]])
                    ts(Bm[0:q, :], Btok[0:q, :], cT["blks"][0:q, b:b + 1], None, ALU.mult, None, reads=[b_Btok, b_cst], writes=[b_Bm])
                    Bsrc, bB = Bm, b_Bm
                else:
                    Bsrc, bB = Btok, b_Btok
                if prompt:
                    y_off_prompt = None
                for j in range(32):
                    bi = 6 + (j % 2) if prompt else 6 + (j % 2)
                    g = j // 4
                    mm(pbank[bi][:, 0:128], xdtd_ap[0:q, j * 128:(j + 1) * 128], Bsrc[0:q, g * 128:(g + 1) * 128], True, True,
                       reads=[b_xdtd, bB], writes=[b_pb[bi]])
                    stt(St[:, j, :], St[:, j, :], dc3[:, j, b:b + 1], pbank[bi][:, 0:128], ALU.mult, ALU.add,
                        reads=[b_St, b_dcol, b_pb[bi]], writes=[b_St])
                if not prompt:
                    dma("sp", ossm_s[l, b].rearrange("j m n -> m j n"), St[:], s_St, reads=[b_St])
                elif ci == NPC - 1:
                    dma("sp", ossm_p[l].rearrange("j m n -> m j n"), St[:], s_St, reads=[b_St])
            if not prompt:
                for k4 in range(4):
                    cp(A5[:, 2048 + k4 * 512:2048 + (k4 + 1) * 512], pbank[yoT_banks[k4]][:, :], reads=[b_pb[yoT_banks[k4]]], writes=[b_stT])

            v3 = lambda t: t[0:q, 0:8 * q].rearrange("p (a b) -> p a b", b=q)
            Rt3, Et3, cbL3 = v3(Rf), v3(Ef), v3(cbLf)
            for g in range(8):
                hs = slice(8 * g, 8 * g + 8)
                tt(Rt3, dat[0:q, hs].unsqueeze(2).to_broadcast([q, 8, q]), TRI[0:q, 0:q].unsqueeze(1).to_broadcast([q, 8, q]),
                   ALU.mult, reads=[b_small, b_cst], writes=[b_R])
                nmm = (8 * q + 511) // 512
                for n in range(nmm):
                    wd_ = min(512, 8 * q - n * 512)
                    mm(pbank[n][0:q, 0:wd_], M1[0:q, 0:q], Rf[0:q, n * 512:n * 512 + wd_], True, True, reads=[b_R, b_cst], writes=[b_pb[n]])
                    act(Ef[0:q, n * 512:n * 512 + wd_], pbank[n][0:q, 0:wd_], AF.Exp, reads=[b_pb[n]], writes=[b_E])
                mm(pbank[4][0:q, 0:q], BT(g), CT(g), True, True, reads=[b_cx], writes=[b_pb[4]])
                tt(cbm[0:q, 0:q], pbank[4][0:q, 0:q], TRIB[0:q, 0:q], ALU.mult, reads=[b_pb[4], b_cst], writes=[b_cbm])
                tt(cbL3, Et3, cbm[0:q, 0:q].unsqueeze(1).to_broadcast([q, 8, q]), ALU.mult, reads=[b_E, b_cbm], writes=[b_cbL])
                for h in range(8):
                    hh = 8 * g + h
                    mm(pbank[2][0:q, h * 64:(h + 1) * 64], cbLf[0:q, h * q:(h + 1) * q], xdt_ap[0:q, hh * 64:(hh + 1) * 64], True, True,
                       reads=[b_cbL, b_xdt], writes=[b_pb[2]])
                if prompt:
                    mm(pbank[3][0:q, :], CT(g), stT_ap[:, g * 512:(g + 1) * 512], True, True, reads=[b_cx, b_stT], writes=[b_pb[3]])
                else:
                    for i in range(4):
                        j = 4 * g + i
                        tr(pbank[3][0:q, i * 128:(i + 1) * 128], A5[:, 2048 + j * 64:2048 + j * 64 + q], cT["ident"][:, :], reads=[b_stT], writes=[b_pb[3]])
                tt(tmpg[0:q, :].rearrange("p (h c) -> p h c", c=64), pbank[3][0:q, :].rearrange("p (h c) -> p h c", c=64),
                   ecs[0:q, hs].unsqueeze(2).to_broadcast([q, 8, 64]), ALU.mult, reads=[b_pb[3], b_small], writes=[b_tmpg])
                tt(tmp2[0:q, :].rearrange("p (h c) -> p h c", c=64), xst_ap[0:q, g * 512:(g + 1) * 512].rearrange("p (h c) -> p h c", c=64),
                   dsk_t[0:q, hs].unsqueeze(2).to_broadcast([q, 8, 64]), ALU.mult, reads=[b_xst, b_par], writes=[b_tmp2])
                tt(tmpg[0:q, :], tmpg[0:q, :], pbank[2][0:q, :], ALU.add, reads=[b_tmpg, b_pb[2]], writes=[b_tmpg])
                tt(y_ap[0:q, g * 512:(g + 1) * 512], tmpg[0:q, :], tmp2[0:q, :], ALU.add, reads=[b_tmpg, b_tmp2], writes=[b_y])

        for l in range(NL):
            dma("sp", n1g_t[:], n1g[l], s_par, writes=[b_par])
            dma("sp", n2g_t[:], n2g[l], s_par, writes=[b_par])
            dma("sp", ssmg_t[:], ssmg[l], s_par, writes=[b_par])
            dma("sp", cw_t[:].rearrange("p j k -> p (j k)"), cwc[l], s_par, writes=[b_par])
            dma("sp", cb_t[:], cbc[l], s_par, writes=[b_par])
            dma("sp", dtb_t[:], dtb_d[l], s_par, writes=[b_par])
            dma("sp", a_t[:], alog_d[l], s_par, writes=[b_par])
            dma("sp", dsk_t[:], dsk_d[l], s_par, writes=[b_par])
            act(a_t[:], a_t[:], AF.Exp, reads=[b_par], writes=[b_par])
            ts(a_t[:], a_t[:], -1.0, None, ALU.mult, None, reads=[b_par], writes=[b_par])
            for kind in ("p", "s"):
                q = 128 if kind == "p" else QS
                wsd = wsp_d if kind == "p" else wss_d
                sbd = sbp_d if kind == "p" else sbs_d
                TRIB = trib_p if kind == "p" else trib_s
                dma("pool", wsl[0:q, 0:16 * q], wsd[l], s_wsl, writes=[b_wsl])
                dma("pool", sbrow[0:1, 0:16 * q], sbd[l], s_sbrow, writes=[b_sbrow])
                tt(wsm[0:q, :, 0:q], wsl[0:q, 0:16 * q].rearrange("p (g t) -> p g t", t=q), TRIB[0:q, 0:q].unsqueeze(1).to_broadcast([q, 16, q]),
                   ALU.mult, reads=[b_wsl, b_cst], writes=[b_wsm])
                if kind == "p":
                    for ci in range(NPC):
                        chunk(l, "p", ci)
                else:
                    chunk(l, "s", 0)
        P.op("sp", None, writes=[b_xt, b_St, b_ctok, b_vn, b_v])
        P.emit()
    return nc


def _consts(QS, NSB):
    c = {}
    c["ident"] = np.eye(128, dtype=np.float32)
    i = np.arange(128)
    c["trip"] = (i[:, None] <= i[None, :]).astype(np.float32)
    c["m1p"] = (i[:, None] > i[None, :]).astype(np.float32)
    s = np.arange(QS)
    same = (s[:, None] // 4) == (s[None, :] // 4)
    c["tris"] = (same & (s[:, None] <= s[None, :])).astype(np.float32)
    c["m1s"] = (same & (s[:, None] > s[None, :])).astype(np.float32)
    c["lh0"] = np.zeros((128, 128), np.float32); c["lh0"][:, :64] = 1
    c["lh1"] = np.zeros((128, 128), np.float32); c["lh1"][:, 64:] = 1
    c["lastp"] = np.zeros((128, 1), np.float32); c["lastp"][127, 0] = 1
    c["lasts"] = np.zeros((QS, NSB), np.float32)
    c["blks"] = np.zeros((QS, NSB), np.float32)
    for b in range(NSB):
        c["lasts"][4 * b + 3, b] = 1
        c["blks"][4 * b:4 * b + 4, b] = 1
    return c


def _layout_weights(w, NL):
    f = np.float32
    m = {}
    col = lambda a, nt: np.ascontiguousarray(a.reshape(NL, nt, 128).transpose(0, 2, 1)).astype(f)
    m["n1g"] = col(w["norm1_g"][:NL], 16); m["n2g"] = col(w["norm2_g"][:NL], 16); m["ssmg"] = col(w["ssm_norm_g"][:NL], 32)
    cw = w["conv_w"][:NL].reshape(NL, 4, 48, 128).transpose(0, 3, 2, 1)
    m["cwc"] = np.ascontiguousarray(cw).reshape(NL, 128, 192).astype(f)
    m["cbc"] = col(w["conv_b"][:NL], 48)
    rep = lambda a: np.ascontiguousarray(np.broadcast_to(a[:, None, :], (a.shape[0], 128, a.shape[1]))).astype(f)
    m["lng"] = rep(w["sgu_ln_g"][:NL]); m["lnb"] = rep(w["sgu_ln_b"][:NL])
    m["dtb"] = rep(w["dt_bias"][:NL]); m["alog"] = rep(w["a_log"][:NL]); m["dsk"] = rep(w["d_skip"][:NL])
    m["fng"] = np.ascontiguousarray(np.broadcast_to(w["final_norm_g"][None, :], (128, D))).astype(f)
    sw = w["sgu_w"][:NL]
    m["wsp"] = np.ascontiguousarray(sw.transpose(0, 3, 1, 2)).reshape(NL, 128, 16 * 128).astype(f)
    return m


def kernel(x_prompt, x_sample, state_ssm, state_conv, norm1_g, w_in, conv_w, conv_b, dt_bias, a_log,
           d_skip, ssm_norm_g, sgu_ln_g, sgu_ln_b, sgu_w, sgu_b, w_out_a, w_out_b, w_o, norm2_g,
           w_ffn_gate, w_ffn_up, w_ffn_down, final_norm_g):
    NL = DEPTH
    NSB = 16
    QS = 64
    f = np.float32
    w = dict(norm1_g=np.asarray(norm1_g), norm2_g=np.asarray(norm2_g), ssm_norm_g=np.asarray(ssm_norm_g), conv_w=np.asarray(conv_w),
             conv_b=np.asarray(conv_b), sgu_ln_g=np.asarray(sgu_ln_g), sgu_ln_b=np.asarray(sgu_ln_b), dt_bias=np.asarray(dt_bias),
             a_log=np.asarray(a_log), d_skip=np.asarray(d_skip), final_norm_g=np.asarray(final_norm_g), sgu_w=np.asarray(sgu_w))
    m = _layout_weights(w, NL)
    sw = np.asarray(sgu_w); sbias = np.asarray(sgu_b)
    w4 = np.ascontiguousarray(sw[:, :, :4, :4].transpose(0, 3, 1, 2))
    m["wss"] = np.ascontiguousarray(np.tile(w4, (1, NSB, 1, NSB))).reshape(NL, QS, 16 * QS).astype(f)
    m["sbp"] = np.ascontiguousarray(sbias.reshape(NL, 1, 16 * 128)).astype(f)
    m["sbs"] = np.ascontiguousarray(np.tile(sbias[:, :, :4], (1, 1, NSB))).reshape(NL, 1, 16 * QS).astype(f)
    big = {"w_in": np.asarray(w_in), "w_oa": np.asarray(w_out_a), "w_ob": np.asarray(w_out_b), "w_o": np.asarray(w_o),
           "w_g": np.asarray(w_ffn_gate), "w_u": np.asarray(w_ffn_up), "w_d": np.asarray(w_ffn_down)}
    cs = _consts(QS, NSB)
    xp_all = np.asarray(x_prompt); xs_all = np.asarray(x_sample)
    ssm_all = np.asarray(state_ssm); conv_all = np.asarray(state_conv)
    nc = build_program(2048, NL, NSB)
    in_maps = []
    for c in range(8):
        d = dict(m)
        d.update(big)
        for k, v in cs.items():
            d["c_" + k] = v
        d["xp"] = np.ascontiguousarray(xp_all[c % 4])
        d["xs"] = np.ascontiguousarray(xs_all[c * NSB:(c + 1) * NSB].reshape(QS, D))
        d["sssm"] = np.ascontiguousarray(ssm_all[:, c * NSB:(c + 1) * NSB].reshape(NL, NSB, 32, 128, 128))
        d["sconv"] = np.ascontiguousarray(conv_all[:, c * NSB:(c + 1) * NSB].reshape(NL, NSB * 3, CONVD))
        in_maps.append(d)
    res = run_bass_kernel_spmd(nc, in_maps, core_ids=list(range(8)))
    R = res.results
    y_prompt = np.stack([R[c]["yp"] for c in range(4)]).astype(f)
    y_sample = np.concatenate([R[c]["ys"].reshape(NSB, 4, D) for c in range(8)], axis=0).astype(f)
    ssm_p = np.stack([R[c]["ossm_p"].reshape(NL, 64, 64, 128) for c in range(4)], axis=1).astype(f)
    conv_p = np.stack([R[c]["oconv_p"] for c in range(4)], axis=1).astype(f)
    ssm_s = np.concatenate([R[c]["ossm_s"].reshape(NL, NSB, 64, 64, 128) for c in range(8)], axis=1).astype(f)
    conv_s = np.concatenate([R[c]["oconv_s"].reshape(NL, NSB, 3, CONVD) for c in range(8)], axis=1).astype(f)
    v_s = np.concatenate([R[c]["ov_s"].reshape(NL, NSB, 4, D) for c in range(8)], axis=1).astype(f)
    return (y_prompt, y_sample, ssm_p, conv_p, ssm_s, conv_s, v_s)
```

```python
import numpy as np
from contextlib import ExitStack
import concourse.bass as bass
import concourse.mybir as mybir
from concourse.bass_utils import run_bass_kernel_spmd

F32 = mybir.dt.float32
BF16 = mybir.dt.bfloat16
AF = mybir.ActivationFunctionType
ALU = mybir.AluOpType

D = 2048
DEPTH = 4
DI = 4096
NH = 64
CONVD = 6144
FFN = 5632
INC = 18496
C_U, C_V, C_Z, C_X, C_DT, C_GA, C_GB = 0, 2048, 4096, 8192, 14336, 14400, 16448
EPS = 1e-6
SEM_EPOCH = 30000
NW = 8


class Buf:
    __slots__ = ("name", "w", "r", "alias")

    def __init__(self, name):
        self.name = name
        self.w = None
        self.r = {}
        self.alias = []


def alias(*bufs):
    for a in bufs:
        for b in bufs:
            if a is not b and b not in a.alias:
                a.alias.append(b)


class Prog:
    ENGS = ("pe", "act", "dve", "pool", "sp")

    def __init__(self, nc, stack):
        self.nc = nc
        self.stack = stack
        self.nsem = 0
        self.dma_total = {}
        self.q = {e: [] for e in self.ENGS}
        self.cnt = {e: 0 for e in self.ENGS}
        self.sem = {e: self.new_sem("c_" + e) for e in self.ENGS}
        self.waited = {e: {} for e in self.ENGS}

    def new_sem(self, name):
        self.nsem += 1
        return self.stack.enter_context(self.nc.semaphore("%s_%d" % (name, self.nsem)))

    def new_dma_sem(self, name):
        s = self.new_sem("d_" + name)
        self.dma_total[id(s)] = 0
        return s

    def op(self, e, fn, reads=(), writes=(), dma_sem=None):
        need = {}

        def add(dep):
            if dep is None:
                return
            sem, val, src = dep
            sid = id(sem)
            if sid in self.dma_total:
                val = self.dma_total[sid]
            elif src == e and e == "pe":
                return
            if need.get(sid, (None, 0))[1] < val:
                need[sid] = (sem, val)

        for b in reads:
            add(b.w)
        for b in writes:
            for bb in [b] + b.alias:
                add(bb.w)
                for r in bb.r.values():
                    add(r)
        waits = []
        wd = self.waited[e]
        for sid, (sem, val) in need.items():
            if wd.get(sid, 0) >= val:
                continue
            wd[sid] = val
            waits.append((sem, val))
        if fn is None:
            self.q[e].append((waits, None, None))
            return None
        if dma_sem is not None:
            sid = id(dma_sem)
            self.dma_total[sid] += 16
            comp = (dma_sem, self.dma_total[sid], "dma")
            inc = (dma_sem, 16)
        else:
            if self.cnt[e] >= SEM_EPOCH:
                self.sem[e] = self.new_sem("c_" + e)
                self.cnt[e] = 0
            self.cnt[e] += 1
            comp = (self.sem[e], self.cnt[e], e)
            inc = (self.sem[e], 1)
        for b in reads:
            old = b.r.get(id(comp[0]))
            if old is None or old[1] < comp[1]:
                b.r[id(comp[0])] = comp
        for b in writes:
            b.w = comp
            b.r = {}
        self.q[e].append((waits, fn, inc))
        return comp

    def emit(self):
        nc = self.nc
        with nc.Block() as block:
            def run(eng, lst):
                for waits, fn, inc in lst:
                    for sem, val in waits:
                        eng.wait_ge(sem, val)
                    if fn is None:
                        continue
                    ins = fn(eng)
                    ins.then_inc(inc[0], inc[1])

            @block.tensor
            def _(eng):
                run(eng, self.q["pe"])

            @block.scalar
            def _(eng):
                run(eng, self.q["act"])

            @block.vector
            def _(eng):
                run(eng, self.q["dve"])

            @block.gpsimd
            def _(eng):
                run(eng, self.q["pool"])

            @block.sync
            def _(eng):
                run(eng, self.q["sp"])


def build_program(TP=2048, NL=DEPTH, NSB=16):
    nc = bass.Bass("TRN2", target_bir_lowering=False)
    QS = NSB * 4
    NPC = TP // 128

    def din(name, shape, dt=F32):
        return nc.dram_tensor(name, list(shape), dt, kind="ExternalInput").ap()

    def dout(name, shape):
        return nc.dram_tensor(name, list(shape), F32, kind="ExternalOutput").ap()

    xp = din("xp", [TP, D]); xs = din("xs", [QS, D])
    sssm = din("sssm", [NL, NSB, 32, 128, 128]); sconv = din("sconv", [NL, NSB * 3, CONVD])
    w_in = din("w_in", [NL, D, INC]); w_oa = din("w_oa", [NL, D, D]); w_ob = din("w_ob", [NL, DI, D])
    w_o = din("w_o", [NL, D, D]); w_g = din("w_g", [NL, D, FFN]); w_u = din("w_u", [NL, D, FFN]); w_d = din("w_d", [NL, FFN, D])
    n1g = din("n1g", [NL, 128, 16]); n2g = din("n2g", [NL, 128, 16]); ssmg = din("ssmg", [NL, 128, 32])
    cwc = din("cwc", [NL, 128, 48 * 4]); cbc = din("cbc", [NL, 128, 48])
    lng_d = din("lng", [NL, 128, D]); lnb_d = din("lnb", [NL, 128, D])
    dtb_d = din("dtb", [NL, 128, NH]); alog_d = din("alog", [NL, 128, NH]); dsk_d = din("dsk", [NL, 128, NH])
    fng_d = din("fng", [128, D])
    wsp_d = din("wsp", [NL, 128, 16 * 128]); wss_d = din("wss", [NL, QS, 16 * QS])
    sbp_d = din("sbp", [NL, 1, 16 * 128]); sbs_d = din("sbs", [NL, 1, 16 * QS])
    cst = {}
    for nm, shp in (("ident", [128, 128]), ("trip", [128, 128]), ("m1p", [128, 128]), ("tris", [QS, QS]), ("m1s", [QS, QS]),
                    ("lh0", [128, 128]), ("lh1", [128, 128]), ("lastp", [128, 1]), ("lasts", [QS, NSB]), ("blks", [QS, NSB])):
        cst[nm] = din("c_" + nm, shp)
    def dscr(name, shape):
        return nc.dram_tensor(name, list(shape), BF16).ap()
    wb_in = [dscr("wb_in%d" % i, [D, INC]) for i in range(NL)]; wb_oa = [dscr("wb_oa%d" % i, [D, D]) for i in range(NL)]
    wb_ob = [dscr("wb_ob%d" % i, [DI, D]) for i in range(NL)]; wb_o = [dscr("wb_o%d" % i, [D, D]) for i in range(NL)]
    wb_g = [dscr("wb_g%d" % i, [D, FFN]) for i in range(NL)]; wb_u = [dscr("wb_u%d" % i, [D, FFN]) for i in range(NL)]
    wb_d = [dscr("wb_d%d" % i, [FFN, D]) for i in range(NL)]
    yp = dout("yp", [TP, D]); ys = dout("ys", [QS, D])
    ossm_p = dout("ossm_p", [NL, 32, 128, 128]); oconv_p = dout("oconv_p", [NL, 3, CONVD])
    ossm_s = dout("ossm_s", [NL, NSB, 32, 128, 128]); oconv_s = dout("oconv_s", [NL, NSB * 3, CONVD])
    ov_s = dout("ov_s", [NL, QS, D])

    with ExitStack() as st:
        P = Prog(nc, st)

        def sb(name, shape, dt):
            return st.enter_context(nc.sbuf_tensor(name, list(shape), dt))

        xt = sb("xt", [128, D], F32); b_xt = Buf("xt")
        hn = sb("hn", [128, D], BF16); b_hn = Buf("hn")
        hT = sb("hT", [128, 16, 128], BF16); b_hT = Buf("hT")
        wring = [sb("wr%d" % i, [128, 1024], BF16) for i in range(NW)]
        b_wr = [Buf("wr%d" % i) for i in range(NW)]
        s_wr = [P.new_dma_sem("wr%d" % i) for i in range(NW)]
        s_wbk = [P.new_dma_sem("wbk%d" % i) for i in range(NW)]
        b_scr = Buf("scr")
        uT = sb("uT", [128, 16, 128], BF16); b_uT = Buf("uT")
        A1 = sb("A1", [128, 4096], F32)
        b_v = Buf("v"); b_vn = Buf("vn"); b_y = Buf("y"); b_sg = Buf("sgate")
        alias(b_v, b_vn, b_y, b_sg)
        v_ap = A1[:, 0:2048]; vn_ap = A1[:, 2048:4096]; y_ap = A1
        sg_ap = A1[:, 0:FFN // 2].bitcast(BF16)
        A2 = sb("A2", [128, 4096], F32)
        b_xdt = Buf("xdt"); b_xdtd = Buf("xdtd"); b_hid = Buf("hid"); b_ctok = Buf("ctok")
        alias(b_xdt, b_xdtd, b_hid, b_ctok)
        xdt_ap = A2[:, 0:2048].bitcast(BF16); xdtd_ap = A2[:, 2048:4096].bitcast(BF16)
        hid_ap = A2[:, 0:FFN // 2].bitcast(BF16)
        ctok_ap = A2[:, 0:1024]
        A3 = sb("A3", [128, 48 * 128], BF16)
        b_cx = Buf("cxT"); b_hidT = Buf("hidT"); alias(b_cx, b_hidT)
        cxT = A3[:].rearrange("p (j t) -> p j t", t=128)
        hidT = A3[:, 0:44 * 128].rearrange("p (j t) -> p j t", t=128)
        A4 = sb("A4", [128, 48 * 131], F32)
        b_xbc = Buf("xbcT"); b_mg = Buf("merged"); b_ybb = Buf("ybb"); b_ybT = Buf("ybT")
        alias(b_xbc, b_mg, b_ybb, b_ybT)
        mg_ap = A4[:, 0:2048]
        ybb_ap = A4[:, 2048:4096].bitcast(BF16)
        ybT = A4[:, 4096:6144].bitcast(BF16).rearrange("p (j t) -> p j t", t=128)
        A5 = sb("A5", [128, 4096], F32)
        b_lng = Buf("lng"); b_lnb = Buf("lnb"); b_xst = Buf("xs_tok"); b_stT = Buf("stT"); b_fng = Buf("fng")
        alias(b_lng, b_lnb, b_xst, b_stT, b_fng)
        lng_ap = A5[:, 0:2048]; lnb_ap = A5[:, 2048:4096]; fng_ap = A5[:, 0:2048]
        xst_ap = A5[:, 0:2048].bitcast(BF16); stT_ap = A5[:, 2048:4096].bitcast(BF16)
        s_ln = P.new_dma_sem("ln")
        sz = sb("sz", [128, DI], BF16); b_sz = Buf("sz")
        Btok = sb("Btok", [128, 1024], BF16); b_Btok = Buf("Btok")
        sga = sb("sga", [128, D], BF16); b_sga = Buf("sga")
        sgb = sb("sgb", [128, D], BF16); b_sgb = Buf("sgb")
        St = sb("St", [128, 32, 128], F32); b_St = Buf("St"); s_St = P.new_dma_sem("St")
        Rf = sb("Rf", [128, 1024], F32); b_R = Buf("R")
        Ef = sb("Ef", [128, 1024], BF16); b_E = Buf("E")
        cbm = sb("cbm", [128, 128], BF16); b_cbm = Buf("cbm")
        cbLf = sb("cbLf", [128, 1024], BF16); b_cbL = Buf("cbL")
        tmpg = sb("tmpg", [128, 512], F32); b_tmpg = Buf("tmpg")
        tmp2 = sb("tmp2", [128, 512], F32); b_tmp2 = Buf("tmp2")
        gel = [sb("gel%d" % i, [128, 512], F32) for i in range(2)]; b_gel = [Buf("gel0"), Buf("gel1")]
        Bm = sb("Bm", [128, 1024], BF16); b_Bm = Buf("Bm")
        dcol = sb("dcol", [128, 32 * 16], F32); b_dcol = Buf("dcol")
        rh0 = sb("rh0", [128, 32 * 16], F32); b_rh0 = Buf("rh0")
        rh1 = sb("rh1", [128, 32 * 16], F32); b_rh1 = Buf("rh1")
        small = sb("small", [128, 8 * 64], F32); b_small = Buf("small")
        dtt = small[:, 0:64]; dat = small[:, 64:128]; cst_ = small[:, 128:192]; dte = small[:, 192:256]; ecs = small[:, 256:320]
        stat = sb("stat", [128, 32], F32); b_stat = Buf("stat")
        wsl = sb("wsl", [128, 16 * 128], BF16); b_wsl = Buf("wsl"); s_wsl = P.new_dma_sem("wsl")
        wsm = sb("wsm", [128, 16, 128], BF16); b_wsm = Buf("wsm")
        sbrow = sb("sbrow", [1, 16 * 128], BF16); b_sbrow = Buf("sbrow"); s_sbrow = P.new_dma_sem("sbrow")
        cw_t = sb("cw_t", [128, 48, 4], F32); cb_t = sb("cb_t", [128, 48], F32)
        n1g_t = sb("n1g_t", [128, 16], F32); n2g_t = sb("n2g_t", [128, 16], F32); ssmg_t = sb("ssmg_t", [128, 32], F32)
        dtb_t = sb("dtb_t", [128, NH], F32); a_t = sb("a_t", [128, NH], F32); dsk_t = sb("dsk_t", [128, NH], F32)
        b_par = Buf("par"); s_par = P.new_dma_sem("par")
        halo = sb("halo", [128, 48, 3], F32); b_halo = Buf("halo")
        acc = [sb("acc%d" % i, [128, 128], F32) for i in range(4)]; b_acc = [Buf("acc%d" % i) for i in range(4)]
        hs_t = sb("hs_t", [NSB * 3 if NSB * 3 <= 128 else 128, CONVD // 4], F32)
        b_hs = Buf("hs"); s_hs = P.new_dma_sem("hs")
        cT = {}
        for nm in cst:
            shp = list(cst[nm].shape)
            cT[nm] = sb("k_" + nm, shp, F32)
        identb = sb("identb", [128, 128], BF16); trib_p = sb("trib_p", [128, 128], BF16); trib_s = sb("trib_s", [QS, QS], BF16)
        onesb = sb("onesb", [1, 128], BF16)
        b_cst = Buf("cst"); s_cst = P.new_dma_sem("cst")
        s_x = P.new_dma_sem("x"); s_out = P.new_dma_sem("out")

        pbank = [st.enter_context(nc.psum_tensor("pb%d" % i, [128, 512], F32)) for i in range(8)]
        b_pb = [Buf("pb%d" % i) for i in range(8)]

        def dma(eng, out, in_, sem, reads=(), writes=()):
            P.op(eng, lambda e: e.dma_start(out=out, in_=in_), reads=reads, writes=writes, dma_sem=sem)

        def mm(out, lhsT, rhs, start, stop, reads, writes):
            P.op("pe", lambda e: e.matmul(out, lhsT=lhsT, rhs=rhs, start=start, stop=stop), reads=reads, writes=writes)

        def tr(out, in_, ident, reads, writes):
            P.op("pe", lambda e: e.transpose(out=out, in_=in_, identity=ident), reads=list(reads) + [b_cst], writes=writes)

        def act(out, in_, func, reads, writes, bias=None, scale=None, accum=None):
            kw = {}
            if bias is not None:
                kw["bias"] = bias
            if scale is not None:
                kw["scale"] = scale
            if accum is not None:
                kw["accum_out"] = accum
            P.op("act", lambda e: e.activation(out=out, in_=in_, func=func, **kw), reads=reads, writes=writes)

        def tt(out, in0, in1, op, reads, writes, eng="dve"):
            P.op(eng, lambda e: e.tensor_tensor(out=out, in0=in0, in1=in1, op=op), reads=reads, writes=writes)

        def ts(out, in0, s1, s2, op0, op1, reads, writes, eng="dve"):
            if op1 is None:
                P.op(eng, lambda e: e.tensor_scalar(out=out, in0=in0, scalar1=s1, scalar2=None, op0=op0), reads=reads, writes=writes)
            else:
                P.op(eng, lambda e: e.tensor_scalar(out=out, in0=in0, scalar1=s1, scalar2=s2, op0=op0, op1=op1), reads=reads, writes=writes)

        def stt(out, in0, scalar, in1, op0, op1, reads, writes, eng="dve"):
            P.op(eng, lambda e: e.scalar_tensor_tensor(out=out, in0=in0, scalar=scalar, in1=in1, op0=op0, op1=op1), reads=reads, writes=writes)

        def cp(out, in_, reads, writes, eng="dve"):
            P.op(eng, lambda e: e.tensor_copy(out=out, in_=in_), reads=reads, writes=writes)

        ring = {"slot": 0, "grp": 0, "trb": 0}

        def linear(lhs_fn, nk, W, col0, ncols, mode, consumer, q, act_reads, Wb=None, first=True):
            for g0 in range(col0, col0 + ncols, 1024):
                gw = min(1024, col0 + ncols - g0)
                gi = ring["grp"]; ring["grp"] ^= 1
                banks = (2 * gi, 2 * gi + 1)
                for k in range(nk):
                    s = ring["slot"]; ring["slot"] = (s + 1) % NW
                    if first:
                        dma("pool", wring[s][:, 0:gw], W[k * 128:(k + 1) * 128, g0:g0 + gw], s_wr[s], writes=[b_wr[s]])
                    else:
                        dma("sp", wring[s][:, 0:gw], Wb[k * 128:(k + 1) * 128, g0:g0 + gw], s_wr[s], writes=[b_wr[s]])
                    if mode == "tok":
                        for n in range((gw + 511) // 512):
                            w = min(512, gw - n * 512)
                            mm(pbank[banks[n]][0:q, 0:w], lhs_fn(k), wring[s][:, n * 512:n * 512 + w], k == 0, k == nk - 1,
                               reads=[b_wr[s]] + act_reads, writes=[b_pb[banks[n]]])
                    else:
                        for jj in range(gw // 128):
                            n, r = divmod(jj, 4)
                            mm(pbank[banks[n]][:, r * 128:r * 128 + q], wring[s][:, jj * 128:(jj + 1) * 128], lhs_fn(k), (k == 0 and r == 0), k == nk - 1,
                               reads=[b_wr[s]] + act_reads, writes=[b_pb[banks[n]]])
                    if first and Wb is not None:
                        dma("sp", Wb[k * 128:(k + 1) * 128, g0:g0 + gw], wring[s][:, 0:gw], s_wbk[s], reads=[b_wr[s], b_scr])
                for n in range((gw + 511) // 512):
                    w = min(512, gw - n * 512)
                    consumer(banks[n], g0 + n * 512, w)

        def transpose_to(dst3, dst_buf, src, src_buf, ntile, q, scale_col=None, dt=BF16):
            for t0 in range(0, ntile, 4):
                nt = min(4, ntile - t0)
                bi = 4 + ring["trb"]; ring["trb"] ^= 1
                pv = pbank[bi][:].bitcast(BF16) if dt == BF16 else pbank[bi][:]
                for i in range(nt):
                    tr(pv[:, i * 128:i * 128 + q], src[0:q, (t0 + i) * 128:(t0 + i + 1) * 128],
                       (identb if dt == BF16 else cT["ident"])[0:q, 0:q], reads=[src_buf], writes=[b_pb[bi]])
                if scale_col is None:
                    if q == 128:
                        cp(dst3[:, t0:t0 + nt, :], pv[:, 0:nt * 128].rearrange("p (a b) -> p a b", b=128), reads=[b_pb[bi]], writes=[dst_buf])
                    else:
                        cp(dst3[:, t0:t0 + nt, 0:q], pv[:, 0:nt * 128].rearrange("p (a b) -> p a b", b=128)[:, :, 0:q], reads=[b_pb[bi]], writes=[dst_buf])
                else:
                    for i in range(nt):
                        act(dst3[:, t0 + i, 0:q], pv[:, i * 128:i * 128 + q], AF.Identity, reads=[b_pb[bi], b_par], writes=[dst_buf],
                            scale=scale_col[:, t0 + i:t0 + i + 1])

        def rmsnorm_to_T(q, gcol):
            act(hn[0:q, :], xt[0:q, :], AF.Square, reads=[b_xt], writes=[b_hn, b_stat], accum=stat[0:q, 0:1])
            P.op("dve", None, reads=[b_hn])
            ts(stat[0:q, 1:2], stat[0:q, 0:1], 1.0 / D, EPS, ALU.mult, ALU.add, reads=[b_hn, b_stat], writes=[b_stat])
            act(stat[0:q, 2:3], stat[0:q, 1:2], AF.Sqrt, reads=[b_stat], writes=[b_stat])
            P.op("dve", lambda e: e.reciprocal(out=stat[0:q, 3:4], in_=stat[0:q, 2:3]), reads=[b_stat], writes=[b_stat])
            act(hn[0:q, :], xt[0:q, :], AF.Identity, reads=[b_xt, b_stat], writes=[b_hn], scale=stat[0:q, 3:4])
            transpose_to(hT, b_hT, hn, b_hn, 16, q, scale_col=gcol)

        def gelu_bank(bi, out_ap, out_buf, shape_fn, extra_writes=()):
            gi = bi % 2
            src = shape_fn(pbank[bi])
            t = shape_fn(gel[gi])
            act(t, src, AF.Square, reads=[b_pb[bi]], writes=[b_gel[gi]])
            ts(t, t, 0.044715, 1.0, ALU.mult, ALU.add, reads=[b_gel[gi]], writes=[b_gel[gi]])
            tt(t, t, src, ALU.mult, reads=[b_gel[gi], b_pb[bi]], writes=[b_gel[gi]])
            act(t, t, AF.Sigmoid, reads=[b_gel[gi]], writes=[b_gel[gi]], scale=1.5957691216057308)
            tt(out_ap, t, src, ALU.mult, reads=[b_gel[gi], b_pb[bi]], writes=[out_buf] + list(extra_writes))

        for nm in cst:
            dma("sp", cT[nm][:], cst[nm], s_cst, writes=[b_cst])
        cp(identb[:], cT["ident"][:], reads=[b_cst], writes=[b_cst])
        cp(trib_p[:], cT["trip"][:], reads=[b_cst], writes=[b_cst])
        cp(trib_s[:], cT["tris"][:], reads=[b_cst], writes=[b_cst])
        P.op("dve", lambda e: e.memset(onesb[:], 1.0), writes=[b_cst])

        def chunk(l, kind, ci):
            prompt = kind == "p"
            q = 128 if prompt else QS
            nb = 1 if prompt else NSB
            qb = q // nb
            last_layer = l == NL - 1
            first = prompt and ci == 0
            if (prompt and ci == 1) or (not prompt and NPC == 1):
                P.op("sp", None, writes=[b_scr])
            TRI = cT["trip"] if prompt else cT["tris"]
            M1 = cT["m1p"] if prompt else cT["m1s"]
            TRIB = trib_p if prompt else trib_s
            if l == 0:
                src = xp[ci * 128:(ci + 1) * 128, :] if prompt else xs[:, :]
            else:
                src = yp[ci * 128:(ci + 1) * 128, :] if prompt else ys[:, :]
            xdst = yp[ci * 128:(ci + 1) * 128, :] if prompt else ys[:, :]
            dma("sp", xt[0:q, :], src, s_x, writes=[b_xt])
            dma("sp", lng_ap, lng_d[l], s_ln, writes=[b_lng])
            dma("sp", lnb_ap, lnb_d[l], s_ln, writes=[b_lnb])
            rmsnorm_to_T(q, n1g_t)
            hfn = lambda k: hT[:, k, 0:q]
            W = w_in[l]

            def cons_u(bi, c0, w):
                j0 = (c0 - C_U) // 128
                gelu_bank(bi, uT[:, j0:j0 + 4, 0:q], b_uT,
                          lambda t: t[:, :].rearrange("p (a b) -> p a b", b=128)[:, :, 0:q])
            linear(hfn, 16, W, C_U, 2048, "feat", cons_u, q, [b_hT], Wb=wb_in[l], first=first)

            def cons_v(bi, c0, w):
                gelu_bank(bi, v_ap[0:q, c0 - C_V:c0 - C_V + w], b_v, lambda t: t[0:q, 0:w])
            linear(hfn, 16, W, C_V, 2048, "tok", cons_v, q, [b_hT], Wb=wb_in[l], first=first)
            act(vn_ap[0:q, :], v_ap[0:q, :], AF.Copy, reads=[b_v], writes=[b_vn, b_stat], accum=stat[0:q, 4:5])
            P.op("dve", None, reads=[b_vn])
            ts(stat[0:q, 5:6], stat[0:q, 4:5], -1.0 / D, None, ALU.mult, None, reads=[b_vn, b_stat], writes=[b_stat])
            act(vn_ap[0:q, :], v_ap[0:q, :], AF.Square, reads=[b_v, b_stat], writes=[b_vn, b_stat], bias=stat[0:q, 5:6], accum=stat[0:q, 6:7])
            P.op("dve", None, reads=[b_vn])
            ts(stat[0:q, 7:8], stat[0:q, 6:7], 1.0 / D, EPS, ALU.mult, ALU.add, reads=[b_vn, b_stat], writes=[b_stat])
            act(stat[0:q, 8:9], stat[0:q, 7:8], AF.Sqrt, reads=[b_stat], writes=[b_stat])
            P.op("dve", lambda e: e.reciprocal(out=stat[0:q, 9:10], in_=stat[0:q, 8:9]), reads=[b_stat], writes=[b_stat])
            tt(stat[0:q, 10:11], stat[0:q, 5:6], stat[0:q, 9:10], ALU.mult, reads=[b_stat], writes=[b_stat])
            act(vn_ap[0:q, :], v_ap[0:q, :], AF.Identity, reads=[b_v, b_stat], writes=[b_vn], bias=stat[0:q, 10:11], scale=stat[0:q, 9:10])
            tt(vn_ap[0:q, :], vn_ap[0:q, :], lng_ap[0:q, :], ALU.mult, reads=[b_vn, b_lng], writes=[b_vn])
            tt(vn_ap[0:q, :], vn_ap[0:q, :], lnb_ap[0:q, :], ALU.add, reads=[b_vn, b_lnb], writes=[b_vn])
            cp(hn[0:q, :], vn_ap[0:q, :], reads=[b_vn], writes=[b_hn])
            if not prompt:
                dma("sp", ov_s[l], vn_ap[0:q, :], s_out, reads=[b_vn])
            for g in range(16):
                bi = 4 + (g % 2)
                mm(pbank[bi][:, 0:q], hn[0:q, g * 128:(g + 1) * 128], wsm[0:q, g, 0:q], True, False, reads=[b_hn, b_wsm], writes=[b_pb[bi]])
                mm(pbank[bi][:, 0:q], onesb[0:1, :], sbrow[0:1, g * q:(g + 1) * q], False, True, reads=[b_sbrow, b_cst], writes=[b_pb[bi]])
                tt(uT[:, g, 0:q], uT[:, g, 0:q], pbank[bi][:, 0:q], ALU.mult, reads=[b_uT, b_pb[bi]], writes=[b_uT])

            def cons_z(bi, c0, w):
                act(sz[0:q, c0 - C_Z:c0 - C_Z + w], pbank[bi][0:q, 0:w], AF.Silu, reads=[b_pb[bi]], writes=[b_sz])
            linear(hfn, 16, W, C_Z, DI, "tok", cons_z, q, [b_hT], Wb=wb_in[l], first=first)

            if prompt:
                xb3 = A4[:, :].rearrange("p (j t) -> p j t", t=131)
                newv = lambda j0, nj: xb3[:, j0:j0 + nj, 3:131]
                if ci == 0:
                    P.op("dve", lambda e: e.memset(halo[:], 0.0), writes=[b_halo])
                cp(xb3[:, :, 0:3], halo[:], reads=[b_halo], writes=[b_xbc])
            else:
                xb4 = A4[:, 0:48 * NSB * 7].rearrange("p (j b t) -> p j b t", b=NSB, t=7)
                newv = lambda j0, nj: xb4[:, j0:j0 + nj, :, 3:7]
                for qq in range(4):
                    dma("sp", hs_t[0:NSB * 3, :], sconv[l][:, qq * 1536:(qq + 1) * 1536], s_hs, writes=[b_hs])
                    for jj in range(12):
                        j = qq * 12 + jj
                        bi = 4 + (j % 2)
                        tr(pbank[bi][:, 0:NSB * 3], hs_t[0:NSB * 3, jj * 128:(jj + 1) * 128], cT["ident"][0:NSB * 3, 0:NSB * 3], reads=[b_hs], writes=[b_pb[bi]])
                        cp(xb4[:, j, :, 0:3], pbank[bi][:, 0:NSB * 3].rearrange("p (b r) -> p b r", r=3), reads=[b_pb[bi]], writes=[b_xbc])

            def cons_x(bi, c0, w):
                j0 = (c0 - C_X) // 128
                if prompt:
                    cp(newv(j0, 4), pbank[bi][:, :].rearrange("p (a b) -> p a b", b=128), reads=[b_pb[bi]], writes=[b_xbc])
                else:
                    cp(newv(j0, 4), pbank[bi][:, :].rearrange("p (a b) -> p a b", b=128)[:, :, 0:q].rearrange("p a (b t) -> p a b t", t=4),
                       reads=[b_pb[bi]], writes=[b_xbc])
            linear(hfn, 16, W, C_X, CONVD, "feat", cons_x, q, [b_hT], Wb=wb_in[l], first=first)

            def cons_dt(bi, c0, w):
                tt(dtt[0:q, :], pbank[bi][0:q, 0:64], dtb_t[0:q, :], ALU.add, reads=[b_pb[bi], b_par], writes=[b_small])
                act(dtt[0:q, :], dtt[0:q, :], AF.Exp, reads=[b_small], writes=[b_small])
                ts(dtt[0:q, :], dtt[0:q, :], 1.0, None, ALU.add, None, reads=[b_small], writes=[b_small])
                act(dtt[0:q, :], dtt[0:q, :], AF.Ln, reads=[b_small], writes=[b_small])
                tt(dat[0:q, :], dtt[0:q, :], a_t[0:q, :], ALU.mult, reads=[b_small, b_par], writes=[b_small])
            linear(hfn, 16, W, C_DT, 64, "tok", cons_dt, q, [b_hT], Wb=wb_in[l], first=first)

            def cons_ga(bi, c0, w):
                act(sga[0:q, c0 - C_GA:c0 - C_GA + w], pbank[bi][0:q, 0:w], AF.Sigmoid, reads=[b_pb[bi]], writes=[b_sga])

            def cons_gb(bi, c0, w):
                act(sgb[0:q, c0 - C_GB:c0 - C_GB + w], pbank[bi][0:q, 0:w], AF.Sigmoid, reads=[b_pb[bi]], writes=[b_sgb])
            linear(hfn, 16, W, C_GA, 2048, "tok", cons_ga, q, [b_hT], Wb=wb_in[l], first=first)
            linear(hfn, 16, W, C_GB, 2048, "tok", cons_gb, q, [b_hT], Wb=wb_in[l], first=first)

            if (prompt and ci == NPC - 1) or not prompt:
                dstv = None if prompt else oconv_s[l].rearrange("(b r) c -> r b c", r=3)
                for bt in range(6):
                    for r in range(1 if prompt else 3):
                        for half in range(2):
                            bi = 4 + half
                            for i in range(4):
                                j = bt * 8 + half * 4 + i
                                if prompt:
                                    tr(pbank[bi][0:q, i * 128:(i + 1) * 128], xb3[:, j, 3:131], cT["ident"][:, :], reads=[b_xbc], writes=[b_pb[bi]])
                                else:
                                    tr(pbank[bi][0:NSB, i * 128:(i + 1) * 128], xb4[:, j, :, 4 + r], cT["ident"][:, :], reads=[b_xbc], writes=[b_pb[bi]])
                            nr = q if prompt else NSB
                            cp(ctok_ap[0:nr, half * 512:(half + 1) * 512], pbank[bi][0:nr, :], reads=[b_pb[bi]], writes=[b_ctok])
                        if prompt:
                            dma("sp", oconv_p[l][:, bt * 1024:(bt + 1) * 1024], ctok_ap[125:128, :], s_out, reads=[b_ctok])
                        else:
                            dma("sp", dstv[r][:, bt * 1024:(bt + 1) * 1024], ctok_ap[0:NSB, :], s_out, reads=[b_ctok])
            if prompt:
                cp(halo[:], xb3[:, :, 128:131], reads=[b_xbc], writes=[b_halo])

            for j in range(48):
                a = acc[j % 4]; ba = b_acc[j % 4]
                if prompt:
                    av = a[:, 0:q]
                    xk = lambda k: xb3[:, j, k:k + q]
                else:
                    av = a[:, 0:q].rearrange("p (b t) -> p b t", t=4)
                    xk = lambda k: xb4[:, j, :, k:k + 4]
                ts(av, xk(0), cw_t[:, j, 0:1], cb_t[:, j:j + 1], ALU.mult, ALU.add, reads=[b_xbc, b_par], writes=[ba])
                for k in range(1, 4):
                    stt(av, xk(k), cw_t[:, j, k:k + 1], av, ALU.mult, ALU.add, reads=[b_xbc, b_par, ba], writes=[ba])
                act(cxT[:, j, 0:q], a[:, 0:q], AF.Silu, reads=[ba], writes=[b_cx])
            transpose_to_tok(q)

            ssd(l, kind, ci, q, nb, qb, TRI, M1, TRIB)

            tt(y_ap[0:q, :], y_ap[0:q, :], sz[0:q, :], ALU.mult, reads=[b_y, b_sz], writes=[b_y])
            for g in range(8):
                act(tmpg[0:q, :], y_ap[0:q, g * 512:(g + 1) * 512], AF.Square, reads=[b_y], writes=[b_tmpg, b_stat], accum=stat[0:q, 12 + g:13 + g])
            P.op("dve", None, reads=[b_tmpg])
            ts(stat[0:q, 20:28], stat[0:q, 12:20], 1.0 / 512, EPS, ALU.mult, ALU.add, reads=[b_tmpg, b_stat], writes=[b_stat])
            act(stat[0:q, 20:28], stat[0:q, 20:28], AF.Sqrt, reads=[b_stat], writes=[b_stat])
            P.op("dve", lambda e: e.reciprocal(out=stat[0:q, 20:28], in_=stat[0:q, 20:28]), reads=[b_stat], writes=[b_stat])
            tt(ybb_ap[0:q, :].rearrange("p (g c) -> p g c", c=512), y_ap[0:q, :].rearrange("p (g c) -> p g c", c=512),
               stat[0:q, 20:28].unsqueeze(2).to_broadcast([q, 8, 512]), ALU.mult, reads=[b_y, b_stat], writes=[b_ybb])
            transpose_to(ybT, b_ybT, ybb_ap, b_ybb, 32, q, scale_col=ssmg_t)

            def cons_pa(bi, c0, w):
                tt(mg_ap[0:q, c0:c0 + w], pbank[bi][0:q, 0:w], sga[0:q, c0:c0 + w], ALU.mult, reads=[b_pb[bi], b_sga], writes=[b_mg])
            linear(lambda k: uT[:, k, 0:q], 16, w_oa[l], 0, D, "tok", cons_pa, q, [b_uT], Wb=wb_oa[l], first=first)

            def cons_pb(bi, c0, w):
                gi = bi % 2
                tt(gel[gi][0:q, 0:w], pbank[bi][0:q, 0:w], sgb[0:q, c0:c0 + w], ALU.mult, reads=[b_pb[bi], b_sgb], writes=[b_gel[gi]])
                tt(mg_ap[0:q, c0:c0 + w], mg_ap[0:q, c0:c0 + w], gel[gi][0:q, 0:w], ALU.add, reads=[b_mg, b_gel[gi]], writes=[b_mg])
            linear(lambda k: ybT[:, k, 0:q], 32, w_ob[l], 0, D, "tok", cons_pb, q, [b_ybT], Wb=wb_ob[l], first=first)
            cp(hn[0:q, :], mg_ap[0:q, :], reads=[b_mg], writes=[b_hn])
            transpose_to(hT, b_hT, hn, b_hn, 16, q)

            def cons_res(bi, c0, w):
                tt(xt[0:q, c0:c0 + w], xt[0:q, c0:c0 + w], pbank[bi][0:q, 0:w], ALU.add, reads=[b_pb[bi], b_xt], writes=[b_xt])
            linear(hfn, 16, w_o[l], 0, D, "tok", cons_res, q, [b_hT], Wb=wb_o[l], first=first)

            rmsnorm_to_T(q, n2g_t)

            def cons_g(bi, c0, w):
                act(sg_ap[0:q, c0:c0 + w], pbank[bi][0:q, 0:w], AF.Silu, reads=[b_pb[bi]], writes=[b_sg])

            def cons_up(bi, c0, w):
                tt(hid_ap[0:q, c0:c0 + w], pbank[bi][0:q, 0:w], sg_ap[0:q, c0:c0 + w], ALU.mult, reads=[b_pb[bi], b_sg], writes=[b_hid])
            linear(hfn, 16, w_g[l], 0, FFN, "tok", cons_g, q, [b_hT], Wb=wb_g[l], first=first)
            linear(hfn, 16, w_u[l], 0, FFN, "tok", cons_up, q, [b_hT], Wb=wb_u[l], first=first)
            transpose_to(hidT, b_hidT, hid_ap, b_hid, 44, q)
            linear(lambda k: hidT[:, k, 0:q], 44, w_d[l], 0, D, "tok", cons_res, q, [b_hidT], Wb=wb_d[l], first=first)

            if last_layer:
                dma("sp", fng_ap, fng_d, s_ln, writes=[b_fng])
                act(hn[0:q, :], xt[0:q, :], AF.Square, reads=[b_xt], writes=[b_hn, b_stat], accum=stat[0:q, 0:1])
                P.op("dve", None, reads=[b_hn])
                ts(stat[0:q, 1:2], stat[0:q, 0:1], 1.0 / D, EPS, ALU.mult, ALU.add, reads=[b_hn, b_stat], writes=[b_stat])
                act(stat[0:q, 2:3], stat[0:q, 1:2], AF.Sqrt, reads=[b_stat], writes=[b_stat])
                P.op("dve", lambda e: e.reciprocal(out=stat[0:q, 3:4], in_=stat[0:q, 2:3]), reads=[b_stat], writes=[b_stat])
                stt(xt[0:q, :], xt[0:q, :], stat[0:q, 3:4], fng_ap[0:q, :], ALU.mult, ALU.mult, reads=[b_xt, b_stat, b_fng], writes=[b_xt])
            dma("sp", xdst, xt[0:q, :], s_x, reads=[b_xt])

        def transpose_to_tok(q):
            for t0 in range(0, 40, 4):
                bi = 4 + ring["trb"]; ring["trb"] ^= 1
                pv = pbank[bi][:].bitcast(BF16)
                for i in range(4):
                    tr(pv[0:q, i * 128:(i + 1) * 128], cxT[:, t0 + i, 0:q], identb[:, :], reads=[b_cx], writes=[b_pb[bi]])
                if t0 < 32:
                    cp(xst_ap[0:q, t0 * 128:(t0 + 4) * 128], pv[0:q, 0:512], reads=[b_pb[bi]], writes=[b_xst])
                else:
                    cp(Btok[0:q, (t0 - 32) * 128:(t0 - 28) * 128], pv[0:q, 0:512], reads=[b_pb[bi]], writes=[b_Btok])

        def ssd(l, kind, ci, q, nb, qb, TRI, M1, TRIB):
            prompt = kind == "p"
            BT = lambda g: cxT[:, 32 + g, 0:q]
            CT = lambda g: cxT[:, 40 + g, 0:q]
            mm(pbank[5][0:q, 0:64], TRI[0:q, 0:q], dat[0:q, :], True, True, reads=[b_small, b_cst], writes=[b_pb[5]])
            mm(pbank[5][0:q, 64:128], M1[0:q, 0:q], dat[0:q, :], True, True, reads=[b_small, b_cst], writes=[b_pb[5]])
            cp(cst_[0:q, :], pbank[5][0:q, 0:64], reads=[b_pb[5]], writes=[b_small])
            act(ecs[0:q, :], pbank[5][0:q, 0:64], AF.Exp, reads=[b_pb[5]], writes=[b_small])
            act(dte[0:q, :], pbank[5][0:q, 64:128], AF.Exp, reads=[b_pb[5]], writes=[b_small])
            xs3 = xst_ap[0:q, :].rearrange("p (h c) -> p h c", c=64)
            tt(xdt_ap[0:q, :].rearrange("p (h c) -> p h c", c=64), xs3, dtt[0:q, :].unsqueeze(2).to_broadcast([q, 64, 64]), ALU.mult,
               reads=[b_xst, b_small], writes=[b_xdt])
            tt(xdtd_ap[0:q, :].rearrange("p (h c) -> p h c", c=64), xdt_ap[0:q, :].rearrange("p (h c) -> p h c", c=64),
               dte[0:q, :].unsqueeze(2).to_broadcast([q, 64, 64]), ALU.mult, reads=[b_xdt, b_small], writes=[b_xdtd])
            LAST = cT["lastp"] if prompt else cT["lasts"]
            cs3 = cst_[0:q, :].rearrange("p (j two) -> p j two", two=2)
            for h2, rh, brh in ((0, rh0, b_rh0), (1, rh1, b_rh1)):
                tt(rh[0:q, 0:32 * nb].rearrange("p (j b) -> p j b", b=nb), cs3[:, :, h2:h2 + 1].to_broadcast([q, 32, nb]),
                   LAST[0:q, 0:nb].unsqueeze(1).to_broadcast([q, 32, nb]), ALU.mult, reads=[b_small, b_cst], writes=[brh])
            mm(pbank[5][:, 128:128 + 32 * nb] if nb == 1 else pbank[6][:, 0:32 * nb], cT["lh0"][0:q, :], rh0[0:q, 0:32 * nb], True, False,
               reads=[b_rh0, b_cst], writes=[b_pb[5] if nb == 1 else b_pb[6]])
            mm(pbank[5][:, 128:128 + 32 * nb] if nb == 1 else pbank[6][:, 0:32 * nb], cT["lh1"][0:q, :], rh1[0:q, 0:32 * nb], False, True,
               reads=[b_rh1, b_cst], writes=[b_pb[5] if nb == 1 else b_pb[6]])
            act(dcol[:, 0:32 * nb], pbank[5][:, 128:128 + 32 * nb] if nb == 1 else pbank[6][:, 0:32 * nb], AF.Exp,
                reads=[b_pb[5] if nb == 1 else b_pb[6]], writes=[b_dcol])
            dc3 = dcol[:, 0:32 * nb].rearrange("p (j b) -> p j b", b=nb)

            if prompt and ci == 0:
                P.op("dve", lambda e: e.memset(St[:], 0.0), writes=[b_St])
            yoT_banks = (0, 1, 2, 3)
            for b in range(nb):
                if not prompt:
                    dma("sp", St[:], sssm[l, b].rearrange("j m n -> m j n"), s_St, writes=[b_St])
                for t0 in range(0, 32, 4):
                    bi = 4 + ring["trb"]; ring["trb"] ^= 1
                    for i in range(4):
                        tr(pbank[bi][:, i * 128:(i + 1) * 128], St[:, t0 + i, :], cT["ident"][:, :], reads=[b_St], writes=[b_pb[bi]])
                    if (t0 // 4) % 2 == 0:
                        cp(stT_ap[:, t0 * 128:(t0 + 4) * 128], pbank[bi][:, :], reads=[b_pb[bi]], writes=[b_stT])
                    else:
                        act(stT_ap[:, t0 * 128:(t0 + 4) * 128], pbank[bi][:, :], AF.Copy, reads=[b_pb[bi]], writes=[b_stT])
                if not prompt:
                    for j in range(32):
                        bk = yoT_banks[j // 8]
                        mm(pbank[bk][:, (j % 8) * 64 + b * 4:(j % 8) * 64 + b * 4 + 4], stT_ap[:, j * 128:(j + 1) * 128], CT(j // 4)[:, b * 4:b * 4 + 4],
                           True, True, reads=[b_stT, b_cx], writes=[b_pb[bk]])
                    ts(Bm[0:q, :], Btok[0:q, :], cT["blks"][0:q, b:b + 1], None, ALU.mult, None, reads=[b_Btok, b_cst], writes=[b_Bm])
                    Bsrc, bB = Bm, b_Bm
                else:
                    Bsrc, bB = Btok, b_Btok
                if prompt:
                    y_off_prompt = None
                for j in range(32):
                    bi = 6 + (j % 2) if prompt else 6 + (j % 2)
                    g = j // 4
                    mm(pbank[bi][:, 0:128], xdtd_ap[0:q, j * 128:(j + 1) * 128], Bsrc[0:q, g * 128:(g + 1) * 128], True, True,
                       reads=[b_xdtd, bB], writes=[b_pb[bi]])
                    stt(St[:, j, :], St[:, j, :], dc3[:, j, b:b + 1], pbank[bi][:, 0:128], ALU.mult, ALU.add,
                        reads=[b_St, b_dcol, b_pb[bi]], writes=[b_St])
                if not prompt:
                    dma("sp", ossm_s[l, b].rearrange("j m n -> m j n"), St[:], s_St, reads=[b_St])
                elif ci == NPC - 1:
                    dma("sp", ossm_p[l].rearrange("j m n -> m j n"), St[:], s_St, reads=[b_St])
            if not prompt:
                for k4 in range(4):
                    cp(A5[:, 2048 + k4 * 512:2048 + (k4 + 1) * 512], pbank[yoT_banks[k4]][:, :], reads=[b_pb[yoT_banks[k4]]], writes=[b_stT])

            v3 = lambda t: t[0:q, 0:8 * q].rearrange("p (a b) -> p a b", b=q)
            Rt3, Et3, cbL3 = v3(Rf), v3(Ef), v3(cbLf)
            for g in range(8):
                hs = slice(8 * g, 8 * g + 8)
                tt(Rt3, dat[0:q, hs].unsqueeze(2).to_broadcast([q, 8, q]), TRI[0:q, 0:q].unsqueeze(1).to_broadcast([q, 8, q]),
                   ALU.mult, reads=[b_small, b_cst], writes=[b_R])
                nmm = (8 * q + 511) // 512
                for n in range(nmm):
                    wd_ = min(512, 8 * q - n * 512)
                    mm(pbank[n][0:q, 0:wd_], M1[0:q, 0:q], Rf[0:q, n * 512:n * 512 + wd_], True, True, reads=[b_R, b_cst], writes=[b_pb[n]])
                    act(Ef[0:q, n * 512:n * 512 + wd_], pbank[n][0:q, 0:wd_], AF.Exp, reads=[b_pb[n]], writes=[b_E])
                mm(pbank[4][0:q, 0:q], BT(g), CT(g), True, True, reads=[b_cx], writes=[b_pb[4]])
                tt(cbm[0:q, 0:q], pbank[4][0:q, 0:q], TRIB[0:q, 0:q], ALU.mult, reads=[b_pb[4], b_cst], writes=[b_cbm])
                tt(cbL3, Et3, cbm[0:q, 0:q].unsqueeze(1).to_broadcast([q, 8, q]), ALU.mult, reads=[b_E, b_cbm], writes=[b_cbL])
                for h in range(8):
                    hh = 8 * g + h
                    mm(pbank[2][0:q, h * 64:(h + 1) * 64], cbLf[0:q, h * q:(h + 1) * q], xdt_ap[0:q, hh * 64:(hh + 1) * 64], True, True,
                       reads=[b_cbL, b_xdt], writes=[b_pb[2]])
                if prompt:
                    mm(pbank[3][0:q, :], CT(g), stT_ap[:, g * 512:(g + 1) * 512], True, True, reads=[b_cx, b_stT], writes=[b_pb[3]])
                else:
                    for i in range(4):
                        j = 4 * g + i
                        tr(pbank[3][0:q, i * 128:(i + 1) * 128], A5[:, 2048 + j * 64:2048 + j * 64 + q], cT["ident"][:, :], reads=[b_stT], writes=[b_pb[3]])
                tt(tmpg[0:q, :].rearrange("p (h c) -> p h c", c=64), pbank[3][0:q, :].rearrange("p (h c) -> p h c", c=64),
                   ecs[0:q, hs].unsqueeze(2).to_broadcast([q, 8, 64]), ALU.mult, reads=[b_pb[3], b_small], writes=[b_tmpg])
                tt(tmp2[0:q, :].rearrange("p (h c) -> p h c", c=64), xst_ap[0:q, g * 512:(g + 1) * 512].rearrange("p (h c) -> p h c", c=64),
                   dsk_t[0:q, hs].unsqueeze(2).to_broadcast([q, 8, 64]), ALU.mult, reads=[b_xst, b_par], writes=[b_tmp2])
                tt(tmpg[0:q, :], tmpg[0:q, :], pbank[2][0:q, :], ALU.add, reads=[b_tmpg, b_pb[2]], writes=[b_tmpg])
                tt(y_ap[0:q, g * 512:(g + 1) * 512], tmpg[0:q, :], tmp2[0:q, :], ALU.add, reads=[b_tmpg, b_tmp2], writes=[b_y])

        for l in range(NL):
            dma("sp", n1g_t[:], n1g[l], s_par, writes=[b_par])
            dma("sp", n2g_t[:], n2g[l], s_par, writes=[b_par])
            dma("sp", ssmg_t[:], ssmg[l], s_par, writes=[b_par])
            dma("sp", cw_t[:].rearrange("p j k -> p (j k)"), cwc[l], s_par, writes=[b_par])
            dma("sp", cb_t[:], cbc[l], s_par, writes=[b_par])
            dma("sp", dtb_t[:], dtb_d[l], s_par, writes=[b_par])
            dma("sp", a_t[:], alog_d[l], s_par, writes=[b_par])
            dma("sp", dsk_t[:], dsk_d[l], s_par, writes=[b_par])
            act(a_t[:], a_t[:], AF.Exp, reads=[b_par], writes=[b_par])
            ts(a_t[:], a_t[:], -1.0, None, ALU.mult, None, reads=[b_par], writes=[b_par])
            for kind in ("p", "s"):
                q = 128 if kind == "p" else QS
                wsd = wsp_d if kind == "p" else wss_d
                sbd = sbp_d if kind == "p" else sbs_d
                TRIB = trib_p if kind == "p" else trib_s
                dma("pool", wsl[0:q, 0:16 * q], wsd[l], s_wsl, writes=[b_wsl])
                dma("pool", sbrow[0:1, 0:16 * q], sbd[l], s_sbrow, writes=[b_sbrow])
                tt(wsm[0:q, :, 0:q], wsl[0:q, 0:16 * q].rearrange("p (g t) -> p g t", t=q), TRIB[0:q, 0:q].unsqueeze(1).to_broadcast([q, 16, q]),
                   ALU.mult, reads=[b_wsl, b_cst], writes=[b_wsm])
                if kind == "p":
                    for ci in range(NPC):
                        chunk(l, "p", ci)
                else:
                    chunk(l, "s", 0)
        P.op("sp", None, writes=[b_xt, b_St, b_ctok, b_vn, b_v])
        P.emit()
    return nc


def _consts(QS, NSB):
    c = {}
    c["ident"] = np.eye(128, dtype=np.float32)
    i = np.arange(128)
    c["trip"] = (i[:, None] <= i[None, :]).astype(np.float32)
    c["m1p"] = (i[:, None] > i[None, :]).astype(np.float32)
    s = np.arange(QS)
    same = (s[:, None] // 4) == (s[None, :] // 4)
    c["tris"] = (same & (s[:, None] <= s[None, :])).astype(np.float32)
    c["m1s"] = (same & (s[:, None] > s[None, :])).astype(np.float32)
    c["lh0"] = np.zeros((128, 128), np.float32); c["lh0"][:, :64] = 1
    c["lh1"] = np.zeros((128, 128), np.float32); c["lh1"][:, 64:] = 1
    c["lastp"] = np.zeros((128, 1), np.float32); c["lastp"][127, 0] = 1
    c["lasts"] = np.zeros((QS, NSB), np.float32)
    c["blks"] = np.zeros((QS, NSB), np.float32)
    for b in range(NSB):
        c["lasts"][4 * b + 3, b] = 1
        c["blks"][4 * b:4 * b + 4, b] = 1
    return c


def _layout_weights(w, NL):
    f = np.float32
    m = {}
    col = lambda a, nt: np.ascontiguousarray(a.reshape(NL, nt, 128).transpose(0, 2, 1)).astype(f)
    m["n1g"] = col(w["norm1_g"][:NL], 16); m["n2g"] = col(w["norm2_g"][:NL], 16); m["ssmg"] = col(w["ssm_norm_g"][:NL], 32)
    cw = w["conv_w"][:NL].reshape(NL, 4, 48, 128).transpose(0, 3, 2, 1)
    m["cwc"] = np.ascontiguousarray(cw).reshape(NL, 128, 192).astype(f)
    m["cbc"] = col(w["conv_b"][:NL], 48)
    rep = lambda a: np.ascontiguousarray(np.broadcast_to(a[:, None, :], (a.shape[0], 128, a.shape[1]))).astype(f)
    m["lng"] = rep(w["sgu_ln_g"][:NL]); m["lnb"] = rep(w["sgu_ln_b"][:NL])
    m["dtb"] = rep(w["dt_bias"][:NL]); m["alog"] = rep(w["a_log"][:NL]); m["dsk"] = rep(w["d_skip"][:NL])
    m["fng"] = np.ascontiguousarray(np.broadcast_to(w["final_norm_g"][None, :], (128, D))).astype(f)
    sw = w["sgu_w"][:NL]
    m["wsp"] = np.ascontiguousarray(sw.transpose(0, 3, 1, 2)).reshape(NL, 128, 16 * 128).astype(f)
    return m


def kernel(x_prompt, x_sample, state_ssm, state_conv, norm1_g, w_in, conv_w, conv_b, dt_bias, a_log,
           d_skip, ssm_norm_g, sgu_ln_g, sgu_ln_b, sgu_w, sgu_b, w_out_a, w_out_b, w_o, norm2_g,
           w_ffn_gate, w_ffn_up, w_ffn_down, final_norm_g):
    NL = DEPTH
    NSB = 16
    QS = 64
    f = np.float32
    w = dict(norm1_g=np.asarray(norm1_g), norm2_g=np.asarray(norm2_g), ssm_norm_g=np.asarray(ssm_norm_g), conv_w=np.asarray(conv_w),
             conv_b=np.asarray(conv_b), sgu_ln_g=np.asarray(sgu_ln_g), sgu_ln_b=np.asarray(sgu_ln_b), dt_bias=np.asarray(dt_bias),
             a_log=np.asarray(a_log), d_skip=np.asarray(d_skip), final_norm_g=np.asarray(final_norm_g), sgu_w=np.asarray(sgu_w))
    m = _layout_weights(w, NL)
    sw = np.asarray(sgu_w); sbias = np.asarray(sgu_b)
    w4 = np.ascontiguousarray(sw[:, :, :4, :4].transpose(0, 3, 1, 2))
    m["wss"] = np.ascontiguousarray(np.tile(w4, (1, NSB, 1, NSB))).reshape(NL, QS, 16 * QS).astype(f)
    m["sbp"] = np.ascontiguousarray(sbias.reshape(NL, 1, 16 * 128)).astype(f)
    m["sbs"] = np.ascontiguousarray(np.tile(sbias[:, :, :4], (1, 1, NSB))).reshape(NL, 1, 16 * QS).astype(f)
    big = {"w_in": np.asarray(w_in), "w_oa": np.asarray(w_out_a), "w_ob": np.asarray(w_out_b), "w_o": np.asarray(w_o),
           "w_g": np.asarray(w_ffn_gate), "w_u": np.asarray(w_ffn_up), "w_d": np.asarray(w_ffn_down)}
    cs = _consts(QS, NSB)
    xp_all = np.asarray(x_prompt); xs_all = np.asarray(x_sample)
    ssm_all = np.asarray(state_ssm); conv_all = np.asarray(state_conv)
    nc = build_program(2048, NL, NSB)
    in_maps = []
    for c in range(8):
        d = dict(m)
        d.update(big)
        for k, v in cs.items():
            d["c_" + k] = v
        d["xp"] = np.ascontiguousarray(xp_all[c % 4])
        d["xs"] = np.ascontiguousarray(xs_all[c * NSB:(c + 1) * NSB].reshape(QS, D))
        d["sssm"] = np.ascontiguousarray(ssm_all[:, c * NSB:(c + 1) * NSB].reshape(NL, NSB, 32, 128, 128))
        d["sconv"] = np.ascontiguousarray(conv_all[:, c * NSB:(c + 1) * NSB].reshape(NL, NSB * 3, CONVD))
        in_maps.append(d)
    res = run_bass_kernel_spmd(nc, in_maps, core_ids=list(range(8)))
    R = res.results
    y_prompt = np.stack([R[c]["yp"] for c in range(4)]).astype(f)
    y_sample = np.concatenate([R[c]["ys"].reshape(NSB, 4, D) for c in range(8)], axis=0).astype(f)
    ssm_p = np.stack([R[c]["ossm_p"].reshape(NL, 64, 64, 128) for c in range(4)], axis=1).astype(f)
    conv_p = np.stack([R[c]["oconv_p"] for c in range(4)], axis=1).astype(f)
    ssm_s = np.concatenate([R[c]["ossm_s"].reshape(NL, NSB, 64, 64, 128) for c in range(8)], axis=1).astype(f)
    conv_s = np.concatenate([R[c]["oconv_s"].reshape(NL, NSB, 3, CONVD) for c in range(8)], axis=1).astype(f)
    v_s = np.concatenate([R[c]["ov_s"].reshape(NL, NSB, 4, D) for c in range(8)], axis=1).astype(f)
    return (y_prompt, y_sample, ssm_p, conv_p, ssm_s, conv_s, v_s)
```

```python
import numpy as np
from contextlib import ExitStack
import concourse.bass as bass
import concourse.mybir as mybir
from concourse.bass_utils import run_bass_kernel_spmd

F32 = mybir.dt.float32
BF16 = mybir.dt.bfloat16
AF = mybir.ActivationFunctionType
ALU = mybir.AluOpType

D = 2048
DEPTH = 4
DI = 4096
NH = 64
CONVD = 6144
FFN = 5632
INC = 18496
C_U, C_V, C_Z, C_X, C_DT, C_GA, C_GB = 0, 2048, 4096, 8192, 14336, 14400, 16448
EPS = 1e-6
SEM_EPOCH = 30000
NW = 7


class Buf:
    __slots__ = ("name", "w", "r", "alias")

    def __init__(self, name):
        self.name = name
        self.w = None
        self.r = {}
        self.alias = []


def alias(*bufs):
    for a in bufs:
        for b in bufs:
            if a is not b and b not in a.alias:
                a.alias.append(b)


class Prog:
    ENGS = ("pe", "act", "dve", "pool", "sp")

    def __init__(self, nc, stack):
        self.nc = nc
        self.stack = stack
        self.nsem = 0
        self.dma_total = {}
        self.q = {e: [] for e in self.ENGS}
        self.cnt = {e: 0 for e in self.ENGS}
        self.sem = {e: self.new_sem("c_" + e) for e in self.ENGS}
        self.waited = {e: {} for e in self.ENGS}

    def new_sem(self, name):
        self.nsem += 1
        return self.stack.enter_context(self.nc.semaphore("%s_%d" % (name, self.nsem)))

    def new_dma_sem(self, name):
        s = self.new_sem("d_" + name)
        self.dma_total[id(s)] = 0
        return s

    def op(self, e, fn, reads=(), writes=(), dma_sem=None):
        need = {}

        def add(dep):
            if dep is None:
                return
            sem, val, src = dep
            sid = id(sem)
            if sid in self.dma_total:
                val = self.dma_total[sid]
            elif src == e and e == "pe":
                return
            if need.get(sid, (None, 0))[1] < val:
                need[sid] = (sem, val)

        for b in reads:
            add(b.w)
        for b in writes:
            for bb in [b] + b.alias:
                add(bb.w)
                for r in bb.r.values():
                    add(r)
        waits = []
        wd = self.waited[e]
        for sid, (sem, val) in need.items():
            if wd.get(sid, 0) >= val:
                continue
            wd[sid] = val
            waits.append((sem, val))
        if fn is None:
            self.q[e].append((waits, None, None))
            return None
        if dma_sem is not None:
            sid = id(dma_sem)
            self.dma_total[sid] += 16
            comp = (dma_sem, self.dma_total[sid], "dma")
            inc = (dma_sem, 16)
        else:
            if self.cnt[e] >= SEM_EPOCH:
                self.sem[e] = self.new_sem("c_" + e)
                self.cnt[e] = 0
            self.cnt[e] += 1
            comp = (self.sem[e], self.cnt[e], e)
            inc = (self.sem[e], 1)
        for b in reads:
            old = b.r.get(id(comp[0]))
            if old is None or old[1] < comp[1]:
                b.r[id(comp[0])] = comp
        for b in writes:
            b.w = comp
            b.r = {}
        self.q[e].append((waits, fn, inc))
        return comp

    def emit(self):
        nc = self.nc
        with nc.Block() as block:
            def run(eng, lst):
                for waits, fn, inc in lst:
                    for sem, val in waits:
                        eng.wait_ge(sem, val)
                    if fn is None:
                        continue
                    ins = fn(eng)
                    ins.then_inc(inc[0], inc[1])

            @block.tensor
            def _(eng):
                run(eng, self.q["pe"])

            @block.scalar
            def _(eng):
                run(eng, self.q["act"])

            @block.vector
            def _(eng):
                run(eng, self.q["dve"])

            @block.gpsimd
            def _(eng):
                run(eng, self.q["pool"])

            @block.sync
            def _(eng):
                run(eng, self.q["sp"])


def build_program(TP=2048, NL=DEPTH, NSB=16):
    nc = bass.Bass("TRN2", target_bir_lowering=False)
    QS = NSB * 4
    NPC = TP // 128

    def din(name, shape, dt=F32):
        return nc.dram_tensor(name, list(shape), dt, kind="ExternalInput").ap()

    def dout(name, shape):
        return nc.dram_tensor(name, list(shape), F32, kind="ExternalOutput").ap()

    xp = din("xp", [TP, D]); xs = din("xs", [QS, D])
    sssm = din("sssm", [NL, NSB, 32, 128, 128]); sconv = din("sconv", [NL, NSB * 3, CONVD])
    w_in = din("w_in", [NL, D, INC]); w_oa = din("w_oa", [NL, D, D]); w_ob = din("w_ob", [NL, DI, D])
    w_o = din("w_o", [NL, D, D]); w_g = din("w_g", [NL, D, FFN]); w_u = din("w_u", [NL, D, FFN]); w_d = din("w_d", [NL, FFN, D])
    n1g = din("n1g", [NL, 128, 16]); n2g = din("n2g", [NL, 128, 16]); ssmg = din("ssmg", [NL, 128, 32])
    cwc = din("cwc", [NL, 128, 48 * 4]); cbc = din("cbc", [NL, 128, 48])
    lng_d = din("lng", [NL, 128, D]); lnb_d = din("lnb", [NL, 128, D])
    dtb_d = din("dtb", [NL, 128, NH]); alog_d = din("alog", [NL, 128, NH]); dsk_d = din("dsk", [NL, 128, NH])
    fng_d = din("fng", [128, D])
    wsp_d = din("wsp", [NL, 128, 16 * 128]); wss_d = din("wss", [NL, QS, 16 * QS])
    sbp_d = din("sbp", [NL, 1, 16 * 128]); sbs_d = din("sbs", [NL, 1, 16 * QS])
    cst = {}
    for nm, shp in (("ident", [128, 128]), ("trip", [128, 128]), ("m1p", [128, 128]), ("tris", [QS, QS]), ("m1s", [QS, QS]),
                    ("lh0", [128, 128]), ("lh1", [128, 128]), ("lastp", [128, 1]), ("lasts", [QS, NSB]), ("blks", [QS, NSB])):
        cst[nm] = din("c_" + nm, shp)
    def dscr(name, shape):
        return nc.dram_tensor(name, list(shape), BF16).ap()
    wb_in = [dscr("wb_in%d" % i, [D, INC]) for i in range(NL)]; wb_oa = [dscr("wb_oa%d" % i, [D, D]) for i in range(NL)]
    wb_ob = [dscr("wb_ob%d" % i, [DI, D]) for i in range(NL)]; wb_o = [dscr("wb_o%d" % i, [D, D]) for i in range(NL)]
    wb_g = [dscr("wb_g%d" % i, [D, FFN]) for i in range(NL)]; wb_u = [dscr("wb_u%d" % i, [D, FFN]) for i in range(NL)]
    wb_d = [dscr("wb_d%d" % i, [FFN, D]) for i in range(NL)]
    yp = dout("yp", [TP, D]); ys = dout("ys", [QS, D])
    ossm_p = dout("ossm_p", [NL, 32, 128, 128]); oconv_p = dout("oconv_p", [NL, 3, CONVD])
    ossm_s = dout("ossm_s", [NL, NSB, 32, 128, 128]); oconv_s = dout("oconv_s", [NL, NSB * 3, CONVD])
    ov_s = dout("ov_s", [NL, QS, D])

    with ExitStack() as st:
        P = Prog(nc, st)

        def sb(name, shape, dt):
            return st.enter_context(nc.sbuf_tensor(name, list(shape), dt))

        xt2 = [sb("xt0", [128, D], F32), sb("xt1", [128, D], F32)]; b_xt2 = [Buf("xt0"), Buf("xt1")]
        hn = sb("hn", [128, D], BF16); b_hn = Buf("hn")
        hT = sb("hT", [128, 16, 128], BF16); b_hT = Buf("hT")
        hT1 = sb("hT1", [128, 16, 128], BF16); b_hT1 = Buf("hT1")
        hT2 = [hT, hT1]; b_hT2 = [b_hT, b_hT1]
        wring = [sb("wr%d" % i, [128, 1024], BF16) for i in range(NW)]
        b_wr = [Buf("wr%d" % i) for i in range(NW)]
        s_wr = [P.new_dma_sem("wr%d" % i) for i in range(NW)]
        s_wbk = [P.new_dma_sem("wbk%d" % i) for i in range(NW)]
        b_scr = Buf("scr")
        uT = sb("uT", [128, 16, 128], BF16); b_uT = Buf("uT")
        A1 = sb("A1", [128, 4096], F32)
        b_v = Buf("v"); b_vn = Buf("vn"); b_y = Buf("y"); b_sg = Buf("sgate")
        alias(b_v, b_vn, b_y, b_sg)
        v_ap = A1[:, 0:2048]; vn_ap = A1[:, 2048:4096]; y_ap = A1
        sg_ap = A1[:, 0:FFN // 2].bitcast(BF16)
        A2 = sb("A2", [128, 4096], F32)
        b_xdt = Buf("xdt"); b_xdtd = Buf("xdtd"); b_hid = Buf("hid"); b_ctok = Buf("ctok")
        alias(b_xdt, b_xdtd, b_hid, b_ctok)
        xdt_ap = A2[:, 0:2048].bitcast(BF16); xdtd_ap = A2[:, 2048:4096].bitcast(BF16)
        hid_ap = A2[:, 0:FFN // 2].bitcast(BF16)
        ctok_ap = A2[:, 0:1024]
        A3 = sb("A3", [128, 48 * 128], BF16)
        b_cx = Buf("cxT"); b_hidT = Buf("hidT"); alias(b_cx, b_hidT)
        cxT = A3[:].rearrange("p (j t) -> p j t", t=128)
        hidT = A3[:, 0:44 * 128].rearrange("p (j t) -> p j t", t=128)
        A4 = sb("A4", [128, 48 * 131], F32)
        b_xbc = Buf("xbcT"); b_mg = Buf("merged"); b_ybb = Buf("ybb"); b_ybT = Buf("ybT")
        alias(b_xbc, b_mg, b_ybb, b_ybT)
        b_sg1 = Buf("sgate1"); b_hid1 = Buf("hid1")
        alias(b_xbc, b_mg, b_ybb, b_ybT, b_sg1, b_hid1)
        sg1_ap = A4[:, 0:FFN // 2].bitcast(BF16); hid1_ap = A4[:, FFN // 2:FFN].bitcast(BF16)
        mg_ap = A4[:, 0:2048]
        ybb_ap = A4[:, 2048:4096].bitcast(BF16)
        ybT = A4[:, 4096:6144].bitcast(BF16).rearrange("p (j t) -> p j t", t=128)
        A5 = sb("A5", [128, 4096], F32)
        b_lng = Buf("lng"); b_lnb = Buf("lnb"); b_xst = Buf("xs_tok"); b_stT = Buf("stT"); b_fng = Buf("fng")
        alias(b_lng, b_lnb, b_xst, b_stT, b_fng)
        b_hidT1 = Buf("hidT1"); alias(b_lng, b_lnb, b_xst, b_stT, b_fng, b_hidT1)
        hidT1 = A5[:, 0:44 * 64].bitcast(BF16).rearrange("p (j t) -> p j t", t=128)
        lng_ap = A5[:, 0:2048]; lnb_ap = A5[:, 2048:4096]; fng_ap = A5[:, 0:2048]
        xst_ap = A5[:, 0:2048].bitcast(BF16); stT_ap = A5[:, 2048:4096].bitcast(BF16)
        s_ln = P.new_dma_sem("ln")
        sz = sb("sz", [128, DI], BF16); b_sz = Buf("sz")
        Btok = sb("Btok", [128, 1024], BF16); b_Btok = Buf("Btok")
        sga = sb("sga", [128, D], BF16); b_sga = Buf("sga")
        sgb = sb("sgb", [128, D], BF16); b_sgb = Buf("sgb")
        St = sb("St", [128, 32, 128], F32); b_St = Buf("St"); s_St = P.new_dma_sem("St")
        Rf = sb("Rf", [128, 1024], F32); b_R = Buf("R")
        Ef = sb("Ef", [128, 1024], BF16); b_E = Buf("E")
        cbm = sb("cbm", [128, 128], BF16); b_cbm = Buf("cbm")
        cbLf = sb("cbLf", [128, 1024], BF16); b_cbL = Buf("cbL")
        tmpg = sb("tmpg", [128, 512], F32); b_tmpg = Buf("tmpg")
        tmp2 = sb("tmp2", [128, 512], F32); b_tmp2 = Buf("tmp2")
        gel = [sb("gel%d" % i, [128, 512], F32) for i in range(2)]; b_gel = [Buf("gel0"), Buf("gel1")]
        Bm = sb("Bm", [128, 1024], BF16); b_Bm = Buf("Bm")
        dcol = sb("dcol", [128, 32 * 16], F32); b_dcol = Buf("dcol")
        rh0 = sb("rh0", [128, 32 * 16], F32); b_rh0 = Buf("rh0")
        rh1 = sb("rh1", [128, 32 * 16], F32); b_rh1 = Buf("rh1")
        small = sb("small", [128, 8 * 64], F32); b_small = Buf("small")
        dtt = small[:, 0:64]; dat = small[:, 64:128]; cst_ = small[:, 128:192]; dte = small[:, 192:256]; ecs = small[:, 256:320]
        stat = sb("stat", [128, 32], F32); b_stat = Buf("stat")
        wsl = A3[:, 0:16 * 128]; b_wsl = Buf("wsl"); s_wsl = P.new_dma_sem("wsl"); alias(b_cx, b_hidT, b_wsl)
        wsm = sb("wsm", [128, 16, 128], BF16); b_wsm = Buf("wsm")
        sbrow = sb("sbrow", [1, 16 * 128], BF16); b_sbrow = Buf("sbrow"); s_sbrow = P.new_dma_sem("sbrow")
        cw_t = sb("cw_t", [128, 48, 4], F32); cb_t = sb("cb_t", [128, 48], F32)
        n1g_t = sb("n1g_t", [128, 16], F32); n2g_t = sb("n2g_t", [128, 16], F32); ssmg_t = sb("ssmg_t", [128, 32], F32)
        dtb_t = sb("dtb_t", [128, NH], F32); a_t = sb("a_t", [128, NH], F32); dsk_t = sb("dsk_t", [128, NH], F32)
        b_par = Buf("par"); s_par = P.new_dma_sem("par")
        halo = sb("halo", [128, 48, 3], F32); b_halo = Buf("halo")
        acc = [sb("acc%d" % i, [128, 128], F32) for i in range(4)]; b_acc = [Buf("acc%d" % i) for i in range(4)]
        hs_t = A2[:, 1024:1024 + CONVD // 4]
        b_hs = Buf("hs"); s_hs = P.new_dma_sem("hs"); alias(b_xdt, b_xdtd, b_hid, b_ctok, b_hs)
        cT = {}
        for nm in cst:
            shp = list(cst[nm].shape)
            cT[nm] = sb("k_" + nm, shp, F32)
        identb = sb("identb", [128, 128], BF16); trib_p = sb("trib_p", [128, 128], BF16); trib_s = sb("trib_s", [QS, QS], BF16)
        onesb = sb("onesb", [1, 128], BF16)
        b_cst = Buf("cst"); s_cst = P.new_dma_sem("cst")
        s_x = P.new_dma_sem("x"); s_out = P.new_dma_sem("out")

        pbank = [st.enter_context(nc.psum_tensor("pb%d" % i, [128, 512], F32)) for i in range(8)]
        b_pb = [Buf("pb%d" % i) for i in range(8)]

        def dma(eng, out, in_, sem, reads=(), writes=()):
            P.op(eng, lambda e: e.dma_start(out=out, in_=in_), reads=reads, writes=writes, dma_sem=sem)

        def mm(out, lhsT, rhs, start, stop, reads, writes):
            P.op("pe", lambda e: e.matmul(out, lhsT=lhsT, rhs=rhs, start=start, stop=stop), reads=reads, writes=writes)

        def tr(out, in_, ident, reads, writes):
            P.op("pe", lambda e: e.transpose(out=out, in_=in_, identity=ident), reads=list(reads) + [b_cst], writes=writes)

        def act(out, in_, func, reads, writes, bias=None, scale=None, accum=None):
            kw = {}
            if bias is not None:
                kw["bias"] = bias
            if scale is not None:
                kw["scale"] = scale
            if accum is not None:
                kw["accum_out"] = accum
            P.op("act", lambda e: e.activation(out=out, in_=in_, func=func, **kw), reads=reads, writes=writes)

        def tt(out, in0, in1, op, reads, writes, eng="dve"):
            P.op(eng, lambda e: e.tensor_tensor(out=out, in0=in0, in1=in1, op=op), reads=reads, writes=writes)

        def ts(out, in0, s1, s2, op0, op1, reads, writes, eng="dve"):
            if op1 is None:
                P.op(eng, lambda e: e.tensor_scalar(out=out, in0=in0, scalar1=s1, scalar2=None, op0=op0), reads=reads, writes=writes)
            else:
                P.op(eng, lambda e: e.tensor_scalar(out=out, in0=in0, scalar1=s1, scalar2=s2, op0=op0, op1=op1), reads=reads, writes=writes)

        def stt(out, in0, scalar, in1, op0, op1, reads, writes, eng="dve"):
            P.op(eng, lambda e: e.scalar_tensor_tensor(out=out, in0=in0, scalar=scalar, in1=in1, op0=op0, op1=op1), reads=reads, writes=writes)

        def cp(out, in_, reads, writes, eng="dve"):
            P.op(eng, lambda e: e.tensor_copy(out=out, in_=in_), reads=reads, writes=writes)

        ring = {"slot": 0, "grp": 0, "trb": 0}

        def linear(lhs_fn, nk, W, col0, ncols, mode, consumer, q, act_reads, Wb=None, first=True):
            for g0 in range(col0, col0 + ncols, 1024):
                gw = min(1024, col0 + ncols - g0)
                gi = ring["grp"]; ring["grp"] ^= 1
                banks = (2 * gi, 2 * gi + 1)
                for k in range(nk):
                    s = ring["slot"]; ring["slot"] = (s + 1) % NW
                    if first:
                        dma("pool", wring[s][:, 0:gw], W[k * 128:(k + 1) * 128, g0:g0 + gw], s_wr[s], writes=[b_wr[s]])
                    else:
                        dma("sp", wring[s][:, 0:gw], Wb[k * 128:(k + 1) * 128, g0:g0 + gw], s_wr[s], writes=[b_wr[s]])
                    if mode == "tok":
                        for n in range((gw + 511) // 512):
                            w = min(512, gw - n * 512)
                            mm(pbank[banks[n]][0:q, 0:w], lhs_fn(k), wring[s][:, n * 512:n * 512 + w], k == 0, k == nk - 1,
                               reads=[b_wr[s]] + act_reads, writes=[b_pb[banks[n]]])
                    else:
                        for jj in range(gw // 128):
                            n, r = divmod(jj, 4)
                            mm(pbank[banks[n]][:, r * 128:r * 128 + q], wring[s][:, jj * 128:(jj + 1) * 128], lhs_fn(k), (k == 0 and r == 0), k == nk - 1,
                               reads=[b_wr[s]] + act_reads, writes=[b_pb[banks[n]]])
                    if first and Wb is not None:
                        dma("sp", Wb[k * 128:(k + 1) * 128, g0:g0 + gw], wring[s][:, 0:gw], s_wbk[s], reads=[b_wr[s], b_scr])
                for n in range((gw + 511) // 512):
                    w = min(512, gw - n * 512)
                    consumer(banks[n], g0 + n * 512, w)

        def transpose_to(dst3, dst_buf, src, src_buf, ntile, q, scale_col=None, dt=BF16):
            for t0 in range(0, ntile, 4):
                nt = min(4, ntile - t0)
                bi = 4 + ring["trb"]; ring["trb"] ^= 1
                pv = pbank[bi][:].bitcast(BF16) if dt == BF16 else pbank[bi][:]
                for i in range(nt):
                    tr(pv[:, i * 128:i * 128 + q], src[0:q, (t0 + i) * 128:(t0 + i + 1) * 128],
                       (identb if dt == BF16 else cT["ident"])[0:q, 0:q], reads=[src_buf], writes=[b_pb[bi]])
                if scale_col is None:
                    if q == 128:
                        cp(dst3[:, t0:t0 + nt, :], pv[:, 0:nt * 128].rearrange("p (a b) -> p a b", b=128), reads=[b_pb[bi]], writes=[dst_buf])
                    else:
                        cp(dst3[:, t0:t0 + nt, 0:q], pv[:, 0:nt * 128].rearrange("p (a b) -> p a b", b=128)[:, :, 0:q], reads=[b_pb[bi]], writes=[dst_buf])
                else:
                    for i in range(nt):
                        act(dst3[:, t0 + i, 0:q], pv[:, i * 128:i * 128 + q], AF.Identity, reads=[b_pb[bi], b_par], writes=[dst_buf],
                            scale=scale_col[:, t0 + i:t0 + i + 1])

        def rmsnorm_to_T(q, gcol, xt, b_xt, hT, b_hT):
            act(hn[0:q, :], xt[0:q, :], AF.Square, reads=[b_xt], writes=[b_hn, b_stat], accum=stat[0:q, 0:1])
            P.op("dve", None, reads=[b_hn])
            ts(stat[0:q, 1:2], stat[0:q, 0:1], 1.0 / D, EPS, ALU.mult, ALU.add, reads=[b_hn, b_stat], writes=[b_stat])
            act(stat[0:q, 2:3], stat[0:q, 1:2], AF.Sqrt, reads=[b_stat], writes=[b_stat])
            P.op("dve", lambda e: e.reciprocal(out=stat[0:q, 3:4], in_=stat[0:q, 2:3]), reads=[b_stat], writes=[b_stat])
            act(hn[0:q, :], xt[0:q, :], AF.Identity, reads=[b_xt, b_stat], writes=[b_hn], scale=stat[0:q, 3:4])
            transpose_to(hT, b_hT, hn, b_hn, 16, q, scale_col=gcol)

        def gelu_bank(bi, out_ap, out_buf, shape_fn, extra_writes=()):
            gi = bi % 2
            src = shape_fn(pbank[bi])
            t = shape_fn(gel[gi])
            act(t, src, AF.Square, reads=[b_pb[bi]], writes=[b_gel[gi]])
            ts(t, t, 0.044715, 1.0, ALU.mult, ALU.add, reads=[b_gel[gi]], writes=[b_gel[gi]])
            tt(t, t, src, ALU.mult, reads=[b_gel[gi], b_pb[bi]], writes=[b_gel[gi]])
            act(t, t, AF.Sigmoid, reads=[b_gel[gi]], writes=[b_gel[gi]], scale=1.5957691216057308)
            tt(out_ap, t, src, ALU.mult, reads=[b_gel[gi], b_pb[bi]], writes=[out_buf] + list(extra_writes))

        for nm in cst:
            dma("sp", cT[nm][:], cst[nm], s_cst, writes=[b_cst])
        cp(identb[:], cT["ident"][:], reads=[b_cst], writes=[b_cst])
        cp(trib_p[:], cT["trip"][:], reads=[b_cst], writes=[b_cst])
        cp(trib_s[:], cT["tris"][:], reads=[b_cst], writes=[b_cst])
        P.op("dve", lambda e: e.memset(onesb[:], 1.0), writes=[b_cst])

        def mixer(l, kind, ci, slot):
            xt = xt2[slot]; b_xt = b_xt2[slot]
            prompt = kind == "p"
            q = 128 if prompt else QS
            nb = 1 if prompt else NSB
            qb = q // nb
            last_layer = l == NL - 1
            first = prompt and ci == 0
            if (prompt and ci in (1, 2)) or (not prompt):
                P.op("sp", None, writes=[b_scr])
            TRI = cT["trip"] if prompt else cT["tris"]
            M1 = cT["m1p"] if prompt else cT["m1s"]
            TRIB = trib_p if prompt else trib_s
            if l == 0:
                src = xp[ci * 128:(ci + 1) * 128, :] if prompt else xs[:, :]
            else:
                src = yp[ci * 128:(ci + 1) * 128, :] if prompt else ys[:, :]
            xdst = yp[ci * 128:(ci + 1) * 128, :] if prompt else ys[:, :]
            dma("sp", xt[0:q, :], src, s_x, writes=[b_xt])
            dma("sp", lng_ap, lng_d[l], s_ln, writes=[b_lng])
            dma("sp", lnb_ap, lnb_d[l], s_ln, writes=[b_lnb])
            rmsnorm_to_T(q, n1g_t, xt, b_xt, hT, b_hT)
            hfn = lambda k: hT[:, k, 0:q]
            W = w_in[l]

            def cons_u(bi, c0, w):
                j0 = (c0 - C_U) // 128
                gelu_bank(bi, uT[:, j0:j0 + 4, 0:q], b_uT,
                          lambda t: t[:, :].rearrange("p (a b) -> p a b", b=128)[:, :, 0:q])
            linear(hfn, 16, W, C_U, 2048, "feat", cons_u, q, [b_hT], Wb=wb_in[l], first=first)

            def cons_v(bi, c0, w):
                gelu_bank(bi, v_ap[0:q, c0 - C_V:c0 - C_V + w], b_v, lambda t: t[0:q, 0:w])
            linear(hfn, 16, W, C_V, 2048, "tok", cons_v, q, [b_hT], Wb=wb_in[l], first=first)
            act(vn_ap[0:q, :], v_ap[0:q, :], AF.Copy, reads=[b_v], writes=[b_vn, b_stat], accum=stat[0:q, 4:5])
            P.op("dve", None, reads=[b_vn])
            ts(stat[0:q, 5:6], stat[0:q, 4:5], -1.0 / D, None, ALU.mult, None, reads=[b_vn, b_stat], writes=[b_stat])
            act(vn_ap[0:q, :], v_ap[0:q, :], AF.Square, reads=[b_v, b_stat], writes=[b_vn, b_stat], bias=stat[0:q, 5:6], accum=stat[0:q, 6:7])
            P.op("dve", None, reads=[b_vn])
            ts(stat[0:q, 7:8], stat[0:q, 6:7], 1.0 / D, EPS, ALU.mult, ALU.add, reads=[b_vn, b_stat], writes=[b_stat])
            act(stat[0:q, 8:9], stat[0:q, 7:8], AF.Sqrt, reads=[b_stat], writes=[b_stat])
            P.op("dve", lambda e: e.reciprocal(out=stat[0:q, 9:10], in_=stat[0:q, 8:9]), reads=[b_stat], writes=[b_stat])
            tt(stat[0:q, 10:11], stat[0:q, 5:6], stat[0:q, 9:10], ALU.mult, reads=[b_stat], writes=[b_stat])
            act(vn_ap[0:q, :], v_ap[0:q, :], AF.Identity, reads=[b_v, b_stat], writes=[b_vn], bias=stat[0:q, 10:11], scale=stat[0:q, 9:10])
            tt(vn_ap[0:q, :], vn_ap[0:q, :], lng_ap[0:q, :], ALU.mult, reads=[b_vn, b_lng], writes=[b_vn])
            tt(vn_ap[0:q, :], vn_ap[0:q, :], lnb_ap[0:q, :], ALU.add, reads=[b_vn, b_lnb], writes=[b_vn])
            cp(hn[0:q, :], vn_ap[0:q, :], reads=[b_vn], writes=[b_hn])
            if not prompt:
                dma("sp", ov_s[l], vn_ap[0:q, :], s_out, reads=[b_vn])
            for g in range(16):
                bi = 4 + (g % 2)
                mm(pbank[bi][:, 0:q], hn[0:q, g * 128:(g + 1) * 128], wsm[0:q, g, 0:q], True, False, reads=[b_hn, b_wsm], writes=[b_pb[bi]])
                mm(pbank[bi][:, 0:q], onesb[0:1, :], sbrow[0:1, g * q:(g + 1) * q], False, True, reads=[b_sbrow, b_cst], writes=[b_pb[bi]])
                tt(uT[:, g, 0:q], uT[:, g, 0:q], pbank[bi][:, 0:q], ALU.mult, reads=[b_uT, b_pb[bi]], writes=[b_uT])

            def cons_z(bi, c0, w):
                act(sz[0:q, c0 - C_Z:c0 - C_Z + w], pbank[bi][0:q, 0:w], AF.Silu, reads=[b_pb[bi]], writes=[b_sz])
            linear(hfn, 16, W, C_Z, DI, "tok", cons_z, q, [b_hT], Wb=wb_in[l], first=first)

            if prompt:
                xb3 = A4[:, :].rearrange("p (j t) -> p j t", t=131)
                newv = lambda j0, nj: xb3[:, j0:j0 + nj, 3:131]
                if ci == 0:
                    P.op("dve", lambda e: e.memset(halo[:], 0.0), writes=[b_halo])
                cp(xb3[:, :, 0:3], halo[:], reads=[b_halo], writes=[b_xbc])
            else:
                xb4 = A4[:, 0:48 * NSB * 7].rearrange("p (j b t) -> p j b t", b=NSB, t=7)
                newv = lambda j0, nj: xb4[:, j0:j0 + nj, :, 3:7]
                for qq in range(4):
                    dma("sp", hs_t[0:NSB * 3, :], sconv[l][:, qq * 1536:(qq + 1) * 1536], s_hs, writes=[b_hs])
                    for jj in range(12):
                        j = qq * 12 + jj
                        bi = 4 + (j % 2)
                        tr(pbank[bi][:, 0:NSB * 3], hs_t[0:NSB * 3, jj * 128:(jj + 1) * 128], cT["ident"][0:NSB * 3, 0:NSB * 3], reads=[b_hs], writes=[b_pb[bi]])
                        cp(xb4[:, j, :, 0:3], pbank[bi][:, 0:NSB * 3].rearrange("p (b r) -> p b r", r=3), reads=[b_pb[bi]], writes=[b_xbc])

            def cons_x(bi, c0, w):
                j0 = (c0 - C_X) // 128
                if prompt:
                    cp(newv(j0, 4), pbank[bi][:, :].rearrange("p (a b) -> p a b", b=128), reads=[b_pb[bi]], writes=[b_xbc])
                else:
                    cp(newv(j0, 4), pbank[bi][:, :].rearrange("p (a b) -> p a b", b=128)[:, :, 0:q].rearrange("p a (b t) -> p a b t", t=4),
                       reads=[b_pb[bi]], writes=[b_xbc])
            linear(hfn, 16, W, C_X, CONVD, "feat", cons_x, q, [b_hT], Wb=wb_in[l], first=first)

            def cons_dt(bi, c0, w):
                tt(dtt[0:q, :], pbank[bi][0:q, 0:64], dtb_t[0:q, :], ALU.add, reads=[b_pb[bi], b_par], writes=[b_small])
                act(dtt[0:q, :], dtt[0:q, :], AF.Exp, reads=[b_small], writes=[b_small])
                ts(dtt[0:q, :], dtt[0:q, :], 1.0, None, ALU.add, None, reads=[b_small], writes=[b_small])
                act(dtt[0:q, :], dtt[0:q, :], AF.Ln, reads=[b_small], writes=[b_small])
                tt(dat[0:q, :], dtt[0:q, :], a_t[0:q, :], ALU.mult, reads=[b_small, b_par], writes=[b_small])
            linear(hfn, 16, W, C_DT, 64, "tok", cons_dt, q, [b_hT], Wb=wb_in[l], first=first)

            def cons_ga(bi, c0, w):
                act(sga[0:q, c0 - C_GA:c0 - C_GA + w], pbank[bi][0:q, 0:w], AF.Sigmoid, reads=[b_pb[bi]], writes=[b_sga])

            def cons_gb(bi, c0, w):
                act(sgb[0:q, c0 - C_GB:c0 - C_GB + w], pbank[bi][0:q, 0:w], AF.Sigmoid, reads=[b_pb[bi]], writes=[b_sgb])
            linear(hfn, 16, W, C_GA, 2048, "tok", cons_ga, q, [b_hT], Wb=wb_in[l], first=first)
            linear(hfn, 16, W, C_GB, 2048, "tok", cons_gb, q, [b_hT], Wb=wb_in[l], first=first)

            if (prompt and ci == NPC - 1) or not prompt:
                dstv = None if prompt else oconv_s[l].rearrange("(b r) c -> r b c", r=3)
                for bt in range(6):
                    for r in range(1 if prompt else 3):
                        for half in range(2):
                            bi = 4 + half
                            for i in range(4):
                                j = bt * 8 + half * 4 + i
                                if prompt:
                                    tr(pbank[bi][0:q, i * 128:(i + 1) * 128], xb3[:, j, 3:131], cT["ident"][:, :], reads=[b_xbc], writes=[b_pb[bi]])
                                else:
                                    tr(pbank[bi][0:NSB, i * 128:(i + 1) * 128], xb4[:, j, :, 4 + r], cT["ident"][:, :], reads=[b_xbc], writes=[b_pb[bi]])
                            nr = q if prompt else NSB
                            cp(ctok_ap[0:nr, half * 512:(half + 1) * 512], pbank[bi][0:nr, :], reads=[b_pb[bi]], writes=[b_ctok])
                        if prompt:
                            dma("sp", oconv_p[l][:, bt * 1024:(bt + 1) * 1024], ctok_ap[125:128, :], s_out, reads=[b_ctok])
                        else:
                            dma("sp", dstv[r][:, bt * 1024:(bt + 1) * 1024], ctok_ap[0:NSB, :], s_out, reads=[b_ctok])
            if prompt:
                cp(halo[:], xb3[:, :, 128:131], reads=[b_xbc], writes=[b_halo])

            for j in range(48):
                a = acc[j % 4]; ba = b_acc[j % 4]
                if prompt:
                    av = a[:, 0:q]
                    xk = lambda k: xb3[:, j, k:k + q]
                else:
                    av = a[:, 0:q].rearrange("p (b t) -> p b t", t=4)
                    xk = lambda k: xb4[:, j, :, k:k + 4]
                ts(av, xk(0), cw_t[:, j, 0:1], cb_t[:, j:j + 1], ALU.mult, ALU.add, reads=[b_xbc, b_par], writes=[ba])
                for k in range(1, 4):
                    stt(av, xk(k), cw_t[:, j, k:k + 1], av, ALU.mult, ALU.add, reads=[b_xbc, b_par, ba], writes=[ba])
                act(cxT[:, j, 0:q], a[:, 0:q], AF.Silu, reads=[ba], writes=[b_cx])
            transpose_to_tok(q)

            ssd(l, kind, ci, q, nb, qb, TRI, M1, TRIB)

            tt(y_ap[0:q, :], y_ap[0:q, :], sz[0:q, :], ALU.mult, reads=[b_y, b_sz], writes=[b_y])
            for g in range(8):
                act(tmpg[0:q, :], y_ap[0:q, g * 512:(g + 1) * 512], AF.Square, reads=[b_y], writes=[b_tmpg, b_stat], accum=stat[0:q, 12 + g:13 + g])
            P.op("dve", None, reads=[b_tmpg])
            ts(stat[0:q, 20:28], stat[0:q, 12:20], 1.0 / 512, EPS, ALU.mult, ALU.add, reads=[b_tmpg, b_stat], writes=[b_stat])
            act(stat[0:q, 20:28], stat[0:q, 20:28], AF.Sqrt, reads=[b_stat], writes=[b_stat])
            P.op("dve", lambda e: e.reciprocal(out=stat[0:q, 20:28], in_=stat[0:q, 20:28]), reads=[b_stat], writes=[b_stat])
            tt(ybb_ap[0:q, :].rearrange("p (g c) -> p g c", c=512), y_ap[0:q, :].rearrange("p (g c) -> p g c", c=512),
               stat[0:q, 20:28].unsqueeze(2).to_broadcast([q, 8, 512]), ALU.mult, reads=[b_y, b_stat], writes=[b_ybb])
            transpose_to(ybT, b_ybT, ybb_ap, b_ybb, 32, q, scale_col=ssmg_t)

            def cons_pa(bi, c0, w):
                tt(mg_ap[0:q, c0:c0 + w], pbank[bi][0:q, 0:w], sga[0:q, c0:c0 + w], ALU.mult, reads=[b_pb[bi], b_sga], writes=[b_mg])
            linear(lambda k: uT[:, k, 0:q], 16, w_oa[l], 0, D, "tok", cons_pa, q, [b_uT], Wb=wb_oa[l], first=first)

            def cons_pb(bi, c0, w):
                gi = bi % 2
                tt(gel[gi][0:q, 0:w], pbank[bi][0:q, 0:w], sgb[0:q, c0:c0 + w], ALU.mult, reads=[b_pb[bi], b_sgb], writes=[b_gel[gi]])
                tt(mg_ap[0:q, c0:c0 + w], mg_ap[0:q, c0:c0 + w], gel[gi][0:q, 0:w], ALU.add, reads=[b_mg, b_gel[gi]], writes=[b_mg])
            linear(lambda k: ybT[:, k, 0:q], 32, w_ob[l], 0, D, "tok", cons_pb, q, [b_ybT], Wb=wb_ob[l], first=first)
            cp(hn[0:q, :], mg_ap[0:q, :], reads=[b_mg], writes=[b_hn])
            transpose_to(hT, b_hT, hn, b_hn, 16, q)

            def cons_res(bi, c0, w):
                tt(xt[0:q, c0:c0 + w], xt[0:q, c0:c0 + w], pbank[bi][0:q, 0:w], ALU.add, reads=[b_pb[bi], b_xt], writes=[b_xt])
            linear(hfn, 16, w_o[l], 0, D, "tok", cons_res, q, [b_hT], Wb=wb_o[l], first=first)


        def linear_multi(entries, nk, W, ncols, Wb, first):
            for g0 in range(0, ncols, 1024):
                gw = min(1024, ncols - g0)
                gi = ring["grp"]; ring["grp"] ^= 1
                for k in range(nk):
                    s = ring["slot"]; ring["slot"] = (s + 1) % NW
                    if first:
                        dma("pool", wring[s][:, 0:gw], W[k * 128:(k + 1) * 128, g0:g0 + gw], s_wr[s], writes=[b_wr[s]])
                    else:
                        dma("sp", wring[s][:, 0:gw], Wb[k * 128:(k + 1) * 128, g0:g0 + gw], s_wr[s], writes=[b_wr[s]])
                    for i, (lhs_fn, q, rds, consumer) in enumerate(entries):
                        for n in range((gw + 511) // 512):
                            w = min(512, gw - n * 512)
                            bk = 2 * i + 4 * gi + n
                            mm(pbank[bk][0:q, 0:w], lhs_fn(k), wring[s][:, n * 512:n * 512 + w], k == 0, k == nk - 1,
                               reads=[b_wr[s]] + rds, writes=[b_pb[bk]])
                    if first:
                        dma("sp", Wb[k * 128:(k + 1) * 128, g0:g0 + gw], wring[s][:, 0:gw], s_wbk[s], reads=[b_wr[s], b_scr])
                for i, (lhs_fn, q, rds, consumer) in enumerate(entries):
                    for n in range((gw + 511) // 512):
                        w = min(512, gw - n * 512)
                        consumer(2 * i + 4 * gi + n, g0 + n * 512, w)

        def ffn(l, items):
            last_layer = l == NL - 1
            first = any(kind == "p" and ci == 0 for kind, ci, slot in items)
            SG = [sg_ap, sg1_ap]; bSG = [b_sg, b_sg1]; HID = [hid_ap, hid1_ap]; bHID = [b_hid, b_hid1]
            HIDT = [hidT, hidT1]; bHIDT = [b_hidT, b_hidT1]
            qs = [128 if kind == "p" else QS for kind, ci, slot in items]
            for i, (kind, ci, slot) in enumerate(items):
                rmsnorm_to_T(qs[i], n2g_t, xt2[slot], b_xt2[slot], hT2[i], b_hT2[i])

            def mk_g(i):
                q = qs[i]
                return lambda bi, c0, w: act(SG[i][0:q, c0:c0 + w], pbank[bi][0:q, 0:w], AF.Silu, reads=[b_pb[bi]], writes=[bSG[i]])

            def mk_u(i):
                q = qs[i]
                return lambda bi, c0, w: tt(HID[i][0:q, c0:c0 + w], pbank[bi][0:q, 0:w], SG[i][0:q, c0:c0 + w], ALU.mult,
                                            reads=[b_pb[bi], bSG[i]], writes=[bHID[i]])

            def mk_r(i):
                q = qs[i]; slot = items[i][2]
                return lambda bi, c0, w: tt(xt2[slot][0:q, c0:c0 + w], xt2[slot][0:q, c0:c0 + w], pbank[bi][0:q, 0:w], ALU.add,
                                            reads=[b_pb[bi], b_xt2[slot]], writes=[b_xt2[slot]])
            ent = lambda mk: [((lambda k, i=i: hT2[i][:, k, 0:qs[i]]), qs[i], [b_hT2[i]], mk(i)) for i in range(len(items))]
            linear_multi(ent(mk_g), 16, w_g[l], FFN, wb_g[l], first)
            linear_multi(ent(mk_u), 16, w_u[l], FFN, wb_u[l], first)
            for i in range(len(items)):
                transpose_to(HIDT[i], bHIDT[i], HID[i], bHID[i], 44, qs[i])
            entd = [((lambda k, i=i: HIDT[i][:, k, 0:qs[i]]), qs[i], [bHIDT[i]], mk_r(i)) for i in range(len(items))]
            linear_multi(entd, 44, w_d[l], D, wb_d[l], first)
            for i, (kind, ci, slot) in enumerate(items):
                q = qs[i]; xt = xt2[slot]; b_xt = b_xt2[slot]
                xdst = yp[ci * 128:(ci + 1) * 128, :] if kind == "p" else ys[:, :]
                if last_layer:
                    dma("sp", fng_ap, fng_d, s_ln, writes=[b_fng])
                    act(hn[0:q, :], xt[0:q, :], AF.Square, reads=[b_xt], writes=[b_hn, b_stat], accum=stat[0:q, 0:1])
                    ts(stat[0:q, 1:2], stat[0:q, 0:1], 1.0 / D, EPS, ALU.mult, ALU.add, reads=[b_hn, b_stat], writes=[b_stat])
                    act(stat[0:q, 2:3], stat[0:q, 1:2], AF.Sqrt, reads=[b_stat], writes=[b_stat])
                    P.op("dve", lambda e, q=q: e.reciprocal(out=stat[0:q, 3:4], in_=stat[0:q, 2:3]), reads=[b_stat], writes=[b_stat])
                    stt(xt[0:q, :], xt[0:q, :], stat[0:q, 3:4], fng_ap[0:q, :], ALU.mult, ALU.mult, reads=[b_xt, b_stat, b_fng], writes=[b_xt])
                dma("sp", xdst, xt[0:q, :], s_x, reads=[b_xt])

        def transpose_to_tok(q):
            for t0 in range(0, 40, 4):
                bi = 4 + ring["trb"]; ring["trb"] ^= 1
                pv = pbank[bi][:].bitcast(BF16)
                for i in range(4):
                    tr(pv[0:q, i * 128:(i + 1) * 128], cxT[:, t0 + i, 0:q], identb[:, :], reads=[b_cx], writes=[b_pb[bi]])
                if t0 < 32:
                    cp(xst_ap[0:q, t0 * 128:(t0 + 4) * 128], pv[0:q, 0:512], reads=[b_pb[bi]], writes=[b_xst])
                else:
                    cp(Btok[0:q, (t0 - 32) * 128:(t0 - 28) * 128], pv[0:q, 0:512], reads=[b_pb[bi]], writes=[b_Btok])

        def ssd(l, kind, ci, q, nb, qb, TRI, M1, TRIB):
            prompt = kind == "p"
            BT = lambda g: cxT[:, 32 + g, 0:q]
            CT = lambda g: cxT[:, 40 + g, 0:q]
            mm(pbank[5][0:q, 0:64], TRI[0:q, 0:q], dat[0:q, :], True, True, reads=[b_small, b_cst], writes=[b_pb[5]])
            mm(pbank[5][0:q, 64:128], M1[0:q, 0:q], dat[0:q, :], True, True, reads=[b_small, b_cst], writes=[b_pb[5]])
            cp(cst_[0:q, :], pbank[5][0:q, 0:64], reads=[b_pb[5]], writes=[b_small])
            act(ecs[0:q, :], pbank[5][0:q, 0:64], AF.Exp, reads=[b_pb[5]], writes=[b_small])
            act(dte[0:q, :], pbank[5][0:q, 64:128], AF.Exp, reads=[b_pb[5]], writes=[b_small])
            xs3 = xst_ap[0:q, :].rearrange("p (h c) -> p h c", c=64)
            tt(xdt_ap[0:q, :].rearrange("p (h c) -> p h c", c=64), xs3, dtt[0:q, :].unsqueeze(2).to_broadcast([q, 64, 64]), ALU.mult,
               reads=[b_xst, b_small], writes=[b_xdt])
            tt(xdtd_ap[0:q, :].rearrange("p (h c) -> p h c", c=64), xdt_ap[0:q, :].rearrange("p (h c) -> p h c", c=64),
               dte[0:q, :].unsqueeze(2).to_broadcast([q, 64, 64]), ALU.mult, reads=[b_xdt, b_small], writes=[b_xdtd])
            LAST = cT["lastp"] if prompt else cT["lasts"]
            cs3 = cst_[0:q, :].rearrange("p (j two) -> p j two", two=2)
            for h2, rh, brh in ((0, rh0, b_rh0), (1, rh1, b_rh1)):
                tt(rh[0:q, 0:32 * nb].rearrange("p (j b) -> p j b", b=nb), cs3[:, :, h2:h2 + 1].to_broadcast([q, 32, nb]),
                   LAST[0:q, 0:nb].unsqueeze(1).to_broadcast([q, 32, nb]), ALU.mult, reads=[b_small, b_cst], writes=[brh])
            mm(pbank[5][:, 128:128 + 32 * nb] if nb == 1 else pbank[6][:, 0:32 * nb], cT["lh0"][0:q, :], rh0[0:q, 0:32 * nb], True, False,
               reads=[b_rh0, b_cst], writes=[b_pb[5] if nb == 1 else b_pb[6]])
            mm(pbank[5][:, 128:128 + 32 * nb] if nb == 1 else pbank[6][:, 0:32 * nb], cT["lh1"][0:q, :], rh1[0:q, 0:32 * nb], False, True,
               reads=[b_rh1, b_cst], writes=[b_pb[5] if nb == 1 else b_pb[6]])
            act(dcol[:, 0:32 * nb], pbank[5][:, 128:128 + 32 * nb] if nb == 1 else pbank[6][:, 0:32 * nb], AF.Exp,
                reads=[b_pb[5] if nb == 1 else b_pb[6]], writes=[b_dcol])
            dc3 = dcol[:, 0:32 * nb].rearrange("p (j b) -> p j b", b=nb)

            if prompt and ci == 0:
                P.op("dve", lambda e: e.memset(St[:], 0.0), writes=[b_St])
            yoT_banks = (0, 1, 2, 3)
            for b in range(nb):
                if not prompt:
                    dma("sp", St[:], sssm[l, b].rearrange("j m n -> m j n"), s_St, writes=[b_St])
                for t0 in range(0, 32, 4):
                    bi = 4 + ring["trb"]; ring["trb"] ^= 1
                    for i in range(4):
                        tr(pbank[bi][:, i * 128:(i + 1) * 128], St[:, t0 + i, :], cT["ident"][:, :], reads=[b_St], writes=[b_pb[bi]])
                    if (t0 // 4) % 2 == 0:
                        cp(stT_ap[:, t0 * 128:(t0 + 4) * 128], pbank[bi][:, :], reads=[b_pb[bi]], writes=[b_stT])
                    else:
                        act(stT_ap[:, t0 * 128:(t0 + 4) * 128], pbank[bi][:, :], AF.Copy, reads=[b_pb[bi]], writes=[b_stT])
                if not prompt:
                    for j in range(32):
                        bk = yoT_banks[j // 8]
                        mm(pbank[bk][:, (j % 8) * 64 + b * 4:(j % 8) * 64 + b * 4 + 4], stT_ap[:, j * 128:(j + 1) * 128], CT(j // 4)[:, b * 4:b * 4 + 4],
                           True, True, reads=[b_stT, b_cx], writes=[b_pb[bk]])
                    ts(Bm[0:q, :], Btok[0:q, :], cT["blks"][0:q, b:b + 1], None, ALU.mult, None, reads=[b_Btok, b_cst], writes=[b_Bm])
                    Bsrc, bB = Bm, b_Bm
                else:
                    Bsrc, bB = Btok, b_Btok
                if prompt:
                    y_off_prompt = None
                for j in range(32):
                    bi = 6 + (j % 2) if prompt else 6 + (j % 2)
                    g = j // 4
                    mm(pbank[bi][:, 0:128], xdtd_ap[0:q, j * 128:(j + 1) * 128], Bsrc[0:q, g * 128:(g + 1) * 128], True, True,
                       reads=[b_xdtd, bB], writes=[b_pb[bi]])
                    stt(St[:, j, :], St[:, j, :], dc3[:, j, b:b + 1], pbank[bi][:, 0:128], ALU.mult, ALU.add,
                        reads=[b_St, b_dcol, b_pb[bi]], writes=[b_St])
                if not prompt:
                    dma("sp", ossm_s[l, b].rearrange("j m n -> m j n"), St[:], s_St, reads=[b_St])
                elif ci == NPC - 1:
                    dma("sp", ossm_p[l].rearrange("j m n -> m j n"), St[:], s_St, reads=[b_St])
            if not prompt:
                for k4 in range(4):
                    cp(A5[:, 2048 + k4 * 512:2048 + (k4 + 1) * 512], pbank[yoT_banks[k4]][:, :], reads=[b_pb[yoT_banks[k4]]], writes=[b_stT])

            v3 = lambda t: t[0:q, 0:8 * q].rearrange("p (a b) -> p a b", b=q)
            Rt3, Et3, cbL3 = v3(Rf), v3(Ef), v3(cbLf)
            for g in range(8):
                hs = slice(8 * g, 8 * g + 8)
                tt(Rt3, dat[0:q, hs].unsqueeze(2).to_broadcast([q, 8, q]), TRI[0:q, 0:q].unsqueeze(1).to_broadcast([q, 8, q]),
                   ALU.mult, reads=[b_small, b_cst], writes=[b_R])
                nmm = (8 * q + 511) // 512
                for n in range(nmm):
                    wd_ = min(512, 8 * q - n * 512)
                    mm(pbank[n][0:q, 0:wd_], M1[0:q, 0:q], Rf[0:q, n * 512:n * 512 + wd_], True, True, reads=[b_R, b_cst], writes=[b_pb[n]])
                    act(Ef[0:q, n * 512:n * 512 + wd_], pbank[n][0:q, 0:wd_], AF.Exp, reads=[b_pb[n]], writes=[b_E])
                mm(pbank[4][0:q, 0:q], BT(g), CT(g), True, True, reads=[b_cx], writes=[b_pb[4]])
                tt(cbm[0:q, 0:q], pbank[4][0:q, 0:q], TRIB[0:q, 0:q], ALU.mult, reads=[b_pb[4], b_cst], writes=[b_cbm])
                tt(cbL3, Et3, cbm[0:q, 0:q].unsqueeze(1).to_broadcast([q, 8, q]), ALU.mult, reads=[b_E, b_cbm], writes=[b_cbL])
                for h in range(8):
                    hh = 8 * g + h
                    mm(pbank[2][0:q, h * 64:(h + 1) * 64], cbLf[0:q, h * q:(h + 1) * q], xdt_ap[0:q, hh * 64:(hh + 1) * 64], True, True,
                       reads=[b_cbL, b_xdt], writes=[b_pb[2]])
                if prompt:
                    mm(pbank[3][0:q, :], CT(g), stT_ap[:, g * 512:(g + 1) * 512], True, True, reads=[b_cx, b_stT], writes=[b_pb[3]])
                else:
                    for i in range(4):
                        j = 4 * g + i
                        tr(pbank[3][0:q, i * 128:(i + 1) * 128], A5[:, 2048 + j * 64:2048 + j * 64 + q], cT["ident"][:, :], reads=[b_stT], writes=[b_pb[3]])
                tt(tmpg[0:q, :].rearrange("p (h c) -> p h c", c=64), pbank[3][0:q, :].rearrange("p (h c) -> p h c", c=64),
                   ecs[0:q, hs].unsqueeze(2).to_broadcast([q, 8, 64]), ALU.mult, reads=[b_pb[3], b_small], writes=[b_tmpg])
                tt(tmp2[0:q, :].rearrange("p (h c) -> p h c", c=64), xst_ap[0:q, g * 512:(g + 1) * 512].rearrange("p (h c) -> p h c", c=64),
                   dsk_t[0:q, hs].unsqueeze(2).to_broadcast([q, 8, 64]), ALU.mult, reads=[b_xst, b_par], writes=[b_tmp2])
                tt(tmpg[0:q, :], tmpg[0:q, :], pbank[2][0:q, :], ALU.add, reads=[b_tmpg, b_pb[2]], writes=[b_tmpg])
                tt(y_ap[0:q, g * 512:(g + 1) * 512], tmpg[0:q, :], tmp2[0:q, :], ALU.add, reads=[b_tmpg, b_tmp2], writes=[b_y])

        for l in range(NL):
            dma("sp", n1g_t[:], n1g[l], s_par, writes=[b_par])
            dma("sp", n2g_t[:], n2g[l], s_par, writes=[b_par])
            dma("sp", ssmg_t[:], ssmg[l], s_par, writes=[b_par])
            dma("sp", cw_t[:].rearrange("p j k -> p (j k)"), cwc[l], s_par, writes=[b_par])
            dma("sp", cb_t[:], cbc[l], s_par, writes=[b_par])
            dma("sp", dtb_t[:], dtb_d[l], s_par, writes=[b_par])
            dma("sp", a_t[:], alog_d[l], s_par, writes=[b_par])
            dma("sp", dsk_t[:], dsk_d[l], s_par, writes=[b_par])
            act(a_t[:], a_t[:], AF.Exp, reads=[b_par], writes=[b_par])
            ts(a_t[:], a_t[:], -1.0, None, ALU.mult, None, reads=[b_par], writes=[b_par])
            for kind in ("p", "s"):
                q = 128 if kind == "p" else QS
                wsd = wsp_d if kind == "p" else wss_d
                sbd = sbp_d if kind == "p" else sbs_d
                TRIB = trib_p if kind == "p" else trib_s
                dma("pool", wsl[0:q, 0:16 * q], wsd[l], s_wsl, writes=[b_wsl])
                dma("pool", sbrow[0:1, 0:16 * q], sbd[l], s_sbrow, writes=[b_sbrow])
                tt(wsm[0:q, :, 0:q], wsl[0:q, 0:16 * q].rearrange("p (g t) -> p g t", t=q), TRIB[0:q, 0:q].unsqueeze(1).to_broadcast([q, 16, q]),
                   ALU.mult, reads=[b_wsl, b_cst], writes=[b_wsm])
                if kind == "p":
                    for c0 in range(0, NPC, 2):
                        items = []
                        for j, ci in enumerate(range(c0, min(c0 + 2, NPC))):
                            mixer(l, "p", ci, j)
                            items.append(("p", ci, j))
                        ffn(l, items)
                else:
                    mixer(l, "s", 0, 0)
                    ffn(l, [("s", 0, 0)])
        P.op("sp", None, writes=[b_xt2[0], b_xt2[1], b_St, b_ctok, b_vn, b_v])
        P.emit()
    return nc


def _consts(QS, NSB):
    c = {}
    c["ident"] = np.eye(128, dtype=np.float32)
    i = np.arange(128)
    c["trip"] = (i[:, None] <= i[None, :]).astype(np.float32)
    c["m1p"] = (i[:, None] > i[None, :]).astype(np.float32)
    s = np.arange(QS)
    same = (s[:, None] // 4) == (s[None, :] // 4)
    c["tris"] = (same & (s[:, None] <= s[None, :])).astype(np.float32)
    c["m1s"] = (same & (s[:, None] > s[None, :])).astype(np.float32)
    c["lh0"] = np.zeros((128, 128), np.float32); c["lh0"][:, :64] = 1
    c["lh1"] = np.zeros((128, 128), np.float32); c["lh1"][:, 64:] = 1
    c["lastp"] = np.zeros((128, 1), np.float32); c["lastp"][127, 0] = 1
    c["lasts"] = np.zeros((QS, NSB), np.float32)
    c["blks"] = np.zeros((QS, NSB), np.float32)
    for b in range(NSB):
        c["lasts"][4 * b + 3, b] = 1
        c["blks"][4 * b:4 * b + 4, b] = 1
    return c


def _layout_weights(w, NL):
    f = np.float32
    m = {}
    col = lambda a, nt: np.ascontiguousarray(a.reshape(NL, nt, 128).transpose(0, 2, 1)).astype(f)
    m["n1g"] = col(w["norm1_g"][:NL], 16); m["n2g"] = col(w["norm2_g"][:NL], 16); m["ssmg"] = col(w["ssm_norm_g"][:NL], 32)
    cw = w["conv_w"][:NL].reshape(NL, 4, 48, 128).transpose(0, 3, 2, 1)
    m["cwc"] = np.ascontiguousarray(cw).reshape(NL, 128, 192).astype(f)
    m["cbc"] = col(w["conv_b"][:NL], 48)
    rep = lambda a: np.ascontiguousarray(np.broadcast_to(a[:, None, :], (a.shape[0], 128, a.shape[1]))).astype(f)
    m["lng"] = rep(w["sgu_ln_g"][:NL]); m["lnb"] = rep(w["sgu_ln_b"][:NL])
    m["dtb"] = rep(w["dt_bias"][:NL]); m["alog"] = rep(w["a_log"][:NL]); m["dsk"] = rep(w["d_skip"][:NL])
    m["fng"] = np.ascontiguousarray(np.broadcast_to(w["final_norm_g"][None, :], (128, D))).astype(f)
    sw = w["sgu_w"][:NL]
    m["wsp"] = np.ascontiguousarray(sw.transpose(0, 3, 1, 2)).reshape(NL, 128, 16 * 128).astype(f)
    return m


def kernel(x_prompt, x_sample, state_ssm, state_conv, norm1_g, w_in, conv_w, conv_b, dt_bias, a_log,
           d_skip, ssm_norm_g, sgu_ln_g, sgu_ln_b, sgu_w, sgu_b, w_out_a, w_out_b, w_o, norm2_g,
           w_ffn_gate, w_ffn_up, w_ffn_down, final_norm_g):
    NL = DEPTH
    NSB = 16
    QS = 64
    f = np.float32
    w = dict(norm1_g=np.asarray(norm1_g), norm2_g=np.asarray(norm2_g), ssm_norm_g=np.asarray(ssm_norm_g), conv_w=np.asarray(conv_w),
             conv_b=np.asarray(conv_b), sgu_ln_g=np.asarray(sgu_ln_g), sgu_ln_b=np.asarray(sgu_ln_b), dt_bias=np.asarray(dt_bias),
             a_log=np.asarray(a_log), d_skip=np.asarray(d_skip), final_norm_g=np.asarray(final_norm_g), sgu_w=np.asarray(sgu_w))
    m = _layout_weights(w, NL)
    sw = np.asarray(sgu_w); sbias = np.asarray(sgu_b)
    w4 = np.ascontiguousarray(sw[:, :, :4, :4].transpose(0, 3, 1, 2))
    m["wss"] = np.ascontiguousarray(np.tile(w4, (1, NSB, 1, NSB))).reshape(NL, QS, 16 * QS).astype(f)
    m["sbp"] = np.ascontiguousarray(sbias.reshape(NL, 1, 16 * 128)).astype(f)
    m["sbs"] = np.ascontiguousarray(np.tile(sbias[:, :, :4], (1, 1, NSB))).reshape(NL, 1, 16 * QS).astype(f)
    big = {"w_in": np.asarray(w_in), "w_oa": np.asarray(w_out_a), "w_ob": np.asarray(w_out_b), "w_o": np.asarray(w_o),
           "w_g": np.asarray(w_ffn_gate), "w_u": np.asarray(w_ffn_up), "w_d": np.asarray(w_ffn_down)}
    cs = _consts(QS, NSB)
    xp_all = np.asarray(x_prompt); xs_all = np.asarray(x_sample)
    ssm_all = np.asarray(state_ssm); conv_all = np.asarray(state_conv)
    nc = build_program(2048, NL, NSB)
    in_maps = []
    for c in range(8):
        d = dict(m)
        d.update(big)
        for k, v in cs.items():
            d["c_" + k] = v
        d["xp"] = np.ascontiguousarray(xp_all[c % 4])
        d["xs"] = np.ascontiguousarray(xs_all[c * NSB:(c + 1) * NSB].reshape(QS, D))
        d["sssm"] = np.ascontiguousarray(ssm_all[:, c * NSB:(c + 1) * NSB].reshape(NL, NSB, 32, 128, 128))
        d["sconv"] = np.ascontiguousarray(conv_all[:, c * NSB:(c + 1) * NSB].reshape(NL, NSB * 3, CONVD))
        in_maps.append(d)
    res = run_bass_kernel_spmd(nc, in_maps, core_ids=list(range(8)))
    R = res.results
    y_prompt = np.stack([R[c]["yp"] for c in range(4)]).astype(f)
    y_sample = np.concatenate([R[c]["ys"].reshape(NSB, 4, D) for c in range(8)], axis=0).astype(f)
    ssm_p = np.stack([R[c]["ossm_p"].reshape(NL, 64, 64, 128) for c in range(4)], axis=1).astype(f)
    conv_p = np.stack([R[c]["oconv_p"] for c in range(4)], axis=1).astype(f)
    ssm_s = np.concatenate([R[c]["ossm_s"].reshape(NL, NSB, 64, 64, 128) for c in range(8)], axis=1).astype(f)
    conv_s = np.concatenate([R[c]["oconv_s"].reshape(NL, NSB, 3, CONVD) for c in range(8)], axis=1).astype(f)
    v_s = np.concatenate([R[c]["ov_s"].reshape(NL, NSB, 4, D) for c in range(8)], axis=1).astype(f)
    return (y_prompt, y_sample, ssm_p, conv_p, ssm_s, conv_s, v_s)
```

```python
import numpy as np
from contextlib import ExitStack
import concourse.bass as bass
import concourse.mybir as mybir
from concourse.bass_utils import run_bass_kernel_spmd

F32 = mybir.dt.float32
BF16 = mybir.dt.bfloat16
AF = mybir.ActivationFunctionType
ALU = mybir.AluOpType

D = 2048
DEPTH = 4
DI = 4096
NH = 64
CONVD = 6144
FFN = 5632
INC = 18496
C_U, C_V, C_Z, C_X, C_DT, C_GA, C_GB = 0, 2048, 4096, 8192, 14336, 14400, 16448
EPS = 1e-6
SEM_EPOCH = 30000
NW = 7


class Buf:
    __slots__ = ("name", "w", "r", "alias")

    def __init__(self, name):
        self.name = name
        self.w = None
        self.r = {}
        self.alias = []


def alias(*bufs):
    for a in bufs:
        for b in bufs:
            if a is not b and b not in a.alias:
                a.alias.append(b)


class Prog:
    ENGS = ("pe", "act", "dve", "pool", "sp")

    def __init__(self, nc, stack):
        self.nc = nc
        self.stack = stack
        self.nsem = 0
        self.dma_total = {}
        self.q = {e: [] for e in self.ENGS}
        self.cnt = {e: 0 for e in self.ENGS}
        self.sem = {e: self.new_sem("c_" + e) for e in self.ENGS}
        self.waited = {e: {} for e in self.ENGS}

    def new_sem(self, name):
        self.nsem += 1
        return self.stack.enter_context(self.nc.semaphore("%s_%d" % (name, self.nsem)))

    def new_dma_sem(self, name):
        s = self.new_sem("d_" + name)
        self.dma_total[id(s)] = 0
        return s

    def op(self, e, fn, reads=(), writes=(), dma_sem=None):
        need = {}

        def add(dep):
            if dep is None:
                return
            sem, val, src = dep
            sid = id(sem)
            if sid in self.dma_total:
                val = self.dma_total[sid]
            elif src == e and e == "pe":
                return
            if need.get(sid, (None, 0))[1] < val:
                need[sid] = (sem, val)

        for b in reads:
            add(b.w)
        for b in writes:
            for bb in [b] + b.alias:
                add(bb.w)
                for r in bb.r.values():
                    add(r)
        waits = []
        wd = self.waited[e]
        for sid, (sem, val) in need.items():
            if wd.get(sid, 0) >= val:
                continue
            wd[sid] = val
            waits.append((sem, val))
        if fn is None:
            self.q[e].append((waits, None, None))
            return None
        if dma_sem is not None:
            sid = id(dma_sem)
            self.dma_total[sid] += 16
            comp = (dma_sem, self.dma_total[sid], "dma")
            inc = (dma_sem, 16)
        else:
            if self.cnt[e] >= SEM_EPOCH:
                self.sem[e] = self.new_sem("c_" + e)
                self.cnt[e] = 0
            self.cnt[e] += 1
            comp = (self.sem[e], self.cnt[e], e)
            inc = (self.sem[e], 1)
        for b in reads:
            old = b.r.get(id(comp[0]))
            if old is None or old[1] < comp[1]:
                b.r[id(comp[0])] = comp
        for b in writes:
            b.w = comp
            b.r = {}
        self.q[e].append((waits, fn, inc))
        return comp

    def emit(self):
        nc = self.nc
        with nc.Block() as block:
            def run(eng, lst):
                for waits, fn, inc in lst:
                    for sem, val in waits:
                        eng.wait_ge(sem, val)
                    if fn is None:
                        continue
                    ins = fn(eng)
                    ins.then_inc(inc[0], inc[1])

            @block.tensor
            def _(eng):
                run(eng, self.q["pe"])

            @block.scalar
            def _(eng):
                run(eng, self.q["act"])

            @block.vector
            def _(eng):
                run(eng, self.q["dve"])

            @block.gpsimd
            def _(eng):
                run(eng, self.q["pool"])

            @block.sync
            def _(eng):
                run(eng, self.q["sp"])


def build_program(TP=2048, NL=DEPTH, NSB=16):
    nc = bass.Bass("TRN2", target_bir_lowering=False)
    QS = NSB * 4
    NPC = TP // 128

    def din(name, shape, dt=F32):
        return nc.dram_tensor(name, list(shape), dt, kind="ExternalInput").ap()

    def dout(name, shape):
        return nc.dram_tensor(name, list(shape), F32, kind="ExternalOutput").ap()

    xp = din("xp", [TP, D]); xs = din("xs", [QS, D])
    sssm = din("sssm", [NL, NSB, 32, 128, 128]); sconv = din("sconv", [NL, NSB * 3, CONVD])
    w_in = din("w_in", [NL, D, INC]); w_oa = din("w_oa", [NL, D, D]); w_ob = din("w_ob", [NL, DI, D])
    w_o = din("w_o", [NL, D, D]); w_g = din("w_g", [NL, D, FFN]); w_u = din("w_u", [NL, D, FFN]); w_d = din("w_d", [NL, FFN, D])
    n1g = din("n1g", [NL, 128, 16]); n2g = din("n2g", [NL, 128, 16]); ssmg = din("ssmg", [NL, 128, 32])
    cwc = din("cwc", [NL, 128, 48 * 4]); cbc = din("cbc", [NL, 128, 48])
    lng_d = din("lng", [NL, 128, D]); lnb_d = din("lnb", [NL, 128, D])
    dtb_d = din("dtb", [NL, 128, NH]); alog_d = din("alog", [NL, 128, NH]); dsk_d = din("dsk", [NL, 128, NH])
    fng_d = din("fng", [128, D])
    wsp_d = din("wsp", [NL, 128, 16 * 128]); wss_d = din("wss", [NL, QS, 16 * QS])
    sbp_d = din("sbp", [NL, 1, 16 * 128]); sbs_d = din("sbs", [NL, 1, 16 * QS])
    cst = {}
    for nm, shp in (("ident", [128, 128]), ("trip", [128, 128]), ("m1p", [128, 128]), ("tris", [QS, QS]), ("m1s", [QS, QS]),
                    ("lh0", [128, 128]), ("lh1", [128, 128]), ("lastp", [128, 1]), ("lasts", [QS, NSB]), ("blks", [QS, NSB])):
        cst[nm] = din("c_" + nm, shp)
    def dscr(name, shape):
        return nc.dram_tensor(name, list(shape), BF16).ap()
    wb_in = [dscr("wb_in%d" % i, [D, INC]) for i in range(NL)]; wb_oa = [dscr("wb_oa%d" % i, [D, D]) for i in range(NL)]
    wb_ob = [dscr("wb_ob%d" % i, [DI, D]) for i in range(NL)]; wb_o = [dscr("wb_o%d" % i, [D, D]) for i in range(NL)]
    wb_g = [dscr("wb_g%d" % i, [D, FFN]) for i in range(NL)]; wb_u = [dscr("wb_u%d" % i, [D, FFN]) for i in range(NL)]
    wb_d = [dscr("wb_d%d" % i, [FFN, D]) for i in range(NL)]
    yp = dout("yp", [TP, D]); ys = dout("ys", [QS, D])
    ossm_p = dout("ossm_p", [NL, 32, 128, 128]); oconv_p = dout("oconv_p", [NL, 3, CONVD])
    ossm_s = dout("ossm_s", [NL, NSB, 32, 128, 128]); oconv_s = dout("oconv_s", [NL, NSB * 3, CONVD])
    ov_s = dout("ov_s", [NL, QS, D])

    with ExitStack() as st:
        P = Prog(nc, st)

        def sb(name, shape, dt):
            return st.enter_context(nc.sbuf_tensor(name, list(shape), dt))

        xt2 = [sb("xt0", [128, D], F32), sb("xt1", [128, D], F32)]; b_xt2 = [Buf("xt0"), Buf("xt1")]
        hn = sb("hn", [128, D], BF16); b_hn = Buf("hn")
        hT = sb("hT", [128, 16, 128], BF16); b_hT = Buf("hT")
        hT1 = sb("hT1", [128, 16, 128], BF16); b_hT1 = Buf("hT1")
        hT2 = [hT, hT1]; b_hT2 = [b_hT, b_hT1]
        wring = [sb("wr%d" % i, [128, 1024], BF16) for i in range(NW)]
        b_wr = [Buf("wr%d" % i) for i in range(NW)]
        s_wr = [P.new_dma_sem("wr%d" % i) for i in range(NW)]
        s_wbk = [P.new_dma_sem("wbk%d" % i) for i in range(NW)]
        b_scr = Buf("scr")
        uT = sb("uT", [128, 16, 128], BF16); b_uT = Buf("uT")
        A1 = sb("A1", [128, 4096], F32)
        b_v = Buf("v"); b_vn = Buf("vn"); b_y = Buf("y"); b_sg = Buf("sgate")
        alias(b_v, b_vn, b_y, b_sg)
        v_ap = A1[:, 0:2048]; vn_ap = A1[:, 2048:4096]; y_ap = A1
        sg_ap = A1[:, 0:FFN // 2].bitcast(BF16)
        A2 = sb("A2", [128, 4096], F32)
        b_xdt = Buf("xdt"); b_xdtd = Buf("xdtd"); b_hid = Buf("hid"); b_ctok = Buf("ctok")
        alias(b_xdt, b_xdtd, b_hid, b_ctok)
        xdt_ap = A2[:, 0:2048].bitcast(BF16); xdtd_ap = A2[:, 2048:4096].bitcast(BF16)
        hid_ap = A2[:, 0:FFN // 2].bitcast(BF16)
        ctok_ap = A2[:, 0:1024]
        A3 = sb("A3", [128, 48 * 128], BF16)
        b_cx = Buf("cxT"); b_hidT = Buf("hidT"); alias(b_cx, b_hidT)
        cxT = A3[:].rearrange("p (j t) -> p j t", t=128)
        hidT = A3[:, 0:44 * 128].rearrange("p (j t) -> p j t", t=128)
        A4 = sb("A4", [128, 48 * 131], F32)
        b_xbc = Buf("xbcT"); b_mg = Buf("merged"); b_ybb = Buf("ybb"); b_ybT = Buf("ybT")
        alias(b_xbc, b_mg, b_ybb, b_ybT)
        b_sg1 = Buf("sgate1"); b_hid1 = Buf("hid1")
        alias(b_xbc, b_mg, b_ybb, b_ybT, b_sg1, b_hid1)
        sg1_ap = A4[:, 0:FFN // 2].bitcast(BF16); hid1_ap = A4[:, FFN // 2:FFN].bitcast(BF16)
        mg_ap = A4[:, 0:2048]
        ybb_ap = A4[:, 2048:4096].bitcast(BF16)
        ybT = A4[:, 4096:6144].bitcast(BF16).rearrange("p (j t) -> p j t", t=128)
        A5 = sb("A5", [128, 4096], F32)
        b_lng = Buf("lng"); b_lnb = Buf("lnb"); b_xst = Buf("xs_tok"); b_stT = Buf("stT"); b_fng = Buf("fng")
        alias(b_lng, b_lnb, b_xst, b_stT, b_fng)
        b_hidT1 = Buf("hidT1"); alias(b_lng, b_lnb, b_xst, b_stT, b_fng, b_hidT1)
        hidT1 = A5[:, 0:44 * 64].bitcast(BF16).rearrange("p (j t) -> p j t", t=128)
        lng_ap = A5[:, 0:2048]; lnb_ap = A5[:, 2048:4096]; fng_ap = A5[:, 0:2048]
        xst_ap = A5[:, 0:2048].bitcast(BF16); stT_ap = A5[:, 2048:4096].bitcast(BF16)
        s_ln = P.new_dma_sem("ln")
        sz = sb("sz", [128, DI], BF16); b_sz = Buf("sz")
        Btok = sb("Btok", [128, 1024], BF16); b_Btok = Buf("Btok")
        sga = sb("sga", [128, D], BF16); b_sga = Buf("sga")
        sgb = sb("sgb", [128, D], BF16); b_sgb = Buf("sgb")
        St = sb("St", [128, 32, 128], F32); b_St = Buf("St"); s_St = P.new_dma_sem("St")
        Rf = sb("Rf", [128, 1024], F32); b_R = Buf("R")
        Ef = sb("Ef", [128, 1024], BF16); b_E = Buf("E")
        cbm = sb("cbm", [128, 128], BF16); b_cbm = Buf("cbm")
        cbLf = sb("cbLf", [128, 1024], BF16); b_cbL = Buf("cbL")
        tmpg = sb("tmpg", [128, 512], F32); b_tmpg = Buf("tmpg")
        tmp2 = sb("tmp2", [128, 512], F32); b_tmp2 = Buf("tmp2")
        gel = [sb("gel%d" % i, [128, 512], F32) for i in range(2)]; b_gel = [Buf("gel0"), Buf("gel1")]
        Bm = sb("Bm", [128, 1024], BF16); b_Bm = Buf("Bm")
        dcol = sb("dcol", [128, 32 * 16], F32); b_dcol = Buf("dcol")
        rh0 = sb("rh0", [128, 32 * 16], F32); b_rh0 = Buf("rh0")
        rh1 = sb("rh1", [128, 32 * 16], F32); b_rh1 = Buf("rh1")
        small = sb("small", [128, 8 * 64], F32); b_small = Buf("small")
        dtt = small[:, 0:64]; dat = small[:, 64:128]; cst_ = small[:, 128:192]; dte = small[:, 192:256]; ecs = small[:, 256:320]
        stat = sb("stat", [128, 32], F32); b_stat = Buf("stat")
        wsl = A3[:, 0:16 * 128]; b_wsl = Buf("wsl"); s_wsl = P.new_dma_sem("wsl"); alias(b_cx, b_hidT, b_wsl)
        wsm = sb("wsm", [128, 16, 128], BF16); b_wsm = Buf("wsm")
        sbrow = sb("sbrow", [1, 16 * 128], BF16); b_sbrow = Buf("sbrow"); s_sbrow = P.new_dma_sem("sbrow")
        cw_t = sb("cw_t", [128, 48, 4], F32); cb_t = sb("cb_t", [128, 48], F32)
        n1g_t = sb("n1g_t", [128, 16], F32); n2g_t = sb("n2g_t", [128, 16], F32); ssmg_t = sb("ssmg_t", [128, 32], F32)
        dtb_t = sb("dtb_t", [128, NH], F32); a_t = sb("a_t", [128, NH], F32); dsk_t = sb("dsk_t", [128, NH], F32)
        b_par = Buf("par"); s_par = P.new_dma_sem("par")
        halo = sb("halo", [128, 48, 3], F32); b_halo = Buf("halo")
        acc = [sb("acc%d" % i, [128, 128], F32) for i in range(4)]; b_acc = [Buf("acc%d" % i) for i in range(4)]
        hs_t = A2[:, 1024:1024 + CONVD // 4]
        b_hs = Buf("hs"); s_hs = P.new_dma_sem("hs"); alias(b_xdt, b_xdtd, b_hid, b_ctok, b_hs)
        cT = {}
        for nm in cst:
            shp = list(cst[nm].shape)
            cT[nm] = sb("k_" + nm, shp, F32)
        identb = sb("identb", [128, 128], BF16); trib_p = sb("trib_p", [128, 128], BF16); trib_s = sb("trib_s", [QS, QS], BF16)
        onesb = sb("onesb", [1, 128], BF16)
        b_cst = Buf("cst"); s_cst = P.new_dma_sem("cst")
        s_x = P.new_dma_sem("x"); s_out = P.new_dma_sem("out")

        pbank = [st.enter_context(nc.psum_tensor("pb%d" % i, [128, 512], F32)) for i in range(8)]
        b_pb = [Buf("pb%d" % i) for i in range(8)]

        def dma(eng, out, in_, sem, reads=(), writes=()):
            P.op(eng, lambda e: e.dma_start(out=out, in_=in_), reads=reads, writes=writes, dma_sem=sem)

        def mm(out, lhsT, rhs, start, stop, reads, writes):
            P.op("pe", lambda e: e.matmul(out, lhsT=lhsT, rhs=rhs, start=start, stop=stop), reads=reads, writes=writes)

        def tr(out, in_, ident, reads, writes):
            P.op("pe", lambda e: e.transpose(out=out, in_=in_, identity=ident), reads=list(reads) + [b_cst], writes=writes)

        def act(out, in_, func, reads, writes, bias=None, scale=None, accum=None):
            kw = {}
            if bias is not None:
                kw["bias"] = bias
            if scale is not None:
                kw["scale"] = scale
            if accum is not None:
                kw["accum_out"] = accum
            P.op("act", lambda e: e.activation(out=out, in_=in_, func=func, **kw), reads=reads, writes=writes)

        def tt(out, in0, in1, op, reads, writes, eng="dve"):
            P.op(eng, lambda e: e.tensor_tensor(out=out, in0=in0, in1=in1, op=op), reads=reads, writes=writes)

        def ts(out, in0, s1, s2, op0, op1, reads, writes, eng="dve"):
            if op1 is None:
                P.op(eng, lambda e: e.tensor_scalar(out=out, in0=in0, scalar1=s1, scalar2=None, op0=op0), reads=reads, writes=writes)
            else:
                P.op(eng, lambda e: e.tensor_scalar(out=out, in0=in0, scalar1=s1, scalar2=s2, op0=op0, op1=op1), reads=reads, writes=writes)

        def stt(out, in0, scalar, in1, op0, op1, reads, writes, eng="dve"):
            P.op(eng, lambda e: e.scalar_tensor_tensor(out=out, in0=in0, scalar=scalar, in1=in1, op0=op0, op1=op1), reads=reads, writes=writes)

        def cp(out, in_, reads, writes, eng="dve"):
            P.op(eng, lambda e: e.tensor_copy(out=out, in_=in_), reads=reads, writes=writes)

        ring = {"slot": 0, "grp": 0, "trb": 0}

        def linear(lhs_fn, nk, W, col0, ncols, mode, consumer, q, act_reads, Wb=None, first=True, jobs=None):
            for g0 in range(col0, col0 + ncols, 1024):
                gw = min(1024, col0 + ncols - g0)
                if jobs is not None:
                    jobs.append(lambda g0=g0, gw=gw: linear_group(lhs_fn, nk, W, g0, gw, mode, consumer, q, act_reads, Wb, first))
                else:
                    linear_group(lhs_fn, nk, W, g0, gw, mode, consumer, q, act_reads, Wb, first)

        def linear_group(lhs_fn, nk, W, g0, gw, mode, consumer, q, act_reads, Wb, first):
            if True:
                gi = ring["grp"]; ring["grp"] ^= 1
                banks = (2 * gi, 2 * gi + 1)
                for k in range(nk):
                    s = ring["slot"]; ring["slot"] = (s + 1) % NW
                    if first:
                        dma("pool", wring[s][:, 0:gw], W[k * 128:(k + 1) * 128, g0:g0 + gw], s_wr[s], writes=[b_wr[s]])
                    else:
                        dma("sp", wring[s][:, 0:gw], Wb[k * 128:(k + 1) * 128, g0:g0 + gw], s_wr[s], writes=[b_wr[s]])
                    if mode == "tok":
                        for n in range((gw + 511) // 512):
                            w = min(512, gw - n * 512)
                            mm(pbank[banks[n]][0:q, 0:w], lhs_fn(k), wring[s][:, n * 512:n * 512 + w], k == 0, k == nk - 1,
                               reads=[b_wr[s]] + act_reads, writes=[b_pb[banks[n]]])
                    else:
                        for jj in range(gw // 128):
                            n, r = divmod(jj, 4)
                            mm(pbank[banks[n]][:, r * 128:r * 128 + q], wring[s][:, jj * 128:(jj + 1) * 128], lhs_fn(k), (k == 0 and r == 0), k == nk - 1,
                               reads=[b_wr[s]] + act_reads, writes=[b_pb[banks[n]]])
                    if first and Wb is not None:
                        dma("sp", Wb[k * 128:(k + 1) * 128, g0:g0 + gw], wring[s][:, 0:gw], s_wbk[s], reads=[b_wr[s], b_scr])
                for n in range((gw + 511) // 512):
                    w = min(512, gw - n * 512)
                    consumer(banks[n], g0 + n * 512, w)

        def transpose_to(dst3, dst_buf, src, src_buf, ntile, q, scale_col=None, dt=BF16):
            for t0 in range(0, ntile, 4):
                nt = min(4, ntile - t0)
                bi = 4 + ring["trb"]; ring["trb"] ^= 1
                pv = pbank[bi][:].bitcast(BF16) if dt == BF16 else pbank[bi][:]
                for i in range(nt):
                    tr(pv[:, i * 128:i * 128 + q], src[0:q, (t0 + i) * 128:(t0 + i + 1) * 128],
                       (identb if dt == BF16 else cT["ident"])[0:q, 0:q], reads=[src_buf], writes=[b_pb[bi]])
                if scale_col is None:
                    if q == 128:
                        cp(dst3[:, t0:t0 + nt, :], pv[:, 0:nt * 128].rearrange("p (a b) -> p a b", b=128), reads=[b_pb[bi]], writes=[dst_buf])
                    else:
                        cp(dst3[:, t0:t0 + nt, 0:q], pv[:, 0:nt * 128].rearrange("p (a b) -> p a b", b=128)[:, :, 0:q], reads=[b_pb[bi]], writes=[dst_buf])
                else:
                    for i in range(nt):
                        act(dst3[:, t0 + i, 0:q], pv[:, i * 128:i * 128 + q], AF.Identity, reads=[b_pb[bi], b_par], writes=[dst_buf],
                            scale=scale_col[:, t0 + i:t0 + i + 1])

        def rmsnorm_to_T(q, gcol, xt, b_xt, hT, b_hT):
            act(hn[0:q, :], xt[0:q, :], AF.Square, reads=[b_xt], writes=[b_hn, b_stat], accum=stat[0:q, 0:1])
            P.op("dve", None, reads=[b_hn])
            ts(stat[0:q, 1:2], stat[0:q, 0:1], 1.0 / D, EPS, ALU.mult, ALU.add, reads=[b_hn, b_stat], writes=[b_stat])
            act(stat[0:q, 2:3], stat[0:q, 1:2], AF.Sqrt, reads=[b_stat], writes=[b_stat])
            P.op("dve", lambda e: e.reciprocal(out=stat[0:q, 3:4], in_=stat[0:q, 2:3]), reads=[b_stat], writes=[b_stat])
            act(hn[0:q, :], xt[0:q, :], AF.Identity, reads=[b_xt, b_stat], writes=[b_hn], scale=stat[0:q, 3:4])
            transpose_to(hT, b_hT, hn, b_hn, 16, q, scale_col=gcol)

        def gelu_bank(bi, out_ap, out_buf, shape_fn, extra_writes=()):
            gi = bi % 2
            src = shape_fn(pbank[bi])
            t = shape_fn(gel[gi])
            act(t, src, AF.Square, reads=[b_pb[bi]], writes=[b_gel[gi]])
            ts(t, t, 0.044715, 1.0, ALU.mult, ALU.add, reads=[b_gel[gi]], writes=[b_gel[gi]])
            tt(t, t, src, ALU.mult, reads=[b_gel[gi], b_pb[bi]], writes=[b_gel[gi]])
            act(t, t, AF.Sigmoid, reads=[b_gel[gi]], writes=[b_gel[gi]], scale=1.5957691216057308)
            tt(out_ap, t, src, ALU.mult, reads=[b_gel[gi], b_pb[bi]], writes=[out_buf] + list(extra_writes))

        for nm in cst:
            dma("sp", cT[nm][:], cst[nm], s_cst, writes=[b_cst])
        cp(identb[:], cT["ident"][:], reads=[b_cst], writes=[b_cst])
        cp(trib_p[:], cT["trip"][:], reads=[b_cst], writes=[b_cst])
        cp(trib_s[:], cT["tris"][:], reads=[b_cst], writes=[b_cst])
        P.op("dve", lambda e: e.memset(onesb[:], 1.0), writes=[b_cst])

        def mixer(l, kind, ci, slot):
            xt = xt2[slot]; b_xt = b_xt2[slot]
            prompt = kind == "p"
            q = 128 if prompt else QS
            nb = 1 if prompt else NSB
            qb = q // nb
            last_layer = l == NL - 1
            first = prompt and ci == 0
            if (prompt and ci in (1, 2)) or (not prompt):
                P.op("sp", None, writes=[b_scr])
            TRI = cT["trip"] if prompt else cT["tris"]
            M1 = cT["m1p"] if prompt else cT["m1s"]
            TRIB = trib_p if prompt else trib_s
            if l == 0:
                src = xp[ci * 128:(ci + 1) * 128, :] if prompt else xs[:, :]
            else:
                src = yp[ci * 128:(ci + 1) * 128, :] if prompt else ys[:, :]
            xdst = yp[ci * 128:(ci + 1) * 128, :] if prompt else ys[:, :]
            dma("sp", xt[0:q, :], src, s_x, writes=[b_xt])
            dma("sp", lng_ap, lng_d[l], s_ln, writes=[b_lng])
            dma("sp", lnb_ap, lnb_d[l], s_ln, writes=[b_lnb])
            rmsnorm_to_T(q, n1g_t, xt, b_xt, hT, b_hT)
            hfn = lambda k: hT[:, k, 0:q]
            W = w_in[l]

            def cons_u(bi, c0, w):
                j0 = (c0 - C_U) // 128
                gelu_bank(bi, uT[:, j0:j0 + 4, 0:q], b_uT,
                          lambda t: t[:, :].rearrange("p (a b) -> p a b", b=128)[:, :, 0:q])
            linear(hfn, 16, W, C_U, 2048, "feat", cons_u, q, [b_hT], Wb=wb_in[l], first=first)

            def cons_v(bi, c0, w):
                gelu_bank(bi, v_ap[0:q, c0 - C_V:c0 - C_V + w], b_v, lambda t: t[0:q, 0:w])
            linear(hfn, 16, W, C_V, 2048, "tok", cons_v, q, [b_hT], Wb=wb_in[l], first=first)
            act(vn_ap[0:q, :], v_ap[0:q, :], AF.Copy, reads=[b_v], writes=[b_vn, b_stat], accum=stat[0:q, 4:5])
            P.op("dve", None, reads=[b_vn])
            ts(stat[0:q, 5:6], stat[0:q, 4:5], -1.0 / D, None, ALU.mult, None, reads=[b_vn, b_stat], writes=[b_stat])
            act(vn_ap[0:q, :], v_ap[0:q, :], AF.Square, reads=[b_v, b_stat], writes=[b_vn, b_stat], bias=stat[0:q, 5:6], accum=stat[0:q, 6:7])
            P.op("dve", None, reads=[b_vn])
            ts(stat[0:q, 7:8], stat[0:q, 6:7], 1.0 / D, EPS, ALU.mult, ALU.add, reads=[b_vn, b_stat], writes=[b_stat])
            act(stat[0:q, 8:9], stat[0:q, 7:8], AF.Sqrt, reads=[b_stat], writes=[b_stat])
            P.op("dve", lambda e: e.reciprocal(out=stat[0:q, 9:10], in_=stat[0:q, 8:9]), reads=[b_stat], writes=[b_stat])
            tt(stat[0:q, 10:11], stat[0:q, 5:6], stat[0:q, 9:10], ALU.mult, reads=[b_stat], writes=[b_stat])
            act(vn_ap[0:q, :], v_ap[0:q, :], AF.Identity, reads=[b_v, b_stat], writes=[b_vn], bias=stat[0:q, 10:11], scale=stat[0:q, 9:10])
            tt(vn_ap[0:q, :], vn_ap[0:q, :], lng_ap[0:q, :], ALU.mult, reads=[b_vn, b_lng], writes=[b_vn])
            tt(vn_ap[0:q, :], vn_ap[0:q, :], lnb_ap[0:q, :], ALU.add, reads=[b_vn, b_lnb], writes=[b_vn])
            cp(hn[0:q, :], vn_ap[0:q, :], reads=[b_vn], writes=[b_hn])
            if not prompt:
                dma("sp", ov_s[l], vn_ap[0:q, :], s_out, reads=[b_vn])
            for g in range(16):
                bi = 4 + (g % 2)
                mm(pbank[bi][:, 0:q], hn[0:q, g * 128:(g + 1) * 128], wsm[0:q, g, 0:q], True, False, reads=[b_hn, b_wsm], writes=[b_pb[bi]])
                mm(pbank[bi][:, 0:q], onesb[0:1, :], sbrow[0:1, g * q:(g + 1) * q], False, True, reads=[b_sbrow, b_cst], writes=[b_pb[bi]])
                tt(uT[:, g, 0:q], uT[:, g, 0:q], pbank[bi][:, 0:q], ALU.mult, reads=[b_uT, b_pb[bi]], writes=[b_uT])

            def cons_z(bi, c0, w):
                act(sz[0:q, c0 - C_Z:c0 - C_Z + w], pbank[bi][0:q, 0:w], AF.Silu, reads=[b_pb[bi]], writes=[b_sz])
            jobs = []
            linear(hfn, 16, W, C_Z, DI, "tok", cons_z, q, [b_hT], Wb=wb_in[l], first=first, jobs=jobs)

            if prompt:
                xb3 = A4[:, :].rearrange("p (j t) -> p j t", t=131)
                newv = lambda j0, nj: xb3[:, j0:j0 + nj, 3:131]
                if ci == 0:
                    P.op("dve", lambda e: e.memset(halo[:], 0.0), writes=[b_halo])
                cp(xb3[:, :, 0:3], halo[:], reads=[b_halo], writes=[b_xbc])
            else:
                xb4 = A4[:, 0:48 * NSB * 7].rearrange("p (j b t) -> p j b t", b=NSB, t=7)
                newv = lambda j0, nj: xb4[:, j0:j0 + nj, :, 3:7]
                for qq in range(4):
                    dma("sp", hs_t[0:NSB * 3, :], sconv[l][:, qq * 1536:(qq + 1) * 1536], s_hs, writes=[b_hs])
                    for jj in range(12):
                        j = qq * 12 + jj
                        bi = 4 + (j % 2)
                        tr(pbank[bi][:, 0:NSB * 3], hs_t[0:NSB * 3, jj * 128:(jj + 1) * 128], cT["ident"][0:NSB * 3, 0:NSB * 3], reads=[b_hs], writes=[b_pb[bi]])
                        cp(xb4[:, j, :, 0:3], pbank[bi][:, 0:NSB * 3].rearrange("p (b r) -> p b r", r=3), reads=[b_pb[bi]], writes=[b_xbc])

            def cons_x(bi, c0, w):
                j0 = (c0 - C_X) // 128
                if prompt:
                    cp(newv(j0, 4), pbank[bi][:, :].rearrange("p (a b) -> p a b", b=128), reads=[b_pb[bi]], writes=[b_xbc])
                else:
                    cp(newv(j0, 4), pbank[bi][:, :].rearrange("p (a b) -> p a b", b=128)[:, :, 0:q].rearrange("p a (b t) -> p a b t", t=4),
                       reads=[b_pb[bi]], writes=[b_xbc])
            linear(hfn, 16, W, C_X, CONVD, "feat", cons_x, q, [b_hT], Wb=wb_in[l], first=first)

            def cons_dt(bi, c0, w):
                tt(dtt[0:q, :], pbank[bi][0:q, 0:64], dtb_t[0:q, :], ALU.add, reads=[b_pb[bi], b_par], writes=[b_small])
                act(dtt[0:q, :], dtt[0:q, :], AF.Exp, reads=[b_small], writes=[b_small])
                ts(dtt[0:q, :], dtt[0:q, :], 1.0, None, ALU.add, None, reads=[b_small], writes=[b_small])
                act(dtt[0:q, :], dtt[0:q, :], AF.Ln, reads=[b_small], writes=[b_small])
                tt(dat[0:q, :], dtt[0:q, :], a_t[0:q, :], ALU.mult, reads=[b_small, b_par], writes=[b_small])
            linear(hfn, 16, W, C_DT, 64, "tok", cons_dt, q, [b_hT], Wb=wb_in[l], first=first)

            def cons_ga(bi, c0, w):
                act(sga[0:q, c0 - C_GA:c0 - C_GA + w], pbank[bi][0:q, 0:w], AF.Sigmoid, reads=[b_pb[bi]], writes=[b_sga])

            def cons_gb(bi, c0, w):
                act(sgb[0:q, c0 - C_GB:c0 - C_GB + w], pbank[bi][0:q, 0:w], AF.Sigmoid, reads=[b_pb[bi]], writes=[b_sgb])
            linear(hfn, 16, W, C_GA, 2048, "tok", cons_ga, q, [b_hT], Wb=wb_in[l], first=first, jobs=jobs)
            linear(hfn, 16, W, C_GB, 2048, "tok", cons_gb, q, [b_hT], Wb=wb_in[l], first=first, jobs=jobs)

            if (prompt and ci == NPC - 1) or not prompt:
                dstv = None if prompt else oconv_s[l].rearrange("(b r) c -> r b c", r=3)
                for bt in range(6):
                    for r in range(1 if prompt else 3):
                        for half in range(2):
                            bi = 4 + half
                            for i in range(4):
                                j = bt * 8 + half * 4 + i
                                if prompt:
                                    tr(pbank[bi][0:q, i * 128:(i + 1) * 128], xb3[:, j, 3:131], cT["ident"][:, :], reads=[b_xbc], writes=[b_pb[bi]])
                                else:
                                    tr(pbank[bi][0:NSB, i * 128:(i + 1) * 128], xb4[:, j, :, 4 + r], cT["ident"][:, :], reads=[b_xbc], writes=[b_pb[bi]])
                            nr = q if prompt else NSB
                            cp(ctok_ap[0:nr, half * 512:(half + 1) * 512], pbank[bi][0:nr, :], reads=[b_pb[bi]], writes=[b_ctok])
                        if prompt:
                            dma("sp", oconv_p[l][:, bt * 1024:(bt + 1) * 1024], ctok_ap[125:128, :], s_out, reads=[b_ctok])
                        else:
                            dma("sp", dstv[r][:, bt * 1024:(bt + 1) * 1024], ctok_ap[0:NSB, :], s_out, reads=[b_ctok])
            if prompt:
                cp(halo[:], xb3[:, :, 128:131], reads=[b_xbc], writes=[b_halo])

            for j in range(48):
                a = acc[j % 4]; ba = b_acc[j % 4]
                if prompt:
                    av = a[:, 0:q]
                    xk = lambda k: xb3[:, j, k:k + q]
                else:
                    av = a[:, 0:q].rearrange("p (b t) -> p b t", t=4)
                    xk = lambda k: xb4[:, j, :, k:k + 4]
                ts(av, xk(0), cw_t[:, j, 0:1], cb_t[:, j:j + 1], ALU.mult, ALU.add, reads=[b_xbc, b_par], writes=[ba])
                for k in range(1, 4):
                    stt(av, xk(k), cw_t[:, j, k:k + 1], av, ALU.mult, ALU.add, reads=[b_xbc, b_par, ba], writes=[ba])
                act(cxT[:, j, 0:q], a[:, 0:q], AF.Silu, reads=[ba], writes=[b_cx])
            transpose_to_tok(q)

            ssd(l, kind, ci, q, nb, qb, TRI, M1, TRIB, jobs)
            for jb in jobs:
                jb()

            tt(y_ap[0:q, :], y_ap[0:q, :], sz[0:q, :], ALU.mult, reads=[b_y, b_sz], writes=[b_y])
            for g in range(8):
                act(tmpg[0:q, :], y_ap[0:q, g * 512:(g + 1) * 512], AF.Square, reads=[b_y], writes=[b_tmpg, b_stat], accum=stat[0:q, 12 + g:13 + g])
            P.op("dve", None, reads=[b_tmpg])
            ts(stat[0:q, 20:28], stat[0:q, 12:20], 1.0 / 512, EPS, ALU.mult, ALU.add, reads=[b_tmpg, b_stat], writes=[b_stat])
            act(stat[0:q, 20:28], stat[0:q, 20:28], AF.Sqrt, reads=[b_stat], writes=[b_stat])
            P.op("dve", lambda e: e.reciprocal(out=stat[0:q, 20:28], in_=stat[0:q, 20:28]), reads=[b_stat], writes=[b_stat])
            tt(ybb_ap[0:q, :].rearrange("p (g c) -> p g c", c=512), y_ap[0:q, :].rearrange("p (g c) -> p g c", c=512),
               stat[0:q, 20:28].unsqueeze(2).to_broadcast([q, 8, 512]), ALU.mult, reads=[b_y, b_stat], writes=[b_ybb])
            transpose_to(ybT, b_ybT, ybb_ap, b_ybb, 32, q, scale_col=ssmg_t)

            def cons_pa(bi, c0, w):
                tt(mg_ap[0:q, c0:c0 + w], pbank[bi][0:q, 0:w], sga[0:q, c0:c0 + w], ALU.mult, reads=[b_pb[bi], b_sga], writes=[b_mg])
            linear(lambda k: uT[:, k, 0:q], 16, w_oa[l], 0, D, "tok", cons_pa, q, [b_uT], Wb=wb_oa[l], first=first)

            def cons_pb(bi, c0, w):
                gi = bi % 2
                tt(gel[gi][0:q, 0:w], pbank[bi][0:q, 0:w], sgb[0:q, c0:c0 + w], ALU.mult, reads=[b_pb[bi], b_sgb], writes=[b_gel[gi]])
                tt(mg_ap[0:q, c0:c0 + w], mg_ap[0:q, c0:c0 + w], gel[gi][0:q, 0:w], ALU.add, reads=[b_mg, b_gel[gi]], writes=[b_mg])
            linear(lambda k: ybT[:, k, 0:q], 32, w_ob[l], 0, D, "tok", cons_pb, q, [b_ybT], Wb=wb_ob[l], first=first)
            cp(hn[0:q, :], mg_ap[0:q, :], reads=[b_mg], writes=[b_hn])
            transpose_to(hT, b_hT, hn, b_hn, 16, q)

            def cons_res(bi, c0, w):
                tt(xt[0:q, c0:c0 + w], xt[0:q, c0:c0 + w], pbank[bi][0:q, 0:w], ALU.add, reads=[b_pb[bi], b_xt], writes=[b_xt])
            linear(hfn, 16, w_o[l], 0, D, "tok", cons_res, q, [b_hT], Wb=wb_o[l], first=first)


        def linear_multi(entries, nk, W, ncols, Wb, first):
            for g0 in range(0, ncols, 1024):
                gw = min(1024, ncols - g0)
                gi = ring["grp"]; ring["grp"] ^= 1
                for k in range(nk):
                    s = ring["slot"]; ring["slot"] = (s + 1) % NW
                    if first:
                        dma("pool", wring[s][:, 0:gw], W[k * 128:(k + 1) * 128, g0:g0 + gw], s_wr[s], writes=[b_wr[s]])
                    else:
                        dma("sp", wring[s][:, 0:gw], Wb[k * 128:(k + 1) * 128, g0:g0 + gw], s_wr[s], writes=[b_wr[s]])
                    for i, (lhs_fn, q, rds, consumer) in enumerate(entries):
                        for n in range((gw + 511) // 512):
                            w = min(512, gw - n * 512)
                            bk = 2 * i + 4 * gi + n
                            mm(pbank[bk][0:q, 0:w], lhs_fn(k), wring[s][:, n * 512:n * 512 + w], k == 0, k == nk - 1,
                               reads=[b_wr[s]] + rds, writes=[b_pb[bk]])
                    if first:
                        dma("sp", Wb[k * 128:(k + 1) * 128, g0:g0 + gw], wring[s][:, 0:gw], s_wbk[s], reads=[b_wr[s], b_scr])
                for i, (lhs_fn, q, rds, consumer) in enumerate(entries):
                    for n in range((gw + 511) // 512):
                        w = min(512, gw - n * 512)
                        consumer(2 * i + 4 * gi + n, g0 + n * 512, w)

        def ffn(l, items):
            last_layer = l == NL - 1
            first = any(kind == "p" and ci == 0 for kind, ci, slot in items)
            SG = [sg_ap, sg1_ap]; bSG = [b_sg, b_sg1]; HID = [hid_ap, hid1_ap]; bHID = [b_hid, b_hid1]
            HIDT = [hidT, hidT1]; bHIDT = [b_hidT, b_hidT1]
            qs = [128 if kind == "p" else QS for kind, ci, slot in items]
            for i, (kind, ci, slot) in enumerate(items):
                rmsnorm_to_T(qs[i], n2g_t, xt2[slot], b_xt2[slot], hT2[i], b_hT2[i])

            def mk_g(i):
                q = qs[i]
                return lambda bi, c0, w: act(SG[i][0:q, c0:c0 + w], pbank[bi][0:q, 0:w], AF.Silu, reads=[b_pb[bi]], writes=[bSG[i]])

            def mk_u(i):
                q = qs[i]
                return lambda bi, c0, w: tt(HID[i][0:q, c0:c0 + w], pbank[bi][0:q, 0:w], SG[i][0:q, c0:c0 + w], ALU.mult,
                                            reads=[b_pb[bi], bSG[i]], writes=[bHID[i]])

            def mk_r(i):
                q = qs[i]; slot = items[i][2]
                return lambda bi, c0, w: tt(xt2[slot][0:q, c0:c0 + w], xt2[slot][0:q, c0:c0 + w], pbank[bi][0:q, 0:w], ALU.add,
                                            reads=[b_pb[bi], b_xt2[slot]], writes=[b_xt2[slot]])
            ent = lambda mk: [((lambda k, i=i: hT2[i][:, k, 0:qs[i]]), qs[i], [b_hT2[i]], mk(i)) for i in range(len(items))]
            linear_multi(ent(mk_g), 16, w_g[l], FFN, wb_g[l], first)
            linear_multi(ent(mk_u), 16, w_u[l], FFN, wb_u[l], first)
            for i in range(len(items)):
                transpose_to(HIDT[i], bHIDT[i], HID[i], bHID[i], 44, qs[i])
            entd = [((lambda k, i=i: HIDT[i][:, k, 0:qs[i]]), qs[i], [bHIDT[i]], mk_r(i)) for i in range(len(items))]
            linear_multi(entd, 44, w_d[l], D, wb_d[l], first)
            for i, (kind, ci, slot) in enumerate(items):
                q = qs[i]; xt = xt2[slot]; b_xt = b_xt2[slot]
                xdst = yp[ci * 128:(ci + 1) * 128, :] if kind == "p" else ys[:, :]
                if last_layer:
                    dma("sp", fng_ap, fng_d, s_ln, writes=[b_fng])
                    act(hn[0:q, :], xt[0:q, :], AF.Square, reads=[b_xt], writes=[b_hn, b_stat], accum=stat[0:q, 0:1])
                    ts(stat[0:q, 1:2], stat[0:q, 0:1], 1.0 / D, EPS, ALU.mult, ALU.add, reads=[b_hn, b_stat], writes=[b_stat])
                    act(stat[0:q, 2:3], stat[0:q, 1:2], AF.Sqrt, reads=[b_stat], writes=[b_stat])
                    P.op("dve", lambda e, q=q: e.reciprocal(out=stat[0:q, 3:4], in_=stat[0:q, 2:3]), reads=[b_stat], writes=[b_stat])
                    stt(xt[0:q, :], xt[0:q, :], stat[0:q, 3:4], fng_ap[0:q, :], ALU.mult, ALU.mult, reads=[b_xt, b_stat, b_fng], writes=[b_xt])
                dma("sp", xdst, xt[0:q, :], s_x, reads=[b_xt])

        def transpose_to_tok(q):
            for t0 in range(0, 40, 4):
                bi = 4 + ring["trb"]; ring["trb"] ^= 1
                pv = pbank[bi][:].bitcast(BF16)
                for i in range(4):
                    tr(pv[0:q, i * 128:(i + 1) * 128], cxT[:, t0 + i, 0:q], identb[:, :], reads=[b_cx], writes=[b_pb[bi]])
                if t0 < 32:
                    cp(xst_ap[0:q, t0 * 128:(t0 + 4) * 128], pv[0:q, 0:512], reads=[b_pb[bi]], writes=[b_xst])
                else:
                    cp(Btok[0:q, (t0 - 32) * 128:(t0 - 28) * 128], pv[0:q, 0:512], reads=[b_pb[bi]], writes=[b_Btok])

        def ssd(l, kind, ci, q, nb, qb, TRI, M1, TRIB, jobs):
            prompt = kind == "p"
            BT = lambda g: cxT[:, 32 + g, 0:q]
            CT = lambda g: cxT[:, 40 + g, 0:q]
            mm(pbank[5][0:q, 0:64], TRI[0:q, 0:q], dat[0:q, :], True, True, reads=[b_small, b_cst], writes=[b_pb[5]])
            mm(pbank[5][0:q, 64:128], M1[0:q, 0:q], dat[0:q, :], True, True, reads=[b_small, b_cst], writes=[b_pb[5]])
            cp(cst_[0:q, :], pbank[5][0:q, 0:64], reads=[b_pb[5]], writes=[b_small])
            act(ecs[0:q, :], pbank[5][0:q, 0:64], AF.Exp, reads=[b_pb[5]], writes=[b_small])
            act(dte[0:q, :], pbank[5][0:q, 64:128], AF.Exp, reads=[b_pb[5]], writes=[b_small])
            xs3 = xst_ap[0:q, :].rearrange("p (h c) -> p h c", c=64)
            tt(xdt_ap[0:q, :].rearrange("p (h c) -> p h c", c=64), xs3, dtt[0:q, :].unsqueeze(2).to_broadcast([q, 64, 64]), ALU.mult,
               reads=[b_xst, b_small], writes=[b_xdt])
            tt(xdtd_ap[0:q, :].rearrange("p (h c) -> p h c", c=64), xdt_ap[0:q, :].rearrange("p (h c) -> p h c", c=64),
               dte[0:q, :].unsqueeze(2).to_broadcast([q, 64, 64]), ALU.mult, reads=[b_xdt, b_small], writes=[b_xdtd])
            LAST = cT["lastp"] if prompt else cT["lasts"]
            cs3 = cst_[0:q, :].rearrange("p (j two) -> p j two", two=2)
            for h2, rh, brh in ((0, rh0, b_rh0), (1, rh1, b_rh1)):
                tt(rh[0:q, 0:32 * nb].rearrange("p (j b) -> p j b", b=nb), cs3[:, :, h2:h2 + 1].to_broadcast([q, 32, nb]),
                   LAST[0:q, 0:nb].unsqueeze(1).to_broadcast([q, 32, nb]), ALU.mult, reads=[b_small, b_cst], writes=[brh])
            mm(pbank[5][:, 128:128 + 32 * nb] if nb == 1 else pbank[6][:, 0:32 * nb], cT["lh0"][0:q, :], rh0[0:q, 0:32 * nb], True, False,
               reads=[b_rh0, b_cst], writes=[b_pb[5] if nb == 1 else b_pb[6]])
            mm(pbank[5][:, 128:128 + 32 * nb] if nb == 1 else pbank[6][:, 0:32 * nb], cT["lh1"][0:q, :], rh1[0:q, 0:32 * nb], False, True,
               reads=[b_rh1, b_cst], writes=[b_pb[5] if nb == 1 else b_pb[6]])
            act(dcol[:, 0:32 * nb], pbank[5][:, 128:128 + 32 * nb] if nb == 1 else pbank[6][:, 0:32 * nb], AF.Exp,
                reads=[b_pb[5] if nb == 1 else b_pb[6]], writes=[b_dcol])
            dc3 = dcol[:, 0:32 * nb].rearrange("p (j b) -> p j b", b=nb)

            if prompt and ci == 0:
                P.op("dve", lambda e: e.memset(St[:], 0.0), writes=[b_St])
            yoT_banks = (0, 1, 2, 3)
            for b in range(nb):
                if not prompt:
                    dma("sp", St[:], sssm[l, b].rearrange("j m n -> m j n"), s_St, writes=[b_St])
                for t0 in range(0, 32, 4):
                    bi = 4 + ring["trb"]; ring["trb"] ^= 1
                    for i in range(4):
                        tr(pbank[bi][:, i * 128:(i + 1) * 128], St[:, t0 + i, :], cT["ident"][:, :], reads=[b_St], writes=[b_pb[bi]])
                    if (t0 // 4) % 2 == 0:
                        cp(stT_ap[:, t0 * 128:(t0 + 4) * 128], pbank[bi][:, :], reads=[b_pb[bi]], writes=[b_stT])
                    else:
                        act(stT_ap[:, t0 * 128:(t0 + 4) * 128], pbank[bi][:, :], AF.Copy, reads=[b_pb[bi]], writes=[b_stT])
                if not prompt:
                    for j in range(32):
                        bk = yoT_banks[j // 8]
                        mm(pbank[bk][:, (j % 8) * 64 + b * 4:(j % 8) * 64 + b * 4 + 4], stT_ap[:, j * 128:(j + 1) * 128], CT(j // 4)[:, b * 4:b * 4 + 4],
                           True, True, reads=[b_stT, b_cx], writes=[b_pb[bk]])
                    ts(Bm[0:q, :], Btok[0:q, :], cT["blks"][0:q, b:b + 1], None, ALU.mult, None, reads=[b_Btok, b_cst], writes=[b_Bm])
                    Bsrc, bB = Bm, b_Bm
                else:
                    Bsrc, bB = Btok, b_Btok
                if prompt:
                    y_off_prompt = None
                for j in range(32):
                    bi = 6 + (j % 2) if prompt else 6 + (j % 2)
                    g = j // 4
                    mm(pbank[bi][:, 0:128], xdtd_ap[0:q, j * 128:(j + 1) * 128], Bsrc[0:q, g * 128:(g + 1) * 128], True, True,
                       reads=[b_xdtd, bB], writes=[b_pb[bi]])
                    stt(St[:, j, :], St[:, j, :], dc3[:, j, b:b + 1], pbank[bi][:, 0:128], ALU.mult, ALU.add,
                        reads=[b_St, b_dcol, b_pb[bi]], writes=[b_St])
                if not prompt:
                    dma("sp", ossm_s[l, b].rearrange("j m n -> m j n"), St[:], s_St, reads=[b_St])
                elif ci == NPC - 1:
                    dma("sp", ossm_p[l].rearrange("j m n -> m j n"), St[:], s_St, reads=[b_St])
            if not prompt:
                for k4 in range(4):
                    cp(A5[:, 2048 + k4 * 512:2048 + (k4 + 1) * 512], pbank[yoT_banks[k4]][:, :], reads=[b_pb[yoT_banks[k4]]], writes=[b_stT])

            v3 = lambda t: t[0:q, 0:8 * q].rearrange("p (a b) -> p a b", b=q)
            Rt3, Et3, cbL3 = v3(Rf), v3(Ef), v3(cbLf)
            for g in range(8):
                hs = slice(8 * g, 8 * g + 8)
                tt(Rt3, dat[0:q, hs].unsqueeze(2).to_broadcast([q, 8, q]), TRI[0:q, 0:q].unsqueeze(1).to_broadcast([q, 8, q]),
                   ALU.mult, reads=[b_small, b_cst], writes=[b_R])
                nmm = (8 * q + 511) // 512
                for n in range(nmm):
                    wd_ = min(512, 8 * q - n * 512)
                    mm(pbank[7][0:q, 0:wd_], M1[0:q, 0:q], Rf[0:q, n * 512:n * 512 + wd_], True, True, reads=[b_R, b_cst], writes=[b_pb[7]])
                    act(Ef[0:q, n * 512:n * 512 + wd_], pbank[7][0:q, 0:wd_], AF.Exp, reads=[b_pb[7]], writes=[b_E])
                mm(pbank[4][0:q, 0:q], BT(g), CT(g), True, True, reads=[b_cx], writes=[b_pb[4]])
                tt(cbm[0:q, 0:q], pbank[4][0:q, 0:q], TRIB[0:q, 0:q], ALU.mult, reads=[b_pb[4], b_cst], writes=[b_cbm])
                tt(cbL3, Et3, cbm[0:q, 0:q].unsqueeze(1).to_broadcast([q, 8, q]), ALU.mult, reads=[b_E, b_cbm], writes=[b_cbL])
                for h in range(8):
                    hh = 8 * g + h
                    mm(pbank[5][0:q, h * 64:(h + 1) * 64], cbLf[0:q, h * q:(h + 1) * q], xdt_ap[0:q, hh * 64:(hh + 1) * 64], True, True,
                       reads=[b_cbL, b_xdt], writes=[b_pb[5]])
                if prompt:
                    mm(pbank[6][0:q, :], CT(g), stT_ap[:, g * 512:(g + 1) * 512], True, True, reads=[b_cx, b_stT], writes=[b_pb[6]])
                else:
                    for i in range(4):
                        j = 4 * g + i
                        tr(pbank[6][0:q, i * 128:(i + 1) * 128], A5[:, 2048 + j * 64:2048 + j * 64 + q], cT["ident"][:, :], reads=[b_stT], writes=[b_pb[6]])
                tt(tmpg[0:q, :].rearrange("p (h c) -> p h c", c=64), pbank[6][0:q, :].rearrange("p (h c) -> p h c", c=64),
                   ecs[0:q, hs].unsqueeze(2).to_broadcast([q, 8, 64]), ALU.mult, reads=[b_pb[6], b_small], writes=[b_tmpg])
                tt(tmp2[0:q, :].rearrange("p (h c) -> p h c", c=64), xst_ap[0:q, g * 512:(g + 1) * 512].rearrange("p (h c) -> p h c", c=64),
                   dsk_t[0:q, hs].unsqueeze(2).to_broadcast([q, 8, 64]), ALU.mult, reads=[b_xst, b_par], writes=[b_tmp2])
                tt(tmpg[0:q, :], tmpg[0:q, :], pbank[5][0:q, :], ALU.add, reads=[b_tmpg, b_pb[5]], writes=[b_tmpg])
                tt(y_ap[0:q, g * 512:(g + 1) * 512], tmpg[0:q, :], tmp2[0:q, :], ALU.add, reads=[b_tmpg, b_tmp2], writes=[b_y])
                if jobs:
                    jobs.pop(0)()

        for l in range(NL):
            dma("sp", n1g_t[:], n1g[l], s_par, writes=[b_par])
            dma("sp", n2g_t[:], n2g[l], s_par, writes=[b_par])
            dma("sp", ssmg_t[:], ssmg[l], s_par, writes=[b_par])
            dma("sp", cw_t[:].rearrange("p j k -> p (j k)"), cwc[l], s_par, writes=[b_par])
            dma("sp", cb_t[:], cbc[l], s_par, writes=[b_par])
            dma("sp", dtb_t[:], dtb_d[l], s_par, writes=[b_par])
            dma("sp", a_t[:], alog_d[l], s_par, writes=[b_par])
            dma("sp", dsk_t[:], dsk_d[l], s_par, writes=[b_par])
            act(a_t[:], a_t[:], AF.Exp, reads=[b_par], writes=[b_par])
            ts(a_t[:], a_t[:], -1.0, None, ALU.mult, None, reads=[b_par], writes=[b_par])
            for kind in ("p", "s"):
                q = 128 if kind == "p" else QS
                wsd = wsp_d if kind == "p" else wss_d
                sbd = sbp_d if kind == "p" else sbs_d
                TRIB = trib_p if kind == "p" else trib_s
                dma("pool", wsl[0:q, 0:16 * q], wsd[l], s_wsl, writes=[b_wsl])
                dma("pool", sbrow[0:1, 0:16 * q], sbd[l], s_sbrow, writes=[b_sbrow])
                tt(wsm[0:q, :, 0:q], wsl[0:q, 0:16 * q].rearrange("p (g t) -> p g t", t=q), TRIB[0:q, 0:q].unsqueeze(1).to_broadcast([q, 16, q]),
                   ALU.mult, reads=[b_wsl, b_cst], writes=[b_wsm])
                if kind == "p":
                    for c0 in range(0, NPC, 2):
                        items = []
                        for j, ci in enumerate(range(c0, min(c0 + 2, NPC))):
                            mixer(l, "p", ci, j)
                            items.append(("p", ci, j))
                        ffn(l, items)
                else:
                    mixer(l, "s", 0, 0)
                    ffn(l, [("s", 0, 0)])
        P.op("sp", None, writes=[b_xt2[0], b_xt2[1], b_St, b_ctok, b_vn, b_v])
        P.emit()
    return nc


def _consts(QS, NSB):
    c = {}
    c["ident"] = np.eye(128, dtype=np.float32)
    i = np.arange(128)
    c["trip"] = (i[:, None] <= i[None, :]).astype(np.float32)
    c["m1p"] = (i[:, None] > i[None, :]).astype(np.float32)
    s = np.arange(QS)
    same = (s[:, None] // 4) == (s[None, :] // 4)
    c["tris"] = (same & (s[:, None] <= s[None, :])).astype(np.float32)
    c["m1s"] = (same & (s[:, None] > s[None, :])).astype(np.float32)
    c["lh0"] = np.zeros((128, 128), np.float32); c["lh0"][:, :64] = 1
    c["lh1"] = np.zeros((128, 128), np.float32); c["lh1"][:, 64:] = 1
    c["lastp"] = np.zeros((128, 1), np.float32); c["lastp"][127, 0] = 1
    c["lasts"] = np.zeros((QS, NSB), np.float32)
    c["blks"] = np.zeros((QS, NSB), np.float32)
    for b in range(NSB):
        c["lasts"][4 * b + 3, b] = 1
        c["blks"][4 * b:4 * b + 4, b] = 1
    return c


def _layout_weights(w, NL):
    f = np.float32
    m = {}
    col = lambda a, nt: np.ascontiguousarray(a.reshape(NL, nt, 128).transpose(0, 2, 1)).astype(f)
    m["n1g"] = col(w["norm1_g"][:NL], 16); m["n2g"] = col(w["norm2_g"][:NL], 16); m["ssmg"] = col(w["ssm_norm_g"][:NL], 32)
    cw = w["conv_w"][:NL].reshape(NL, 4, 48, 128).transpose(0, 3, 2, 1)
    m["cwc"] = np.ascontiguousarray(cw).reshape(NL, 128, 192).astype(f)
    m["cbc"] = col(w["conv_b"][:NL], 48)
    rep = lambda a: np.ascontiguousarray(np.broadcast_to(a[:, None, :], (a.shape[0], 128, a.shape[1]))).astype(f)
    m["lng"] = rep(w["sgu_ln_g"][:NL]); m["lnb"] = rep(w["sgu_ln_b"][:NL])
    m["dtb"] = rep(w["dt_bias"][:NL]); m["alog"] = rep(w["a_log"][:NL]); m["dsk"] = rep(w["d_skip"][:NL])
    m["fng"] = np.ascontiguousarray(np.broadcast_to(w["final_norm_g"][None, :], (128, D))).astype(f)
    sw = w["sgu_w"][:NL]
    m["wsp"] = np.ascontiguousarray(sw.transpose(0, 3, 1, 2)).reshape(NL, 128, 16 * 128).astype(f)
    return m


def kernel(x_prompt, x_sample, state_ssm, state_conv, norm1_g, w_in, conv_w, conv_b, dt_bias, a_log,
           d_skip, ssm_norm_g, sgu_ln_g, sgu_ln_b, sgu_w, sgu_b, w_out_a, w_out_b, w_o, norm2_g,
           w_ffn_gate, w_ffn_up, w_ffn_down, final_norm_g):
    NL = DEPTH
    NSB = 16
    QS = 64
    f = np.float32
    w = dict(norm1_g=np.asarray(norm1_g), norm2_g=np.asarray(norm2_g), ssm_norm_g=np.asarray(ssm_norm_g), conv_w=np.asarray(conv_w),
             conv_b=np.asarray(conv_b), sgu_ln_g=np.asarray(sgu_ln_g), sgu_ln_b=np.asarray(sgu_ln_b), dt_bias=np.asarray(dt_bias),
             a_log=np.asarray(a_log), d_skip=np.asarray(d_skip), final_norm_g=np.asarray(final_norm_g), sgu_w=np.asarray(sgu_w))
    m = _layout_weights(w, NL)
    sw = np.asarray(sgu_w); sbias = np.asarray(sgu_b)
    w4 = np.ascontiguousarray(sw[:, :, :4, :4].transpose(0, 3, 1, 2))
    m["wss"] = np.ascontiguousarray(np.tile(w4, (1, NSB, 1, NSB))).reshape(NL, QS, 16 * QS).astype(f)
    m["sbp"] = np.ascontiguousarray(sbias.reshape(NL, 1, 16 * 128)).astype(f)
    m["sbs"] = np.ascontiguousarray(np.tile(sbias[:, :, :4], (1, 1, NSB))).reshape(NL, 1, 16 * QS).astype(f)
    big = {"w_in": np.asarray(w_in), "w_oa": np.asarray(w_out_a), "w_ob": np.asarray(w_out_b), "w_o": np.asarray(w_o),
           "w_g": np.asarray(w_ffn_gate), "w_u": np.asarray(w_ffn_up), "w_d": np.asarray(w_ffn_down)}
    cs = _consts(QS, NSB)
    xp_all = np.asarray(x_prompt); xs_all = np.asarray(x_sample)
    ssm_all = np.asarray(state_ssm); conv_all = np.asarray(state_conv)
    nc = build_program(2048, NL, NSB)
    in_maps = []
    for c in range(8):
        d = dict(m)
        d.update(big)
        for k, v in cs.items():
            d["c_" + k] = v
        d["xp"] = np.ascontiguousarray(xp_all[c % 4])
        d["xs"] = np.ascontiguousarray(xs_all[c * NSB:(c + 1) * NSB].reshape(QS, D))
        d["sssm"] = np.ascontiguousarray(ssm_all[:, c * NSB:(c + 1) * NSB].reshape(NL, NSB, 32, 128, 128))
        d["sconv"] = np.ascontiguousarray(conv_all[:, c * NSB:(c + 1) * NSB].reshape(NL, NSB * 3, CONVD))
        in_maps.append(d)
    res = run_bass_kernel_spmd(nc, in_maps, core_ids=list(range(8)))
    R = res.results
    y_prompt = np.stack([R[c]["yp"] for c in range(4)]).astype(f)
    y_sample = np.concatenate([R[c]["ys"].reshape(NSB, 4, D) for c in range(8)], axis=0).astype(f)
    ssm_p = np.stack([R[c]["ossm_p"].reshape(NL, 64, 64, 128) for c in range(4)], axis=1).astype(f)
    conv_p = np.stack([R[c]["oconv_p"] for c in range(4)], axis=1).astype(f)
    ssm_s = np.concatenate([R[c]["ossm_s"].reshape(NL, NSB, 64, 64, 128) for c in range(8)], axis=1).astype(f)
    conv_s = np.concatenate([R[c]["oconv_s"].reshape(NL, NSB, 3, CONVD) for c in range(8)], axis=1).astype(f)
    v_s = np.concatenate([R[c]["ov_s"].reshape(NL, NSB, 4, D) for c in range(8)], axis=1).astype(f)
    return (y_prompt, y_sample, ssm_p, conv_p, ssm_s, conv_s, v_s)
```
